# Optimizing a Trainium2 kernel written in Bass

```python
import jax, jax.numpy as jnp
from jax import lax
import numpy as np

D_MODEL = 1024
BATCH = 4
SEQ = 4096
DEPTH = 4

MEM_LEN = 256
HEAD_DIM = 64
N_SB_HEADS = 8
N_FOX_HEADS = 8
N_MEM_HEADS = 4
MEM_HEAD_DIM = 128
SB_W = N_SB_HEADS * HEAD_DIM
FOX_W = N_FOX_HEADS * HEAD_DIM
MEM_W = N_MEM_HEADS * MEM_HEAD_DIM
N_BRANCH = 3
IN_W = 3 * SB_W + 3 * FOX_W + N_FOX_HEADS + MEM_W
D_FF = ((8 * D_MODEL // 3 + 127) // 128) * 128
Q_BLOCK = 128
RMS_EPS = 1e-6

kernel_name = 'hybrid_sb_fox_mem_macaron'


def _rmsnorm(t, g):
    t32 = t.astype(jnp.float32)
    t32 = t32 * lax.rsqrt(jnp.mean(t32 * t32, axis=-1, keepdims=True) + RMS_EPS)
    return t32.astype(t.dtype) * g


def _swiglu(t, w_gate, w_up, w_down):
    return (jax.nn.silu(t @ w_gate) * (t @ w_up)) @ w_down


def _split_heads(t, n_heads):
    b, s, _ = t.shape
    return t.reshape(b, s, n_heads, -1).transpose(0, 2, 1, 3)


def _merge_heads(t):
    b, h, s, d = t.shape
    return t.transpose(0, 2, 1, 3).reshape(b, s, h * d)


def _query_blocks(t):
    b, h, s = t.shape[:3]
    t = t.reshape((b, h, s // Q_BLOCK, Q_BLOCK) + t.shape[3:])
    return jnp.moveaxis(t, 2, 0)


def _unblock(o):
    nb, b, h, blk, d = o.shape
    return jnp.moveaxis(o, 0, 2).reshape(b, h, nb * blk, d)


def _stick_breaking_attention(q, k, v):
    b, h, s_len, d = q.shape
    scale = d ** -0.5
    key_pos = jnp.arange(s_len)

    def block(args):
        qb, i = args
        z = jnp.einsum('bhqd,bhkd->bhqk', qb, k).astype(jnp.float32) * scale
        q_pos = i * Q_BLOCK + jnp.arange(Q_BLOCK)
        mask = key_pos[None, :] < q_pos[:, None]
        log_beta = jax.nn.log_sigmoid(z)
        log_not = jnp.where(mask, log_beta - z, 0.0)
        log_between = lax.cumsum(log_not, axis=3, reverse=True) - log_not
        w = jnp.where(mask, jnp.exp(log_beta + log_between), 0.0)
        return jnp.einsum('bhqk,bhkd->bhqd', w.astype(v.dtype), v)

    out = lax.map(block, (_query_blocks(q), jnp.arange(s_len // Q_BLOCK)))
    return _unblock(out)


def _forgetting_attention(q, k, v, log_f):
    b, h, s_len, d = q.shape
    scale = d ** -0.5
    key_pos = jnp.arange(s_len)
    c = lax.cumsum(log_f.astype(jnp.float32), axis=2)
    neg = jnp.finfo(jnp.float32).min

    def block(args):
        qb, cb, i = args
        z = jnp.einsum('bhqd,bhkd->bhqk', qb, k).astype(jnp.float32) * scale
        z = z + cb[..., :, None] - c[..., None, :]
        q_pos = i * Q_BLOCK + jnp.arange(Q_BLOCK)
        mask = key_pos[None, :] <= q_pos[:, None]
        p = jax.nn.softmax(jnp.where(mask, z, neg), axis=-1)
        return jnp.einsum('bhqk,bhkd->bhqd', p.astype(v.dtype), v)

    out = lax.map(block, (_query_blocks(q), _query_blocks(c), jnp.arange(s_len // Q_BLOCK)))
    return _unblock(out)


def _memory_attention(q, k, v):
    z = jnp.einsum('bhqd,bhkd->bhqk', q, k).astype(jnp.float32) * (q.shape[-1] ** -0.5)
    p = jax.nn.softmax(z, axis=-1)
    return jnp.einsum('bhqk,bhkd->bhqd', p.astype(v.dtype), v)


def setup_inputs(seed: int = 0) -> dict:
    key = jax.random.key(seed)
    ks = jax.random.split(key, 32)
    L = DEPTH

    def w(k, shape, fan_in):
        return jax.random.normal(k, shape, jnp.float32) * (fan_in ** -0.5)

    def gain(k, shape):
        return 1.0 + 0.05 * jax.random.normal(k, shape, jnp.float32)

    return {
        'x': jax.random.normal(ks[0], (BATCH, SEQ, D_MODEL), jnp.float32),
        'mem': jax.random.normal(ks[1], (BATCH, MEM_LEN, D_MODEL), jnp.float32),
        'ffn1_pre_g': gain(ks[2], (L, D_MODEL)),
        'ffn1_post_g': gain(ks[3], (L, D_MODEL)),
        'ffn1_w_gate': w(ks[4], (L, D_MODEL, D_FF), D_MODEL),
        'ffn1_w_up': w(ks[5], (L, D_MODEL, D_FF), D_MODEL),
        'ffn1_w_down': w(ks[6], (L, D_FF, D_MODEL), D_FF),
        'mix_pre_g': gain(ks[7], (L, D_MODEL)),
        'mix_post_g': gain(ks[8], (L, D_MODEL)),
        'w_in': w(ks[9], (L, D_MODEL, IN_W), D_MODEL),
        'b_forget': 2.0 + 0.5 * jax.random.normal(ks[10], (L, N_FOX_HEADS), jnp.float32),
        'mem_norm_g': gain(ks[11], (D_MODEL,)),
        'w_mem_kv': w(ks[12], (L, D_MODEL, 2 * MEM_W), D_MODEL),
        'w_gate': w(ks[13], (L, D_MODEL, N_BRANCH * D_MODEL), D_MODEL),
        'b_gate': 0.02 * jax.random.normal(ks[14], (L, N_BRANCH * D_MODEL), jnp.float32),
        'w_br_sb': w(ks[15], (L, SB_W, D_MODEL), SB_W),
        'w_br_fox': w(ks[16], (L, FOX_W, D_MODEL), FOX_W),
        'w_br_mem': w(ks[17], (L, MEM_W, D_MODEL), MEM_W),
        'w_out': w(ks[18], (L, D_MODEL, D_MODEL), D_MODEL),
        'ffn2_pre_g': gain(ks[19], (L, D_MODEL)),
        'ffn2_post_g': gain(ks[20], (L, D_MODEL)),
        'ffn2_w_gate': w(ks[21], (L, D_MODEL, D_FF), D_MODEL),
        'ffn2_w_up': w(ks[22], (L, D_MODEL, D_FF), D_MODEL),
        'ffn2_w_down': w(ks[23], (L, D_FF, D_MODEL), D_FF),
    }


def reference(x, mem, ffn1_pre_g, ffn1_post_g, ffn1_w_gate, ffn1_w_up, ffn1_w_down,
              mix_pre_g, mix_post_g, w_in, b_forget, mem_norm_g, w_mem_kv, w_gate, b_gate,
              w_br_sb, w_br_fox, w_br_mem, w_out,
              ffn2_pre_g, ffn2_post_g, ffn2_w_gate, ffn2_w_up, ffn2_w_down):
    mem_n = _rmsnorm(mem, mem_norm_g)
    split_at = np.cumsum([SB_W, SB_W, SB_W, FOX_W, FOX_W, FOX_W, N_FOX_HEADS])
    h = x
    for l in range(DEPTH):
        f = _swiglu(_rmsnorm(h, ffn1_pre_g[l]), ffn1_w_gate[l], ffn1_w_up[l], ffn1_w_down[l])
        h = h + 0.5 * _rmsnorm(f, ffn1_post_g[l])

        u = _rmsnorm(h, mix_pre_g[l])
        proj = u @ w_in[l]
        q_sb, k_sb, v_sb, q_fx, k_fx, v_fx, f_logit, q_mem = jnp.split(proj, split_at, axis=-1)

        o_sb = _stick_breaking_attention(_split_heads(q_sb, N_SB_HEADS), _split_heads(k_sb, N_SB_HEADS),
                                         _split_heads(v_sb, N_SB_HEADS))
        log_f = jax.nn.log_sigmoid((f_logit + b_forget[l]).astype(jnp.float32)).transpose(0, 2, 1)
        o_fx = _forgetting_attention(_split_heads(q_fx, N_FOX_HEADS), _split_heads(k_fx, N_FOX_HEADS),
                                     _split_heads(v_fx, N_FOX_HEADS), log_f)
        k_mem, v_mem = jnp.split(mem_n @ w_mem_kv[l], 2, axis=-1)
        o_mem = _memory_attention(_split_heads(q_mem, N_MEM_HEADS), _split_heads(k_mem, N_MEM_HEADS),
                                  _split_heads(v_mem, N_MEM_HEADS))

        g_sb, g_fx, g_mem = jnp.split(jax.nn.sigmoid(u @ w_gate[l] + b_gate[l]), N_BRANCH, axis=-1)
        merged = (g_sb * (_merge_heads(o_sb) @ w_br_sb[l])
                  + g_fx * (_merge_heads(o_fx) @ w_br_fox[l])
                  + g_mem * (_merge_heads(o_mem) @ w_br_mem[l]))
        h = h + _rmsnorm(merged @ w_out[l], mix_post_g[l])

        f = _swiglu(_rmsnorm(h, ffn2_pre_g[l]), ffn2_w_gate[l], ffn2_w_up[l], ffn2_w_down[l])
        h = h + 0.5 * _rmsnorm(f, ffn2_post_g[l])
    return h
```

```python
import numpy as np
import ml_dtypes
from contextlib import ExitStack

import concourse.bass as bass
import concourse.mybir as mybir
from concourse.bass_utils import run_bass_kernel_spmd

F32 = mybir.dt.float32
BF16 = mybir.dt.bfloat16
AF = mybir.ActivationFunctionType
ALU = mybir.AluOpType

L = 4
D = 1024
KC = 8
FF = 2816
FC = 22
S = 4096
NT = 2048
NTILE = 16
TG = 512
SG = 1024
NSG = NT // SG
IN_W = 3592
EPS = 1e-6
NEG = -30000.0
RANK_CHUNKS = [[0, 3, 4, 7], [1, 2, 5, 6]]
CHUNK_OWNER = {}
for _r in range(2):
    for _g, _c in enumerate(RANK_CHUNKS[_r]):
        CHUNK_OWNER[_c] = (_r, _g)

ENGS = ("pe", "act", "dve", "pool", "sp")


def _ap(x):
    if isinstance(x, bass.AP):
        return x
    return x[tuple(slice(None) for _ in x.shape)]


class Ticket:
    __slots__ = ("eng", "sem", "val", "pos")

    def __init__(self, eng):
        self.eng = eng
        self.sem = None
        self.val = None
        self.pos = None


class Buf:
    __slots__ = ("name", "w", "r", "const", "dsem", "dcount", "dq", "excl")

    def __init__(self, name, const=False):
        self.name = name
        self.w = []
        self.r = []
        self.const = const
        self.dsem = None
        self.dcount = 0
        self.dq = None
        self.excl = False


class Item:
    __slots__ = ("fn", "deps", "sig", "dma_t", "inc1")

    def __init__(self, fn, deps, sig, dma_t=None):
        self.fn = fn
        self.deps = deps
        self.sig = sig
        self.dma_t = dma_t
        self.inc1 = False


class Prog:
    def __init__(self, nc, stack):
        self.nc = nc
        self.stack = stack
        self.items = {e: [] for e in ENGS}
        self.nsem = 0
        self.pre = {}
        self.pending_dma = []
        self.sem_free = {"sp": [], "pool": [], "act": []}
        self.scopes = [[]]

    def new_sem(self, name):
        self.nsem += 1
        return self.stack.enter_context(self.nc.semaphore(f"{name}_{self.nsem}"))

    def barrier(self):
        ts = []
        for eng in ENGS:
            if eng == "sp" or not self.items[eng]:
                continue
            for it in reversed(self.items[eng]):
                if it.dma_t is not None:
                    continue
                assert it.sig is not None, eng
                ts.append(it.sig)
                break
        ts += self.pending_dma
        self.pending_dma = []
        for eng in ENGS:
            self.pre[eng] = self.pre.get(eng, []) + ts

    def push_scope(self):
        self.scopes.append([])

    def pop_scope(self):
        for b in self.scopes.pop():
            if b.dsem is not None and b.dcount < 24000:
                self.sem_free[b.dq].append((b.dsem, b.dcount))
            b.dsem = None
            b.dq = None

    def _deps(self, eng, t, reads, writes, waits):
        deps = list(self.pre.pop(eng, []))
        for b in reads:
            deps += b.w
            if b.excl:
                deps += [d for d in b.r if d.eng != eng]
        for b in writes:
            deps += b.w
            deps += b.r
        deps += list(waits)
        out = []
        seen = set()
        for d in deps:
            if d is t or id(d) in seen:
                continue
            if eng == "pe" and d.eng == "pe":
                continue
            seen.add(id(d))
            out.append(d)
        return out

    def _book(self, t, reads, writes):
        for b in reads:
            if b.const:
                continue
            if not b.r or b.r[-1] is not t:
                b.r.append(t)
        for b in writes:
            if b.const:
                if not b.w or b.w[-1] is not t:
                    b.w.append(t)
            else:
                b.w = [t]
                b.r = []

    def op(self, eng, fn, reads=(), writes=(), ticket=None, last=True, waits=()):
        t = ticket if ticket is not None else Ticket(eng)
        deps = self._deps(eng, t, reads, writes, waits)
        self.items[eng].append(Item(fn, deps, t if last else None))
        self._book(t, reads, writes)
        return t

    def dma_ticket(self, buf, q):
        t = Ticket("dma")
        assert buf.dq in (None, q), (buf.name, buf.dq, q)
        buf.dq = q
        if buf.dsem is None:
            if self.sem_free[q]:
                buf.dsem, buf.dcount = self.sem_free[q].pop()
            else:
                buf.dsem = self.new_sem("d")
                buf.dcount = 0
            self.scopes[-1].append(buf)
        t.sem = buf.dsem
        t.val = buf.dcount
        t.pos = buf
        return t

    def dma(self, q, out_ap, in_ap, reads=(), writes=(), sembuf=None, ticket=None, waits=()):
        sb = sembuf if sembuf is not None else (writes[0] if writes else reads[0])
        t = ticket if ticket is not None else self.dma_ticket(sb, q)
        assert t.pos is sb and sb.dq == q
        sb.dcount += 16
        t.val = sb.dcount
        deps = self._deps(q, t, reads, writes, waits)

        out_ap, in_ap = _ap(out_ap), _ap(in_ap)

        def fn(e, out_ap=out_ap, in_ap=in_ap):
            return e.dma_start(out=out_ap, in_=in_ap)

        self.items[q].append(Item(fn, deps, None, dma_t=t))
        self._book(t, reads, writes)
        if not self.pending_dma or self.pending_dma[-1] is not t:
            self.pending_dma.append(t)
        return t

    def collective(self, in_ap, out_ap, groups):
        if not hasattr(self, "ccsem"):
            self.ccsem = self.new_sem("cc")
            self.cccount = 0
        self.cccount += 1
        t = Ticket("dma")
        t.sem = self.ccsem
        t.val = self.cccount
        deps = self._deps("pool", t, (), (), ())

        def fn(e):
            return e.collective_compute("AllGather", ALU.bypass, replica_groups=groups, ins=[in_ap], outs=[out_ap])

        it = Item(fn, deps, None, dma_t=t)
        it.inc1 = True
        self.items["pool"].append(it)
        self.pending_dma.append(t)
        return t

    def matmul(self, out, lhsT, rhs, start, stop, reads, writes, ticket=None, last=True):
        return self.op("pe", lambda e: e.matmul(out, lhsT, rhs, start=start, stop=stop),
                       reads, writes, ticket, last)

    def transpose(self, out, in_, ident, reads, writes, ticket=None, last=True):
        return self.op("pe", lambda e: e.transpose(out, in_, ident), reads, writes, ticket, last)

    def act(self, out, in_, func, reads, writes, bias=None, scale=None, accum_out=None, eng="act"):
        kw = {}
        if bias is not None:
            kw["bias"] = bias
        if scale is not None:
            kw["scale"] = scale
        if accum_out is not None:
            kw["accum_out"] = accum_out
        return self.op("act", lambda e: e.activation(out, in_, func, **kw), reads, writes)

    def tt(self, eng, out, in0, in1, op, reads, writes):
        return self.op(eng, lambda e: e.tensor_tensor(out, in0, in1, op), reads, writes)

    def ts(self, eng, out, in0, s1, s2, op0, op1, reads, writes):
        if op1 is None:
            return self.op(eng, lambda e: e.tensor_scalar(out, in0, s1, None, op0), reads, writes)
        return self.op(eng, lambda e: e.tensor_scalar(out, in0, s1, s2, op0, op1), reads, writes)

    def stt(self, out, in0, scalar, in1, op0, op1, reads, writes):
        return self.op("dve", lambda e: e.scalar_tensor_tensor(out, in0, scalar, in1, op0, op1),
                       reads, writes)

    def copy(self, eng, out, in_, reads, writes):
        if eng == "act":
            return self.op("act", lambda e: e.copy(out, in_), reads, writes)
        return self.op(eng, lambda e: e.tensor_copy(out, in_), reads, writes)

    def memset(self, eng, ap, val, writes):
        return self.op(eng, lambda e: e.memset(ap, val), (), writes)

    def finalize(self):
        LIM = 30000
        for eng in ENGS:
            if eng == "sp":
                continue
            sem = None
            cnt = 0
            for pos, it in enumerate(self.items[eng]):
                if it.sig is not None:
                    if sem is None or cnt >= LIM:
                        sem = self.new_sem("e" + eng)
                        cnt = 0
                    cnt += 1
                    it.sig.sem = sem
                    it.sig.val = cnt
                    it.sig.pos = pos
        for eng in ENGS:
            for pos, it in enumerate(self.items[eng]):
                for d in it.deps:
                    assert d.val is not None and d.sem is not None, (eng, pos, d.eng)
                    if d.eng == eng:
                        assert d.pos < pos, ("self-deadlock", eng, pos, d.pos)

    def replay(self, eng, e):
        waited = {}
        for it in self.items[eng]:
            for d in it.deps:
                k = id(d.sem)
                if waited.get(k, 0) < d.val:
                    e.wait_ge(d.sem, d.val)
                    waited[k] = d.val
            ins = it.fn(e)
            if it.dma_t is not None:
                if it.inc1:
                    ins.then_inc(it.dma_t.sem)
                else:
                    ins.then_inc(it.dma_t.sem, 16)
            elif it.sig is not None:
                ins.then_inc(it.sig.sem, 1)


class Scope(ExitStack):
    def __init__(self, P):
        super().__init__()
        self.P = P
        P.push_scope()

    def __exit__(self, *a):
        self.P.barrier()
        self.P.pop_scope()
        return super().__exit__(*a)


class Ring:
    def __init__(self, aps):
        self.aps = aps
        self.bufs = [Buf(f"ring{i}") for i in range(len(aps))]
        self.i = 0

    def next(self):
        k = self.i % len(self.aps)
        self.i += 1
        return self.aps[k], self.bufs[k]


class Builder:
    def __init__(self, mode, layers, dbg=None, ncores=8):
        self.mode = mode
        self.layers = layers
        self.ncores = ncores
        self.dbg = dbg or {}
        self.nc = bass.Bass("TRN2", target_bir_lowering=False)
        self.stack = ExitStack()
        self.P = Prog(self.nc, self.stack)
        self.dram = {}
        self.final_tickets = []

    def din(self, name, shape, dt=F32):
        t = self.nc.dram_tensor(name, list(shape), dt, kind="ExternalInput")
        self.dram[name] = t
        return t.ap()

    def dout(self, name, shape, dt=F32):
        t = self.nc.dram_tensor(name, list(shape), dt, kind="ExternalOutput")
        self.dram[name] = t
        return t.ap()

    def dscr(self, name, shape, dt):
        t = self.nc.dram_tensor(name, list(shape), dt)
        self.dram[name] = t
        return t.ap()

    def sb(self, name, shape, dt):
        return self.stack.enter_context(self.nc.sbuf_tensor(name, list(shape), dt))

    def sb_in(self, st, name, shape, dt):
        self.uid = getattr(self, "uid", 0) + 1
        if not hasattr(self, "tn"):
            self.tn = {}
        self.tn.setdefault(name, []).append(f"{name}_{self.uid}")
        return st.enter_context(self.nc.sbuf_tensor(f"{name}_{self.uid}", list(shape), dt))

    def build(self):
        nc, P = self.nc, self.P
        mode = self.mode
        nl = len(self.layers)
        self.nl = nl
        hasA = mode in ("A", "FULL")
        hasBC = mode in ("BC", "FULL")
        full = mode == "FULL"

        self.x_in = self.din("x_in", [NT, D])
        self.h_out = self.dout("h_out", [NT, D])
        self.cmat = self.din("cmat", [4, 128, 128])
        self.pregT = self.din("pregT", [128, nl * 3 * 8])
        self.postg = self.din("postg", [nl * 3, D])
        if hasA:
            self.W_f1g = self.din("f1g", [nl, FC, 128, KC * 128])
            self.W_f1u = self.din("f1u", [nl, FC, 128, KC * 128])
            self.W_f1d = self.din("f1d", [nl, FC, 128, D])
            self.W_inT = self.din("winT", [nl, 20, 128, KC * 128])
            self.W_inV = self.din("winV", [nl, 2, 128, KC * 512])
            self.W_inF = self.din("winF", [nl, 128, KC * 8])
            self.nbf = self.din("nbf", [8, nl])
        if hasBC:
            self.W_f2g = self.din("f2g", [nl, FC, 128, KC * 128])
            self.W_f2u = self.din("f2u", [nl, FC, 128, KC * 128])
            self.W_f2d = self.din("f2d", [nl, FC, 128, D])
            self.W_gate = self.din("wgate", [nl, 24, 128, KC * 128])
            self.bgT = self.din("bgT", [128, nl * 24])
            self.W_br = self.din("wbr", [nl, 24, 128, 4 * 128])
            self.W_out = self.din("wout", [nl, 128, KC * D])
            self.W_mk = self.din("wmk", [nl, 4, 128, KC * 128])
            self.W_mv = self.din("wmv", [nl, 128, KC * 512])
            self.mem_in = self.din("mem", [256, D])
            self.memgT = self.din("memgT", [128, 8])
            self.cmask = self.din("cmask", [128, 32 * 514], BF16)
            self.selc = self.din("selc", [8, 32])
        mk_loc_in = self.din if mode == "BC" else (self.dout if mode == "A" else self.dscr)
        mk_loc_out = self.dout if mode == "A" else self.dscr

        def loc(name, shape, dt, needed_in_bc):
            if mode == "A":
                return self.dout(name, shape, dt)
            if mode == "BC":
                return self.din(name, shape, dt) if needed_in_bc else None
            return self.dscr(name, shape, dt)

        self.s_qsb = loc("s_qsb", [512, NT], BF16, True)
        self.s_qfx = loc("s_qfx", [512, NT], BF16, True)
        self.s_qmem = loc("s_qmem", [512, NT], BF16, True)
        self.s_ksb = loc("s_ksb", [512, NT], BF16, False)
        self.s_kfx = loc("s_kfx", [512, NT], BF16, False)
        self.s_vsb = loc("s_vsb", [NT, 512], BF16, False)
        self.s_vfx = loc("s_vfx", [NT, 512], BF16, False)
        self.s_lf = loc("s_lf", [8, NT], F32, True)
        if hasBC:
            gk = self.din if mode == "BC" else self.dscr
            self.g_ksb = gk("g_ksb", [2, 512, NT], BF16)
            self.g_kfx = gk("g_kfx", [2, 512, NT], BF16)
            self.g_vsb = gk("g_vsb", [2, NT, 512], BF16)
            self.g_vfx = gk("g_vfx", [2, NT, 512], BF16)
            self.g_lf = gk("g_lf", [2, 8, NT], F32)
            self.s_mrow = self.dscr("s_mrow", [8, NT], BF16)
            self.s_crow = self.dscr("s_crow", [3, 8, S], BF16)
            self.s_osb = self.dscr("s_osb", [512, NT], BF16)
            self.s_ofx = self.dscr("s_ofx", [512, NT], BF16)
            self.s_omem = self.dscr("s_omem", [512, NT], BF16)
        self.dbg_out = {}
        for name, shape in self.dbg.items():
            if name not in ("stop", "plvl", "gph", "split", "v1", "v2"):
                self.dbg_out[name] = self.dout("dbg_" + name, shape)

        self.h = self.sb("h", [128, NTILE, D], F32)
        self.hb = [Buf(f"h{i}") for i in range(NTILE)]
        self.ident = self.sb("ident", [128, 128], BF16)
        self.negtri = self.sb("negtri", [128, 128], BF16)
        self.ones = self.sb("ones", [128, 128], BF16)
        self.negones = self.sb("negones", [128, 128], BF16)
        self.identf = self.sb("identf", [128, 128], F32)
        self.pregs = self.sb("pregs", [128, nl * 24], F32)
        self.cb = Buf("consts", const=True)
        self.ssq = self.sb("ssq", [128, 8], F32)
        self.lnv = self.sb("lnv", [128, 8], F32)
        self.rstd = self.sb("rstd", [128, 8], F32)
        self.b_ssq = Buf("ssq")
        self.b_lnv = Buf("lnv")
        self.b_rstd = Buf("rstd")
        self.small = self.sb("small", [128, 4 * 4], F32)
        self.r_small = Ring([self.small[:, 4 * k:4 * k + 4] for k in range(4)])
        self.b_gpost = Buf("gpost")

        self.ps2 = [self.stack.enter_context(nc.psum_tensor(f"ps{k}", [128, 1024], F32)) for k in range(4)]
        self.psb = [Buf(f"bank{k}") for k in range(8)]
        for b_ in self.psb:
            b_.excl = True
        self.bank_ctr = 0

        self.cols = self.sb("cols", [128, 4], F32)
        self.eps_col = self.cols[:, 0:1]
        self.lnhalf_col = self.cols[:, 1:2]
        self.one_col = self.cols[:, 2:3]
        P.memset("dve", self.cols[:, 0:1], EPS, writes=[self.cb])
        P.memset("dve", self.cols[:, 1:2], float(np.log(0.5)), writes=[self.cb])
        P.memset("dve", self.cols[:, 2:3], 1.0, writes=[self.cb])
        self.cdb = Buf("cdma")
        ct = P.dma_ticket(self.cdb, "sp")
        self.cdbp = Buf("cdmap")
        ctp = P.dma_ticket(self.cdbp, "pool")
        self.ct = ct
        for k, t in enumerate((self.ident, self.negtri, self.ones, self.negones)):
            P.dma("pool", t[:, :], self.cmat[k], writes=[self.cb], sembuf=self.cdbp, ticket=ctp)
        P.dma("sp", self.identf[:, :], self.cmat[0], writes=[self.cb], sembuf=self.cdb, ticket=ct)
        P.dma("sp", self.pregs[:, :], self.pregT[:, :], writes=[self.cb], sembuf=self.cdb, ticket=ct)
        self.hdb = Buf("hdma")
        ht = P.dma_ticket(self.hdb, "sp")
        for i in range(NTILE):
            P.dma("sp", self.h[:, i, :], self.x_in[i * 128:(i + 1) * 128, :], writes=[self.hb[i]],
                  sembuf=self.hdb, ticket=ht)

        if hasBC:
            self.memT = self.sb("memT", [128, KC, 256], BF16)
            self.memgs = self.sb("memgs", [128, 8], F32)
            self.bgs = self.sb("bgs", [128, nl * 24], F32)
            P.dma("sp", self.memgs[:, :], self.memgT[:, :], writes=[self.cb], sembuf=self.cdb, ticket=ct)
            P.dma("sp", self.bgs[:, :], self.bgT[:, :], writes=[self.cb], sembuf=self.cdb, ticket=ct)
        if hasA:
            self.nbfs = self.sb("nbfs", [8, nl], F32)
            P.dma("sp", self.nbfs[:, :], self.nbf[:, :], writes=[self.cb], sembuf=self.cdb, ticket=ct)
            P.ts("dve", self.nbfs[:, :], self.nbfs[:, :], -1.0, None, ALU.mult, None, reads=[self.cb], writes=[self.cb])
        if hasBC:
            self.prep_mem()

        stop = self.dbg.get("stop")
        for li in range(nl):
            if stop == "init":
                break
            if hasA:
                for sgi in range(NSG):
                    self.ffn(li, 0, sgi)
                    if stop in ("prenorm", "gateup", "ffn", "down", "post1"):
                        break
                if stop in ("prenorm", "gateup", "ffn", "down", "post1"):
                    break
                for sgi in range(NSG):
                    self.proj(li, sgi)
            if full:
                self.exchange(li)
            if hasBC:
                self.attention(li)
                for sgi in range(NSG):
                    self.merge(li, sgi)
                for sgi in range(NSG):
                    self.ffn(li, 2, sgi)

        P.barrier()
        ot = P.dma_ticket(self.hdb, "sp")
        for i in range(NTILE):
            P.dma("sp", self.h_out[i * 128:(i + 1) * 128, :], self.h[:, i, :], reads=[self.hb[i]],
                  sembuf=self.hdb, ticket=ot)
        P.barrier()
        P.op("dve", lambda e: e.memset(self.small[:, 0:1], 0.0), (), ())
        P.finalize()
        with nc.Block() as block:
            @block.tensor
            def _(e):
                P.replay("pe", e)

            @block.scalar
            def _(e):
                P.replay("act", e)

            @block.vector
            def _(e):
                P.replay("dve", e)

            @block.gpsimd
            def _(e):
                P.replay("pool", e)

            @block.sync
            def _(e):
                P.replay("sp", e)
        return nc

    def bank(self):
        k = self.bank_ctr % 8
        self.bank_ctr += 1
        return self.ps2[k // 2][:, (k % 2) * 512:(k % 2) * 512 + 512], self.psb[k]

    def bank2(self):
        if self.bank_ctr % 2:
            self.bank_ctr += 1
        k = self.bank_ctr % 8
        self.bank_ctr += 2
        return self.ps2[k // 2][:, :], [self.psb[k], self.psb[k + 1]]

    def load_gpost(self, li, w):
        self.P.dma("sp", self.gpost[:, :], self.postg[li * 3 + w:li * 3 + w + 1, :].partition_broadcast(128),
                   writes=[self.b_gpost])

    def prenorm(self, tiles, gcol, uT, uTb):
        self.prenorm_src([(self.h[:, i, :], self.hb[i]) for i in tiles], gcol, uT, uTb)

    def prenorm_src(self, srcs, gcol, uT, uTb):
        P = self.P
        n = len(srcs)
        for k, (xa, xbuf) in enumerate(srcs):
            junk, jb = self.r_junk.next()
            P.act(junk, xa, AF.Square, reads=[xbuf], writes=[jb, self.b_ssq],
                  accum_out=self.ssq[:, k:k + 1])
        P.act(self.lnv[:, 0:n], self.ssq[:, 0:n], AF.Ln, reads=[self.b_ssq, self.cb], writes=[self.b_lnv],
              scale=1.0 / D, bias=self.eps_col[:, 0:1])
        P.act(self.rstd[:, 0:n], self.lnv[:, 0:n], AF.Exp, reads=[self.b_lnv], writes=[self.b_rstd],
              scale=-0.5)
        for k, (xa, xbuf) in enumerate(srcs):
            xn, xb = self.r_xn.next()
            P.ts("dve", xn, xa, self.rstd[:, k:k + 1], None, ALU.mult, None,
                 reads=[xbuf, self.b_rstd], writes=[xb])
            bk, bb = self.bank()
            bkb = bk.bitcast(BF16)
            t = Ticket("pe")
            for c in range(KC):
                P.transpose(bkb[:, c * 128:(c + 1) * 128], xn[:, c * 128:(c + 1) * 128], self.ident[:, :],
                            reads=[xb, self.cb], writes=[bb], ticket=t, last=(c == KC - 1))
            P.tt("dve", uT[:, :, k * 128:(k + 1) * 128],
                 bkb.rearrange("p (c n) -> p c n", c=KC),
                 gcol.unsqueeze(2).to_broadcast([128, KC, 128]), ALU.mult,
                 reads=[bb, self.cb], writes=[uTb(k)])

    def postnorm(self, o2, o2b, gp, factor, i):
        P = self.P
        sm, smb = self.r_small.next()
        junk, jb = self.r_junk.next()
        P.act(junk, o2, AF.Square, reads=o2b, writes=[jb, smb], accum_out=sm[:, 0:1])
        lvl = int(self.dbg.get("plvl", 9))
        if lvl < 1:
            return
        P.act(sm[:, 1:2], sm[:, 0:1], AF.Ln, reads=[smb], writes=[smb], scale=1.0 / D, bias=self.eps_col[:, 0:1])
        P.act(sm[:, 2:3], sm[:, 1:2], AF.Exp, reads=[smb], writes=[smb], scale=-0.5,
              bias=(self.lnhalf_col[:, 0:1] if factor == 0.5 else None))
        if lvl < 2:
            return
        tw, twb = self.r_tw.next()
        if self.dbg.get("gph"):
            gp = self.h[:, i, :]
        if self.dbg.get("v1"):
            P.tt("dve", tw, o2, self.h[:, i, :], ALU.mult, reads=o2b, writes=[twb])
        elif self.dbg.get("v2"):
            P.tt("dve", tw, self.h[:, i, :], gp, ALU.mult, reads=o2b + [self.b_gpost], writes=[twb])
        elif self.dbg.get("split"):
            P.tt("dve", tw[:, 0:512], o2[:, 0:512], gp[:, 0:512], ALU.mult, reads=o2b + [self.b_gpost], writes=[twb])
            P.tt("dve", tw[:, 512:1024], o2[:, 512:1024], gp[:, 512:1024], ALU.mult, reads=o2b + [self.b_gpost], writes=[twb])
        else:
            P.tt("dve", tw, o2, gp, ALU.mult, reads=o2b + [self.b_gpost, smb], writes=[twb])
        if lvl < 3:
            return
        P.stt(self.h[:, i, :], tw, sm[:, 2:3], self.h[:, i, :], ALU.mult, ALU.add,
              reads=[twb, smb, self.hb[i]], writes=[self.hb[i]])

    def open_scope(self):
        self.P.barrier()
        return Scope(self.P)

    def norm_scratch(self, st):
        junk = self.sb_in(st, "junk", [128, D], BF16)
        self.r_junk = Ring([junk[:, 0:D]])
        xn = self.sb_in(st, "xn", [128, 2 * D], BF16)
        self.r_xn = Ring([xn[:, k * D:(k + 1) * D] for k in range(2)])
        tw = self.sb_in(st, "tw", [128, D], F32)
        self.r_tw = Ring([tw[:, 0:D]])
        self.gpost = self.sb_in(st, "gpost", [128, D], F32)
        self.b_gpost = Buf("gpost")

    def ffn(self, li, which, sgi):
        P, nc = self.P, self.nc
        Wg, Wu, Wd = (self.W_f1g, self.W_f1u, self.W_f1d) if which == 0 else (self.W_f2g, self.W_f2u, self.W_f2d)
        with self.open_scope() as st:
            self.norm_scratch(st)
            uT = self.sb_in(st, "uT", [128, KC, SG], BF16)
            actT = self.sb_in(st, "actT", [128, FC, SG], BF16)
            wd = self.sb_in(st, "wd", [128, FC, D], BF16)
            wgu = self.sb_in(st, "wgu", [128, 6, KC * 128], BF16)
            sil = self.sb_in(st, "sil", [128, 2, TG], F32)
            r_wg = Ring([wgu[:, k, :] for k in range(3)])
            r_wu = Ring([wgu[:, 3 + k, :] for k in range(3)])
            r_sil = Ring([sil[:, k, :] for k in range(2)])
            uTb = [Buf("uT0"), Buf("uT1")]
            actb = [Buf("act0"), Buf("act1")]
            wdb = Buf("wd")
            self.load_gpost(li, which)
            gcol = self.pregs[:, (li * 3 + which) * 8:(li * 3 + which) * 8 + 8]
            self.prenorm([sgi * 8 + k for k in range(8)], gcol, uT, lambda k: uTb[k // 4])
            if self.dbg.get("stop") == "prenorm":
                return
            wd_t = P.dma_ticket(wdb, "pool")
            for j in range(FC):
                wg, wgb = r_wg.next()
                wu, wub = r_wu.next()
                P.dma("pool", wg, Wg[li, j], writes=[wgb])
                P.dma("pool", wu, Wu[li, j], writes=[wub])
                P.dma("pool", wd[:, j, :], Wd[li, j], writes=[wdb], ticket=wd_t)
                for tg in range(2):
                    gk, gb = self.bank()
                    uk, ub = self.bank()
                    t = Ticket("pe")
                    for c in range(KC):
                        P.matmul(gk, wg[:, c * 128:(c + 1) * 128], uT[:, c, tg * TG:(tg + 1) * TG],
                                 c == 0, c == KC - 1, reads=[wgb, uTb[tg]], writes=[gb], ticket=t, last=(c == KC - 1))
                    t = Ticket("pe")
                    for c in range(KC):
                        P.matmul(uk, wu[:, c * 128:(c + 1) * 128], uT[:, c, tg * TG:(tg + 1) * TG],
                                 c == 0, c == KC - 1, reads=[wub, uTb[tg]], writes=[ub], ticket=t, last=(c == KC - 1))
                    s, sbf = r_sil.next()
                    P.act(s, gk, AF.Silu, reads=[gb], writes=[sbf])
                    P.tt("dve", actT[:, j, tg * TG:(tg + 1) * TG], s, uk, ALU.mult,
                         reads=[sbf, ub], writes=[actb[tg]])
            gp = self.gpost[:, :]
            if self.dbg.get("stop") == "gateup":
                return
            for k in range(8):
                o2, o2b = self.bank2()
                for hf in range(2):
                    t = Ticket("pe")
                    for j in range(FC):
                        P.matmul(o2[:, hf * 512:(hf + 1) * 512], actT[:, j, k * 128:(k + 1) * 128],
                                 wd[:, j, hf * 512:(hf + 1) * 512], j == 0, j == FC - 1,
                                 reads=[actb[k // 4], wdb], writes=[o2b[hf]], ticket=t, last=(j == FC - 1))
                if self.dbg.get("stop") == "down":
                    continue
                self.postnorm(o2, o2b, gp, 0.5, sgi * 8 + k)
                if self.dbg.get("stop") == "post1":
                    break
            P.barrier()

    def proj(self, li, sgi):
        P = self.P
        with self.open_scope() as st:
            self.norm_scratch(st)
            uT = self.sb_in(st, "uT", [128, KC, SG], BF16)
            wblk = self.sb_in(st, "wblk", [128, 3, KC * 128], BF16)
            wv = self.sb_in(st, "wv", [128, KC * 512], BF16)
            wf = self.sb_in(st, "wf", [128, KC * 8], BF16)
            stg = self.sb_in(st, "stg", [128, 4, TG], BF16)
            lft = self.sb_in(st, "lft", [8, 3, TG], F32)
            r_w = Ring([wblk[:, k, :] for k in range(3)])
            r_stg = Ring([stg[:, k, :] for k in range(4)])
            uTb = [Buf("uT0"), Buf("uT1")]
            wvb, wfb, lfb = Buf("wv"), Buf("wf"), Buf("lf")
            gcol = self.pregs[:, (li * 3 + 1) * 8:(li * 3 + 1) * 8 + 8]
            self.prenorm([sgi * 8 + k for k in range(8)], gcol, uT, lambda k: uTb[k // 4])
            n0 = sgi * SG
            dests = [(self.s_qsb, 0.125)] * 4 + [(self.s_ksb, 1.0)] * 4 + [(self.s_qfx, 0.125)] * 4 + \
                    [(self.s_kfx, 1.0)] * 4 + [(self.s_qmem, 1.0)] * 4
            for blk in range(20):
                w, wb = r_w.next()
                P.dma("pool", w, self.W_inT[li, blk], writes=[wb])
                dst, scl = dests[blk]
                r0 = (blk % 4) * 128
                for tg in range(2):
                    bk, bb = self.bank()
                    t = Ticket("pe")
                    for c in range(KC):
                        P.matmul(bk, w[:, c * 128:(c + 1) * 128], uT[:, c, tg * TG:(tg + 1) * TG],
                                 c == 0, c == KC - 1, reads=[wb, uTb[tg]], writes=[bb], ticket=t, last=(c == KC - 1))
                    sg_, sgb = r_stg.next()
                    if (blk + tg) % 2 == 0:
                        P.act(sg_, bk, AF.Copy, reads=[bb], writes=[sgb], scale=scl)
                    else:
                        P.ts("dve", sg_, bk, scl, None, ALU.mult, None, reads=[bb], writes=[sgb])
                    P.dma("sp", dst[r0:r0 + 128, n0 + tg * TG:n0 + (tg + 1) * TG], sg_, reads=[sgb])
            for vi, dst in enumerate((self.s_vsb, self.s_vfx)):
                P.dma("pool", wv, self.W_inV[li, vi], writes=[wvb])
                for k in range(8):
                    bk, bb = self.bank()
                    t = Ticket("pe")
                    for c in range(KC):
                        P.matmul(bk, uT[:, c, k * 128:(k + 1) * 128], wv[:, c * 512:(c + 1) * 512],
                                 c == 0, c == KC - 1, reads=[wvb, uTb[k // 4]], writes=[bb], ticket=t,
                                 last=(c == KC - 1))
                    sg_, sgb = r_stg.next()
                    if k % 2 == 0:
                        P.act(sg_, bk, AF.Copy, reads=[bb], writes=[sgb])
                    else:
                        P.copy("dve", sg_, bk, reads=[bb], writes=[sgb])
                    P.dma("sp", dst[n0 + k * 128:n0 + (k + 1) * 128, :], sg_, reads=[sgb])
            P.dma("pool", wf, self.W_inF[li], writes=[wfb])
            for tg in range(2):
                bk, bb = self.bank()
                t = Ticket("pe")
                for c in range(KC):
                    P.matmul(bk[0:8, :], wf[:, c * 8:(c + 1) * 8], uT[:, c, tg * TG:(tg + 1) * TG],
                             c == 0, c == KC - 1, reads=[wfb, uTb[tg]], writes=[bb], ticket=t, last=(c == KC - 1))
                P.act(lft[:, 0, :], bk[0:8, :], AF.Exp, reads=[bb, self.cb], writes=[lfb], scale=-1.0,
                      bias=self.nbfs[:, li:li + 1])
                P.act(lft[:, 1, :], lft[:, 0, :], AF.Ln, reads=[lfb], writes=[lfb], bias=self.one_col[0:8, 0:1])
                P.ts("dve", lft[:, 2, :], lft[:, 1, :], -1.0, None, ALU.mult, None, reads=[lfb], writes=[lfb])
                P.dma("sp", self.s_lf[:, n0 + tg * TG:n0 + (tg + 1) * TG], lft[:, 2, :], reads=[lfb])
            P.barrier()

    def prep_mem(self):
        P = self.P
        with self.open_scope() as st:
            self.norm_scratch(st)
            mt = self.sb_in(st, "memtile", [128, 2, D], F32)
            mb = [Buf("m0"), Buf("m1")]
            for k in range(2):
                P.dma("sp", mt[:, k, :], self.mem_in[k * 128:(k + 1) * 128, :], writes=[mb[k]])
            ub = Buf("memT")
            self.memTb = Buf("memTc", const=True)
            self.prenorm_src([(mt[:, k, :], mb[k]) for k in range(2)], self.memgs[:, 0:8], self.memT, lambda k: ub)
            P.barrier()

    def mem_kv(self, li, KmT, Vm, kvb):
        P = self.P
        with self.open_scope() as st:
            wmk = self.sb_in(st, "wmk", [128, 2, KC * 128], BF16)
            wmv = self.sb_in(st, "wmv", [128, KC * 512], BF16)
            r_w = Ring([wmk[:, k, :] for k in range(2)])
            wvb = Buf("wmv")
            for hm in range(4):
                w, wb = r_w.next()
                P.dma("pool", w, self.W_mk[li, hm], writes=[wb])
                bk, bb = self.bank()
                t = Ticket("pe")
                for c in range(KC):
                    P.matmul(bk[:, 0:256], w[:, c * 128:(c + 1) * 128], self.memT[:, c, :], c == 0, c == KC - 1,
                             reads=[wb], writes=[bb], ticket=t, last=(c == KC - 1))
                P.copy("dve", KmT[:, hm, :], bk[:, 0:256], reads=[bb], writes=[kvb])
            P.dma("pool", wmv, self.W_mv[li], writes=[wvb])
            for blk in range(2):
                bk, bb = self.bank()
                t = Ticket("pe")
                for c in range(KC):
                    P.matmul(bk, self.memT[:, c, blk * 128:(blk + 1) * 128], wmv[:, c * 512:(c + 1) * 512],
                             c == 0, c == KC - 1, reads=[wvb], writes=[bb], ticket=t, last=(c == KC - 1))
                P.copy("dve", Vm[:, blk, :], bk, reads=[bb], writes=[kvb])
            P.barrier()

    def attention(self, li):
        P, nc = self.P, self.nc
        with self.open_scope() as st0:
            KmT = self.sb_in(st0, "KmT", [128, 4, 256], BF16)
            Vm = self.sb_in(st0, "Vm", [128, 2, 512], BF16)
            kvb = Buf("memkv")
            self.mem_kv(li, KmT, Vm, kvb)
            with self.open_scope() as st:
                lfg = self.sb_in(st, "lfg", [8, S], F32)
                cT = self.sb_in(st, "cT", [8, S], F32)
                lfl = self.sb_in(st, "lfl", [8, NT], F32)
                cl = self.sb_in(st, "cl", [8, NT], F32)
                mrow = self.sb_in(st, "mrow", [8, NT], BF16)
                tot = self.sb_in(st, "tot", [8, 64], F32)
                sel = self.sb_in(st, "sel", [8, 32], F32)
                b1, b2, b3 = Buf("lfg"), Buf("lfl"), Buf("misc")
                t = P.dma_ticket(b1, "sp")
                for c in range(8):
                    r, gl = CHUNK_OWNER[c]
                    P.dma("sp", lfg[:, c * 512:(c + 1) * 512], self.g_lf[r, :, gl * 512:(gl + 1) * 512],
                          writes=[b1], ticket=t)
                P.dma("sp", lfl[:, :], self.s_lf[:, :], writes=[b2])
                P.dma("sp", sel[:, :], self.selc[:, :], writes=[b3])
                onesb = self.one_col[0:8, 0:1].to_broadcast([8, S])
                cTb = Buf("cT")
                P.op("dve", lambda e: e.tensor_tensor_scan(cT[:, :], onesb, lfg[:, :], 0.0, ALU.mult, ALU.add),
                     reads=[b1, self.cb], writes=[cTb])
                P.op("dve", lambda e: e.tensor_reduce(tot[:, 0:8], lfg[:, :].rearrange("p (c n) -> p c n", c=8),
                                                      mybir.AxisListType.X, ALU.add),
                     reads=[b1], writes=[b3])
                for g in range(4):
                    P.tt("dve", tot[:, 16 + g * 8:24 + g * 8], tot[:, 0:8], sel[:, g * 8:(g + 1) * 8], ALU.mult,
                         reads=[b3], writes=[b3])
                    P.op("dve", lambda e, g=g: e.tensor_reduce(tot[:, 8 + g:9 + g], tot[:, 16 + g * 8:24 + g * 8],
                                                               mybir.AxisListType.X, ALU.add),
                         reads=[b3], writes=[b3])
                clb = Buf("cl")
                for g in range(4):
                    ob = self.one_col[0:8, 0:1].to_broadcast([8, 512])
                    P.op("dve", lambda e, g=g, ob=ob: e.tensor_tensor_scan(
                        cl[:, g * 512:(g + 1) * 512], ob, lfl[:, g * 512:(g + 1) * 512],
                        tot[:, 8 + g:9 + g], ALU.mult, ALU.add), reads=[b2, b3, self.cb], writes=[clb])
                P.copy("dve", mrow[:, :], cl[:, :], reads=[clb], writes=[clb])
                mrow_t = P.dma("sp", self.s_mrow[:, :], mrow[:, :], reads=[clb])
                cs3 = self.sb_in(st, "cs3", [8, 3, S], BF16)
                csr = self.sb_in(st, "csr", [8, S], F32)
                csb = Buf("cs3")
                P.ts("dve", csr[:, :], cT[:, :], -1.0, None, ALU.mult, None, reads=[cTb], writes=[csb])
                P.copy("dve", cs3[:, 0, :], csr[:, :], reads=[csb], writes=[csb])
                P.tt("dve", csr[:, :], csr[:, :], cs3[:, 0, :], ALU.subtract, reads=[csb], writes=[csb])
                P.copy("dve", cs3[:, 1, :], csr[:, :], reads=[csb], writes=[csb])
                P.tt("dve", csr[:, :], csr[:, :], cs3[:, 1, :], ALU.subtract, reads=[csb], writes=[csb])
                P.copy("dve", cs3[:, 2, :], csr[:, :], reads=[csb], writes=[csb])
                crow_t = P.dma("sp", self.s_crow.rearrange("j h n -> h j n"), cs3[:, :, :], reads=[csb])
                if "cT" in self.dbg_out:
                    P.dma("sp", self.dbg_out["cT"][:, :], cT[:, :], reads=[cTb])
                if "cl" in self.dbg_out:
                    P.dma("sp", self.dbg_out["cl"][:, :], cl[:, :], reads=[clb])
                P.barrier()
            with self.open_scope() as st:
                mask = self.sb_in(st, "mask", [128, 32, 514], BF16)
                maskb = Buf("mask", const=True)
                P.dma("sp", mask[:, :, :], self.cmask.rearrange("p (u j) -> p u j", u=32), writes=[maskb])
                KT = [[self.sb_in(st, f"KT{s}{k}", [128, S], BF16) for k in range(2)] for s in range(2)]
                VA = [[self.sb_in(st, f"VA{s}{k}", [128, 32, 128], BF16) for k in range(2)] for s in range(2)]
                QT = [[self.sb_in(st, f"QT{s}{k}", [128, NT], BF16) for k in range(2)] for s in range(2)]
                KTb = [[Buf("KT") for k in range(2)] for s in range(2)]
                VAb = [[Buf("VA") for k in range(2)] for s in range(2)]
                QTb = [[Buf("QT") for k in range(2)] for s in range(2)]
                eT = self.sb_in(st, "eT", [128, 2, TG], F32)
                spT = self.sb_in(st, "spT", [128, 2, TG], BF16)
                wT = self.sb_in(st, "wT", [128, 3, TG], BF16)
                cbT = self.sb_in(st, "cbT", [128, 3, TG], BF16)
                pT = self.sb_in(st, "pT", [128, 3, TG], BF16)
                recT = self.sb_in(st, "recT", [128, 1, TG], F32)
                osT = self.sb_in(st, "osT", [128, 4, TG], BF16)
                r_e = Ring([eT[:, k, :] for k in range(2)])
                r_sp = Ring([spT[:, k, :] for k in range(2)])
                r_w = Ring([wT[:, k, :] for k in range(3)])
                r_cb = Ring([cbT[:, k, :] for k in range(3)])
                r_p = Ring([pT[:, k, :] for k in range(3)])
                r_rec = Ring([recT[:, k, :] for k in range(1)])
                r_os = Ring([osT[:, k, :] for k in range(4)])
                cst = Buf("attnconst")
                for s in range(2):
                    for k in range(2):
                        P.memset("pool", KT[s][k][64:128, :], 0.0, writes=[KTb[s][k]])
                        P.memset("pool", QT[s][k][64:128, :], 1.0 if s == 1 else 0.0, writes=[QTb[s][k]])
                        if s == 1:
                            P.memset("pool", KT[s][k][64:65, :], 1.0, writes=[KTb[s][k]])
                            P.memset("pool", VA[s][k][:, :, 64:128], 1.0, writes=[VAb[s][k]])

                def bankk(k):
                    return self.ps2[k // 2][:, (k % 2) * 512:(k % 2) * 512 + 512], self.psb[k]
                xs_ring = [bankk(0), bankk(1), bankk(2)]
                ots_ring = [bankk(3), bankk(4)]
                xf_ring = [bankk(5), bankk(6)]
                otf, otfb = bankk(7)
                self.bank_ctr = 0

                def load_head(h):
                    k = h % 2
                    for s, (gk, gv, sq) in enumerate(((self.g_ksb, self.g_vsb, self.s_qsb),
                                                      (self.g_kfx, self.g_vfx, self.s_qfx))):
                        tk = P.dma_ticket(KTb[s][k], "sp")
                        for c in range(8):
                            r, gl = CHUNK_OWNER[c]
                            P.dma("sp", KT[s][k][0:64, c * 512:(c + 1) * 512],
                                  gk[r, h * 64:(h + 1) * 64, gl * 512:(gl + 1) * 512], writes=[KTb[s][k]], ticket=tk)
                        if s == 1:
                            for j in range(3):
                                P.dma("sp", KT[s][k][65 + j:66 + j, :], self.s_crow[j, h:h + 1, :], writes=[KTb[s][k]],
                                      ticket=tk, waits=[crow_t])
                        if s == 1 or h % 2 == 0:
                            kv = k if s == 1 else (h // 2) % 2
                            tv = P.dma_ticket(VAb[s][kv], "sp")
                            ncol = 64 if s == 1 else 128
                            c0 = h * 64
                            for c in range(8):
                                r, gl = CHUNK_OWNER[c]
                                P.dma("sp", VA[s][kv][:, c * 4:(c + 1) * 4, 0:ncol],
                                      gv[r, gl * 512:(gl + 1) * 512, c0:c0 + ncol].rearrange("(b p) d -> p b d", p=128),
                                      writes=[VAb[s][kv]], ticket=tv)
                        tq = P.dma_ticket(QTb[s][k], "sp")
                        P.dma("sp", QT[s][k][0:64, :], sq[h * 64:(h + 1) * 64, :], writes=[QTb[s][k]], ticket=tq)
                        if s == 1:
                            P.dma("sp", QT[s][k][64:65, :], self.s_mrow[h:h + 1, :], writes=[QTb[s][k]], ticket=tq,
                                  waits=[mrow_t])

                def sbA(u):
                    h, g, kb, first, lastu = u[:5]
                    k = h % 2
                    x, xb = xs_ring[u[5] % 3]
                    masked = kb >= 8 * g
                    t = Ticket("pe")
                    P.matmul(x, KT[0][k][:, kb * 128:(kb + 1) * 128], QT[0][k][:, g * TG:(g + 1) * TG],
                             True, True, reads=[KTb[0][k], QTb[0][k]], writes=[xb], ticket=t, last=not masked)
                    if masked:
                        P.op("pe", lambda e: e.matmul(x, self.ident[:, :], mask[:, kb, 0:512], start=False, stop=True,
                                                      skip_group_check=True),
                             reads=[maskb, self.cb], writes=[xb], ticket=t, last=True)
                    e_, eb = r_e.next()
                    P.act(e_, x, AF.Exp, reads=[xb], writes=[eb])
                    sp, spb = r_sp.next()
                    P.act(sp, e_, AF.Ln, reads=[eb], writes=[spb], bias=1.0)
                    u[6]["x"] = (x, xb)
                    u[6]["sp"] = (sp, spb)
                    if not lastu:
                        rn, rnb = r_cb.next()
                        if first:
                            P.copy("pool", rn, sp, reads=[spb], writes=[rnb])
                        else:
                            rp, rpb = u[6]["rprev"]
                            P.tt("pool", rn, rp, sp, ALU.add, reads=[rpb, spb], writes=[rnb])
                        u[6]["R"] = (rn, rnb)

                def sbB(u):
                    h, g, kb, first, lastu = u[:5]
                    x, xb = u[6]["x"]
                    sp, spb = u[6]["sp"]
                    t = Ticket("pe")
                    P.op("pe", lambda e: e.matmul(x, self.negtri[:, :], sp, start=False, stop=True, skip_group_check=True),
                         reads=[spb, self.cb], writes=[xb], ticket=t, last=first)
                    if not first:
                        rp, rpb = u[6]["rprev"]
                        P.op("pe", lambda e: e.matmul(x, self.negones[:, :], rp, start=False, stop=True,
                                                      skip_group_check=True),
                             reads=[rpb, self.cb], writes=[xb], ticket=t, last=True)
                    w, wb = r_w.next()
                    P.act(w, x, AF.Exp, reads=[xb], writes=[wb])
                    u[6]["w"] = (w, wb)

                def sbC(u):
                    h, g, kb, first, lastu = u[:5]
                    kv = (h // 2) % 2
                    r0 = (h % 2) * 64
                    w, wb = u[6]["w"]
                    ots, otsb = ots_ring[(h * 4 + g) % 2]
                    P.op("pe", lambda e: e.matmul(ots, VA[0][kv][:, kb, :], w, start=first, stop=lastu,
                                                  skip_group_check=True),
                         reads=[wb, VAb[0][kv]], writes=[otsb])
                    if lastu:
                        os_, osb_ = r_os.next()
                        P.copy("dve", os_[r0:r0 + 64, :], ots[r0:r0 + 64, :], reads=[otsb], writes=[osb_])
                        P.dma("sp", self.s_osb[h * 64:(h + 1) * 64, g * TG:(g + 1) * TG], os_[r0:r0 + 64, :],
                              reads=[osb_])

                def fxA(u):
                    h, g, kb, first, lastu = u[:5]
                    k = h % 2
                    x, xb = xf_ring[u[5] % 2]
                    masked = kb >= 8 * g
                    t = Ticket("pe")
                    P.matmul(x, KT[1][k][:, kb * 128:(kb + 1) * 128], QT[1][k][:, g * TG:(g + 1) * TG],
                             True, True, reads=[KTb[1][k], QTb[1][k]], writes=[xb], ticket=t, last=not masked)
                    if masked:
                        P.op("pe", lambda e: e.matmul(x, self.ident[:, :], mask[:, kb, 1:513], start=False, stop=True,
                                                      skip_group_check=True),
                             reads=[maskb, self.cb], writes=[xb], ticket=t, last=True)
                    p, pb = r_p.next()
                    P.act(p, x, AF.Exp, reads=[xb], writes=[pb])
                    u[6]["p"] = (p, pb)

                def fxC(u):
                    h, g, kb, first, lastu = u[:5]
                    k = h % 2
                    p, pb = u[6]["p"]
                    P.op("pe", lambda e: e.matmul(otf, VA[1][k][:, kb, :], p, start=first, stop=lastu,
                                                  skip_group_check=True),
                         reads=[pb, VAb[1][k]], writes=[otfb])
                    if lastu:
                        r0 = 0
                        d0 = 64
                        rec, recb = r_rec.next()
                        P.op("dve", lambda e: e.reciprocal(rec[d0:d0 + 64, :], otf[d0:d0 + 64, :]),
                             reads=[otfb], writes=[recb])
                        os_, osb_ = r_os.next()
                        P.tt("dve", os_[r0:r0 + 64, :], otf[r0:r0 + 64, :], rec[d0:d0 + 64, :], ALU.mult,
                             reads=[otfb, recb], writes=[osb_])
                        P.dma("sp", self.s_ofx[h * 64:(h + 1) * 64, g * TG:(g + 1) * TG], os_[r0:r0 + 64, :],
                              reads=[osb_])

                units = []
                for h in range(8):
                    for g in range(4):
                        n = 8 * (g + 1)
                        for kb in range(n - 1, -1, -1):
                            units.append([h, g, kb, kb == n - 1, kb == 0, len(units), {}])
                units_f = [[u[0], u[1], u[2], u[3], u[4], u[5], {}] for u in units]
                load_head(0)
                N = len(units)
                for i in range(N + 2):
                    if i < N:
                        u = units[i]
                        if u[1] == 0 and u[2] == 4 and u[0] + 1 < 8:
                            load_head(u[0] + 1)
                        if not u[3]:
                            u[6]["rprev"] = units[i - 1][6]["R"]
                        sbA(units[i])
                        fxA(units_f[i])
                    if 1 <= i <= N:
                        ub_ = units[i - 1]
                        sbB(ub_)
                        fxC(units_f[i - 1])
                    if 2 <= i <= N + 1:
                        sbC(units[i - 2])

                P.barrier()
            with self.open_scope() as st:
                pT = self.sb_in(st, "pTm", [128, 3, TG], BF16)
                recT = self.sb_in(st, "recTm", [128, 2, TG], F32)
                osT = self.sb_in(st, "osTm", [128, 4, TG], BF16)
                r_p = Ring([pT[:, k, :] for k in range(3)])
                r_rec = Ring([recT[:, k, :] for k in range(2)])
                r_os = Ring([osT[:, k, :] for k in range(4)])
                qm = self.sb_in(st, "qm", [128, 2, NT], BF16)
                qmb = [Buf("qm0"), Buf("qm1")]
                scale_m = 128.0 ** -0.5
                for hm in range(4):
                    k = hm % 2
                    P.dma("sp", qm[:, k, :], self.s_qmem[hm * 128:(hm + 1) * 128, :], writes=[qmb[k]])
                    for g in range(4):
                        ps = []
                        for blk in range(2):
                            bk, bb = self.bank()
                            P.matmul(bk, KmT[:, hm, blk * 128:(blk + 1) * 128], qm[:, k, g * TG:(g + 1) * TG], True, True,
                                     reads=[kvb, qmb[k]], writes=[bb])
                            p, pb = r_p.next()
                            P.act(p, bk, AF.Exp, reads=[bb], writes=[pb], scale=scale_m)
                            ps.append((p, pb))
                        ok, okb = self.bank()
                        dk, dkb = self.bank()
                        t = Ticket("pe")
                        for blk in range(2):
                            P.matmul(ok, Vm[:, blk, hm * 128:(hm + 1) * 128], ps[blk][0], blk == 0, blk == 1,
                                     reads=[kvb, ps[blk][1]], writes=[okb], ticket=t, last=(blk == 1))
                        t = Ticket("pe")
                        for blk in range(2):
                            P.matmul(dk, self.ones[:, :], ps[blk][0], blk == 0, blk == 1,
                                     reads=[self.cb, ps[blk][1]], writes=[dkb], ticket=t, last=(blk == 1))
                        rec, recb = r_rec.next()
                        P.op("dve", lambda e, rec=rec, dk=dk: e.reciprocal(rec, dk), reads=[dkb], writes=[recb])
                        os_, osb_ = r_os.next()
                        P.tt("dve", os_, ok, rec, ALU.mult, reads=[okb, recb], writes=[osb_])
                        P.dma("sp", self.s_omem[hm * 128:(hm + 1) * 128, g * TG:(g + 1) * TG], os_, reads=[osb_])
                P.barrier()

    def merge(self, li, sgi):
        P = self.P
        with self.open_scope() as st:
            self.norm_scratch(st)
            uT = self.sb_in(st, "uT", [128, KC, SG], BF16)
            oT = [self.sb_in(st, f"oT{b}", [128, 4, SG], BF16) for b in range(3)]
            mT = self.sb_in(st, "mT", [128, KC, SG], BF16)
            wgt = self.sb_in(st, "wgt", [128, 6, KC * 128], BF16)
            wbr = self.sb_in(st, "wbr", [128, 6, 4 * 128], BF16)
            wo = self.sb_in(st, "wo", [128, KC * D], BF16)
            sg = self.sb_in(st, "sg", [128, 3, TG], F32)
            mm = self.sb_in(st, "mm", [128, 4, TG], F32)
            r_wg = Ring([wgt[:, k, :] for k in range(6)])
            r_wb = Ring([wbr[:, k, :] for k in range(6)])
            r_sg = Ring([sg[:, k, :] for k in range(3)])
            r_mm = Ring([mm[:, k, :] for k in range(4)])
            uTb = [Buf("uT0"), Buf("uT1")]
            oTb = [Buf("oT") for _ in range(3)]
            mTb = [Buf("mT0"), Buf("mT1")]
            wob = Buf("wo")
            n0 = sgi * SG
            self.load_gpost(li, 1)
            for b, src_ in enumerate((self.s_osb, self.s_ofx, self.s_omem)):
                t = P.dma_ticket(oTb[b], "sp")
                for c in range(4):
                    P.dma("sp", oT[b][:, c, :], src_[c * 128:(c + 1) * 128, n0:n0 + SG], writes=[oTb[b]], ticket=t)
            P.dma("pool", wo, self.W_out[li], writes=[wob])
            gcol = self.pregs[:, (li * 3 + 1) * 8:(li * 3 + 1) * 8 + 8]
            self.prenorm([sgi * 8 + k for k in range(8)], gcol, uT, lambda k: uTb[k // 4])
            for fc in range(8):
                ws = []
                for b in range(3):
                    wg_, wgb = r_wg.next()
                    wb_, wbb = r_wb.next()
                    P.dma("pool", wg_, self.W_gate[li, b * 8 + fc], writes=[wgb])
                    P.dma("pool", wb_, self.W_br[li, b * 8 + fc], writes=[wbb])
                    ws.append((wg_, wgb, wb_, wbb))
                for tg in range(2):
                    ms = []
                    for b in range(3):
                        wg_, wgb, wb_, wbb = ws[b]
                        gk, gb = self.bank()
                        t = Ticket("pe")
                        for c in range(KC):
                            P.matmul(gk, wg_[:, c * 128:(c + 1) * 128], uT[:, c, tg * TG:(tg + 1) * TG],
                                     c == 0, c == KC - 1, reads=[wgb, uTb[tg]], writes=[gb], ticket=t,
                                     last=(c == KC - 1))
                        s_, sb_ = r_sg.next()
                        col = li * 24 + b * 8 + fc
                        P.act(s_, gk, AF.Sigmoid, reads=[gb, self.cb], writes=[sb_], bias=self.bgs[:, col:col + 1])
                        bk, bb = self.bank()
                        t = Ticket("pe")
                        for c in range(4):
                            P.matmul(bk, wb_[:, c * 128:(c + 1) * 128], oT[b][:, c, tg * TG:(tg + 1) * TG],
                                     c == 0, c == 3, reads=[wbb, oTb[b]], writes=[bb], ticket=t, last=(c == 3))
                        m_, mb_ = r_mm.next()
                        P.tt("dve", m_, s_, bk, ALU.mult, reads=[sb_, bb], writes=[mb_])
                        ms.append((m_, mb_))
                    P.tt("pool", ms[0][0], ms[0][0], ms[1][0], ALU.add, reads=[ms[0][1], ms[1][1]], writes=[ms[0][1]])
                    P.tt("pool", mT[:, fc, tg * TG:(tg + 1) * TG], ms[0][0], ms[2][0], ALU.add,
                         reads=[ms[0][1], ms[2][1]], writes=[mTb[tg]])
            gp = self.gpost[:, :]
            for k in range(8):
                o2, o2b = self.bank2()
                for hf in range(2):
                    t = Ticket("pe")
                    for c in range(KC):
                        P.matmul(o2[:, hf * 512:(hf + 1) * 512], mT[:, c, k * 128:(k + 1) * 128],
                                 wo[:, c * D + hf * 512:c * D + (hf + 1) * 512], c == 0, c == KC - 1,
                                 reads=[mTb[k // 4], wob], writes=[o2b[hf]], ticket=t, last=(c == KC - 1))
                self.postnorm(o2, o2b, gp, 1.0, sgi * 8 + k)
            P.barrier()

    def exchange(self, li):
        P = self.P
        P.barrier()
        groups = [[2 * i, 2 * i + 1] for i in range(self.ncores // 2)]
        for loc_, gat in ((self.s_ksb, self.g_ksb), (self.s_kfx, self.g_kfx), (self.s_vsb, self.g_vsb),
                          (self.s_vfx, self.g_vfx), (self.s_lf, self.g_lf)):
            P.collective(loc_[:, :], gat.rearrange("r a b -> (r a) b"), groups)
        P.barrier()


_BF = ml_dtypes.bfloat16
_CACHE = {}


def _prog(mode, nl):
    key = (mode, nl)
    if key not in _CACHE:
        b = Builder(mode, list(range(nl)))
        _CACHE[key] = b.build()
    return _CACHE[key]


def _pkn(w, n):
    lead = w.shape[:-2]
    k = w.shape[-2] // 128
    w = w.reshape(lead + (k, 128, n))
    w = np.swapaxes(w, -3, -2)
    return np.ascontiguousarray(w.reshape(lead + (128, k * n)))


def _blocks(w, starts, width):
    return np.stack([_pkn(w[:, :, s:s + width], width) for s in starts], axis=1)


def _consts():
    cm = np.zeros((4, 128, 128), np.float32)
    cm[3] = -1.0
    cm[0] = np.eye(128, dtype=np.float32)
    j = np.arange(128)[:, None]
    s = np.arange(128)[None, :]
    cm[1] = np.where(j >= s, -1.0, 0.0)
    cm[2] = 1.0
    masks, sels = [], []
    for r in range(2):
        m = np.zeros((128, 32, 514), np.float32)
        p = np.arange(128)[:, None]
        jj = np.arange(514)[None, :]
        for kb in range(32):
            g = kb // 8
            c = RANK_CHUNKS[r][g]
            qpos = 512 * c + jj - 1
            kpos = 128 * kb + p
            m[:, kb, :] = np.where(kpos > qpos, NEG, 0.0)
        masks.append(m.reshape(128, 32 * 514).astype(_BF))
        sel = np.zeros((8, 4, 8), np.float32)
        for g in range(4):
            sel[:, g, :RANK_CHUNKS[r][g]] = 1.0
        sels.append(sel.reshape(8, 32))
    return cm, masks, sels


def _layout(inp):
    f = lambda k: np.asarray(inp[k], np.float32)
    W = {}
    for tag, pre in (("f1", "ffn1"), ("f2", "ffn2")):
        W[tag + "g"] = _blocks(f(pre + "_w_gate"), [j * 128 for j in range(FC)], 128)
        W[tag + "u"] = _blocks(f(pre + "_w_up"), [j * 128 for j in range(FC)], 128)
        W[tag + "d"] = np.ascontiguousarray(f(pre + "_w_down").reshape(L, FC, 128, D))
    w_in = f("w_in")
    st = [0, 128, 256, 384, 512, 640, 768, 896, 1536, 1664, 1792, 1920, 2048, 2176, 2304, 2432,
          3080, 3208, 3336, 3464]
    W["winT"] = _blocks(w_in, st, 128)
    W["winV"] = _blocks(w_in, [1024, 2560], 512)
    W["winF"] = _pkn(w_in[:, :, 3072:3080], 8)
    W["nbf"] = np.ascontiguousarray(np.transpose(f("b_forget"), (1, 0)))
    wg = f("w_gate")
    W["wgate"] = _blocks(wg, [b * 1024 + fc * 128 for b in range(3) for fc in range(8)], 128)
    W["bgT"] = np.ascontiguousarray(f("b_gate").reshape(L, 24, 128).transpose(2, 0, 1).reshape(128, L * 24))
    br = [f("w_br_sb"), f("w_br_fox"), f("w_br_mem")]
    W["wbr"] = np.stack([_pkn(br[b][:, :, fc * 128:(fc + 1) * 128], 128) for b in range(3) for fc in range(8)], axis=1)
    W["wout"] = _pkn(f("w_out"), D)
    wm = f("w_mem_kv")
    W["wmk"] = _blocks(wm, [0, 128, 256, 384], 128)
    W["wmv"] = _pkn(wm[:, :, 512:1024], 512)
    pre = np.stack([f("ffn1_pre_g"), f("mix_pre_g"), f("ffn2_pre_g")], axis=1)
    W["pregT"] = np.ascontiguousarray(pre.reshape(L, 3, 8, 128).transpose(3, 0, 1, 2).reshape(128, L * 24))
    W["postg"] = np.ascontiguousarray(
        np.stack([f("ffn1_post_g"), f("mix_post_g"), f("ffn2_post_g")], axis=1).reshape(L * 3, D))
    W["memgT"] = np.ascontiguousarray(f("mem_norm_g").reshape(8, 128).T)
    return W


_PER_LAYER = {"f1g", "f1u", "f1d", "f2g", "f2u", "f2d", "winT", "winV", "winF", "wgate", "wbr", "wout", "wmk", "wmv"}


def _layer_slice(W, name, l):
    a = W[name]
    if name in _PER_LAYER:
        return a[l:l + 1]
    if name == "nbf":
        return np.ascontiguousarray(a[:, l:l + 1])
    if name in ("bgT", "pregT"):
        return np.ascontiguousarray(a[:, l * 24:(l + 1) * 24])
    if name == "postg":
        return a[l * 3:(l + 1) * 3]
    return a


A_NAMES = ["pregT", "postg", "f1g", "f1u", "f1d", "winT", "winV", "winF", "nbf"]
BC_NAMES = ["pregT", "postg", "f2g", "f2u", "f2d", "wgate", "bgT", "wbr", "wout", "wmk", "wmv", "memgT"]
LOC = ["s_qsb", "s_qfx", "s_qmem", "s_ksb", "s_kfx", "s_vsb", "s_vfx", "s_lf"]


def kernel(**inputs):
    x = np.asarray(inputs["x"], np.float32)
    mem = np.asarray(inputs["mem"], np.float32)
    W = _layout(inputs)
    cm, masks, sels = _consts()
    ncore = 8
    prog = _prog("FULL", L)
    maps = []
    for c in range(ncore):
        b, r = c // 2, c % 2
        m = {"x_in": np.ascontiguousarray(np.concatenate([x[b, ch * 512:(ch + 1) * 512] for ch in RANK_CHUNKS[r]], 0)),
             "cmat": cm, "mem": np.ascontiguousarray(mem[b]), "cmask": masks[r], "selc": sels[r]}
        for n in set(A_NAMES + BC_NAMES):
            m[n] = W[n]
        maps.append(m)
    res = run_bass_kernel_spmd(prog, maps, core_ids=list(range(ncore))).results
    out = np.zeros((4, S, D), np.float32)
    for c in range(ncore):
        b, r = c // 2, c % 2
        hc = np.asarray(res[c]["h_out"])
        for g, ch in enumerate(RANK_CHUNKS[r]):
            out[b, ch * 512:(ch + 1) * 512] = hc[g * 512:(g + 1) * 512]
    return out
```

```python
import numpy as np
import ml_dtypes
from contextlib import ExitStack

import concourse.bass as bass
import concourse.mybir as mybir
from concourse.bass_utils import run_bass_kernel_spmd

F32 = mybir.dt.float32
BF16 = mybir.dt.bfloat16
AF = mybir.ActivationFunctionType
ALU = mybir.AluOpType

L = 4
D = 1024
KC = 8
FF = 2816
FC = 22
S = 4096
NT = 2048
NTILE = 16
TG = 512
SG = 1024
NSG = NT // SG
IN_W = 3592
EPS = 1e-6
NEG = -30000.0
RANK_CHUNKS = [[0, 3, 4, 7], [1, 2, 5, 6]]
CHUNK_OWNER = {}
for _r in range(2):
    for _g, _c in enumerate(RANK_CHUNKS[_r]):
        CHUNK_OWNER[_c] = (_r, _g)

ENGS = ("pe", "act", "dve", "pool", "sp")


def _ap(x):
    if isinstance(x, bass.AP):
        return x
    return x[tuple(slice(None) for _ in x.shape)]


class Ticket:
    __slots__ = ("eng", "sem", "val", "pos")

    def __init__(self, eng):
        self.eng = eng
        self.sem = None
        self.val = None
        self.pos = None


class Buf:
    __slots__ = ("name", "w", "r", "const", "dsem", "dcount", "dq", "excl")

    def __init__(self, name, const=False):
        self.name = name
        self.w = []
        self.r = []
        self.const = const
        self.dsem = None
        self.dcount = 0
        self.dq = None
        self.excl = False


class Item:
    __slots__ = ("fn", "deps", "sig", "dma_t", "inc1")

    def __init__(self, fn, deps, sig, dma_t=None):
        self.fn = fn
        self.deps = deps
        self.sig = sig
        self.dma_t = dma_t
        self.inc1 = False


class Prog:
    def __init__(self, nc, stack):
        self.nc = nc
        self.stack = stack
        self.items = {e: [] for e in ENGS}
        self.nsem = 0
        self.pre = {}
        self.pending_dma = []
        self.sem_free = {"sp": [], "pool": [], "act": []}
        self.scopes = [[]]

    def new_sem(self, name):
        self.nsem += 1
        return self.stack.enter_context(self.nc.semaphore(f"{name}_{self.nsem}"))

    def barrier(self):
        ts = []
        for eng in ENGS:
            if eng == "sp" or not self.items[eng]:
                continue
            for it in reversed(self.items[eng]):
                if it.dma_t is not None:
                    continue
                assert it.sig is not None, eng
                ts.append(it.sig)
                break
        ts += self.pending_dma
        self.pending_dma = []
        for eng in ENGS:
            self.pre[eng] = self.pre.get(eng, []) + ts

    def push_scope(self):
        self.scopes.append([])

    def pop_scope(self):
        for b in self.scopes.pop():
            if b.dsem is not None and b.dcount < 24000:
                self.sem_free[b.dq].append((b.dsem, b.dcount))
            b.dsem = None
            b.dq = None

    def _deps(self, eng, t, reads, writes, waits):
        deps = list(self.pre.pop(eng, []))
        for b in reads:
            deps += b.w
            if b.excl:
                deps += [d for d in b.r if d.eng != eng]
        for b in writes:
            deps += b.w
            deps += b.r
        deps += list(waits)
        out = []
        seen = set()
        for d in deps:
            if d is t or id(d) in seen:
                continue
            if eng == "pe" and d.eng == "pe":
                continue
            seen.add(id(d))
            out.append(d)
        return out

    def _book(self, t, reads, writes):
        for b in reads:
            if b.const:
                continue
            if not b.r or b.r[-1] is not t:
                b.r.append(t)
        for b in writes:
            if b.const:
                if not b.w or b.w[-1] is not t:
                    b.w.append(t)
            else:
                b.w = [t]
                b.r = []

    def op(self, eng, fn, reads=(), writes=(), ticket=None, last=True, waits=()):
        t = ticket if ticket is not None else Ticket(eng)
        deps = self._deps(eng, t, reads, writes, waits)
        self.items[eng].append(Item(fn, deps, t if last else None))
        self._book(t, reads, writes)
        return t

    def dma_ticket(self, buf, q):
        t = Ticket("dma")
        assert buf.dq in (None, q), (buf.name, buf.dq, q)
        buf.dq = q
        if buf.dsem is None:
            if self.sem_free[q]:
                buf.dsem, buf.dcount = self.sem_free[q].pop()
            else:
                buf.dsem = self.new_sem("d")
                buf.dcount = 0
            self.scopes[-1].append(buf)
        t.sem = buf.dsem
        t.val = buf.dcount
        t.pos = buf
        return t

    def dma(self, q, out_ap, in_ap, reads=(), writes=(), sembuf=None, ticket=None, waits=()):
        sb = sembuf if sembuf is not None else (writes[0] if writes else reads[0])
        t = ticket if ticket is not None else self.dma_ticket(sb, q)
        assert t.pos is sb and sb.dq == q
        sb.dcount += 16
        t.val = sb.dcount
        deps = self._deps(q, t, reads, writes, waits)

        out_ap, in_ap = _ap(out_ap), _ap(in_ap)

        def fn(e, out_ap=out_ap, in_ap=in_ap):
            return e.dma_start(out=out_ap, in_=in_ap)

        self.items[q].append(Item(fn, deps, None, dma_t=t))
        self._book(t, reads, writes)
        if not self.pending_dma or self.pending_dma[-1] is not t:
            self.pending_dma.append(t)
        return t

    def wait_all(self, t):
        for eng in ENGS:
            self.pre[eng] = self.pre.get(eng, []) + [t]

    def collective(self, in_ap, out_ap, groups):
        if not hasattr(self, "ccsem"):
            self.ccsem = self.new_sem("cc")
            self.cccount = 0
        self.cccount += 1
        t = Ticket("dma")
        t.sem = self.ccsem
        t.val = self.cccount
        deps = self._deps("pool", t, (), (), ())

        def fn(e):
            return e.collective_compute("AllGather", ALU.bypass, replica_groups=groups, ins=[in_ap], outs=[out_ap])

        it = Item(fn, deps, None, dma_t=t)
        it.inc1 = True
        self.items["pool"].append(it)
        return t

    def matmul(self, out, lhsT, rhs, start, stop, reads, writes, ticket=None, last=True):
        return self.op("pe", lambda e: e.matmul(out, lhsT, rhs, start=start, stop=stop),
                       reads, writes, ticket, last)

    def transpose(self, out, in_, ident, reads, writes, ticket=None, last=True):
        return self.op("pe", lambda e: e.transpose(out, in_, ident), reads, writes, ticket, last)

    def act(self, out, in_, func, reads, writes, bias=None, scale=None, accum_out=None, eng="act"):
        kw = {}
        if bias is not None:
            kw["bias"] = bias
        if scale is not None:
            kw["scale"] = scale
        if accum_out is not None:
            kw["accum_out"] = accum_out
        return self.op("act", lambda e: e.activation(out, in_, func, **kw), reads, writes)

    def tt(self, eng, out, in0, in1, op, reads, writes):
        return self.op(eng, lambda e: e.tensor_tensor(out, in0, in1, op), reads, writes)

    def ts(self, eng, out, in0, s1, s2, op0, op1, reads, writes):
        if op1 is None:
            return self.op(eng, lambda e: e.tensor_scalar(out, in0, s1, None, op0), reads, writes)
        return self.op(eng, lambda e: e.tensor_scalar(out, in0, s1, s2, op0, op1), reads, writes)

    def stt(self, out, in0, scalar, in1, op0, op1, reads, writes):
        return self.op("dve", lambda e: e.scalar_tensor_tensor(out, in0, scalar, in1, op0, op1),
                       reads, writes)

    def copy(self, eng, out, in_, reads, writes):
        if eng == "act":
            return self.op("act", lambda e: e.copy(out, in_), reads, writes)
        return self.op(eng, lambda e: e.tensor_copy(out, in_), reads, writes)

    def memset(self, eng, ap, val, writes):
        return self.op(eng, lambda e: e.memset(ap, val), (), writes)

    def finalize(self):
        LIM = 30000
        for eng in ENGS:
            if eng == "sp":
                continue
            sem = None
            cnt = 0
            for pos, it in enumerate(self.items[eng]):
                if it.sig is not None:
                    if sem is None or cnt >= LIM:
                        sem = self.new_sem("e" + eng)
                        cnt = 0
                    cnt += 1
                    it.sig.sem = sem
                    it.sig.val = cnt
                    it.sig.pos = pos
        for eng in ENGS:
            for pos, it in enumerate(self.items[eng]):
                for d in it.deps:
                    assert d.val is not None and d.sem is not None, (eng, pos, d.eng)
                    if d.eng == eng:
                        assert d.pos < pos, ("self-deadlock", eng, pos, d.pos)

    def replay(self, eng, e):
        waited = {}
        for it in self.items[eng]:
            for d in it.deps:
                k = id(d.sem)
                if waited.get(k, 0) < d.val:
                    e.wait_ge(d.sem, d.val)
                    waited[k] = d.val
            ins = it.fn(e)
            if it.dma_t is not None:
                if it.inc1:
                    ins.then_inc(it.dma_t.sem)
                else:
                    ins.then_inc(it.dma_t.sem, 16)
            elif it.sig is not None:
                ins.then_inc(it.sig.sem, 1)


class Scope(ExitStack):
    def __init__(self, P):
        super().__init__()
        self.P = P
        P.push_scope()

    def __exit__(self, *a):
        self.P.barrier()
        self.P.pop_scope()
        return super().__exit__(*a)


class Ring:
    def __init__(self, aps):
        self.aps = aps
        self.bufs = [Buf(f"ring{i}") for i in range(len(aps))]
        self.i = 0

    def next(self):
        k = self.i % len(self.aps)
        self.i += 1
        return self.aps[k], self.bufs[k]


class Builder:
    def __init__(self, mode, layers, dbg=None, ncores=8):
        self.mode = mode
        self.layers = layers
        self.ncores = ncores
        self.dbg = dbg or {}
        self.nc = bass.Bass("TRN2", target_bir_lowering=False)
        self.stack = ExitStack()
        self.P = Prog(self.nc, self.stack)
        self.dram = {}
        self.final_tickets = []

    def din(self, name, shape, dt=F32):
        t = self.nc.dram_tensor(name, list(shape), dt, kind="ExternalInput")
        self.dram[name] = t
        return t.ap()

    def dout(self, name, shape, dt=F32):
        t = self.nc.dram_tensor(name, list(shape), dt, kind="ExternalOutput")
        self.dram[name] = t
        return t.ap()

    def dscr(self, name, shape, dt):
        t = self.nc.dram_tensor(name, list(shape), dt)
        self.dram[name] = t
        return t.ap()

    def sb(self, name, shape, dt):
        return self.stack.enter_context(self.nc.sbuf_tensor(name, list(shape), dt))

    def sb_in(self, st, name, shape, dt):
        self.uid = getattr(self, "uid", 0) + 1
        if not hasattr(self, "tn"):
            self.tn = {}
        self.tn.setdefault(name, []).append(f"{name}_{self.uid}")
        return st.enter_context(self.nc.sbuf_tensor(f"{name}_{self.uid}", list(shape), dt))

    def build(self):
        nc, P = self.nc, self.P
        mode = self.mode
        nl = len(self.layers)
        self.nl = nl
        hasA = mode in ("A", "FULL")
        hasBC = mode in ("BC", "FULL")
        full = mode == "FULL"

        self.x_in = self.din("x_in", [NT, D])
        self.h_out = self.dout("h_out", [NT, D])
        self.cmat = self.din("cmat", [4, 128, 128])
        self.pregT = self.din("pregT", [128, nl * 3 * 8])
        self.postg = self.din("postg", [nl * 3, D])
        if hasA:
            self.W_f1g = self.din("f1g", [nl, FC, 128, KC * 128])
            self.W_f1u = self.din("f1u", [nl, FC, 128, KC * 128])
            self.W_f1d = self.din("f1d", [nl, FC, 128, D])
            self.W_inT = self.din("winT", [nl, 20, 128, KC * 128])
            self.W_inV = self.din("winV", [nl, 2, 128, KC * 512])
            self.W_inF = self.din("winF", [nl, 128, KC * 8])
            self.nbf = self.din("nbf", [8, nl])
        if hasBC:
            self.W_f2g = self.din("f2g", [nl, FC, 128, KC * 128])
            self.W_f2u = self.din("f2u", [nl, FC, 128, KC * 128])
            self.W_f2d = self.din("f2d", [nl, FC, 128, D])
            self.W_gate = self.din("wgate", [nl, 24, 128, KC * 128])
            self.bgT = self.din("bgT", [128, nl * 24])
            self.W_br = self.din("wbr", [nl, 24, 128, 4 * 128])
            self.W_out = self.din("wout", [nl, 128, KC * D])
            self.W_mk = self.din("wmk", [nl, 4, 128, KC * 128])
            self.W_mv = self.din("wmv", [nl, 128, KC * 512])
            self.mem_in = self.din("mem", [256, D])
            self.memgT = self.din("memgT", [128, 8])
            self.cmask = self.din("cmask", [128, 32 * 514], BF16)
            self.selc = self.din("selc", [8, 32])
        mk_loc_in = self.din if mode == "BC" else (self.dout if mode == "A" else self.dscr)
        mk_loc_out = self.dout if mode == "A" else self.dscr

        def loc(name, shape, dt, needed_in_bc):
            if mode == "A":
                return self.dout(name, shape, dt)
            if mode == "BC":
                return self.din(name, shape, dt) if needed_in_bc else None
            return self.dscr(name, shape, dt)

        self.s_qsb = loc("s_qsb", [512, NT], BF16, True)
        self.s_qfx = loc("s_qfx", [512, NT], BF16, True)
        self.s_qmem = loc("s_qmem", [512, NT], BF16, True)
        if False:
            self.s_ksb = pk[0:512, :]
            self.s_kfx = pk[512:1024, :]
            self.s_vsb = pk[1024:1536, :].rearrange("r (b c) -> (r b) c", c=512)
            self.s_vfx = pk[1536:2048, :].rearrange("r (b c) -> (r b) c", c=512)
            self.s_lf = self.s_lfd
        else:
            self.s_ksb = loc("s_ksb", [512, NT], BF16, False)
            self.s_kfx = loc("s_kfx", [512, NT], BF16, False)
            self.s_vsb = loc("s_vsb", [NT, 512], BF16, False)
            self.s_vfx = loc("s_vfx", [NT, 512], BF16, False)
            self.s_lf = loc("s_lf", [8, NT], F32, True)
        if hasBC:
            if False:
                self.g_ksb = g3[:, 0:512, :]
                self.g_kfx = g3[:, 512:1024, :]
                self.g_vsb = g3[:, 1024:1536, :].rearrange("r a (b c) -> r (a b) c", c=512)
                self.g_vfx = g3[:, 1536:2048, :].rearrange("r a (b c) -> r (a b) c", c=512)
                self.g_lf = self.g_lfd.rearrange("(r h) n -> r h n", r=2)
            else:
                gk = self.din if mode == "BC" else self.dscr
                self.g_ksb = gk("g_ksb", [2, 512, NT], BF16)
                self.g_kfx = gk("g_kfx", [2, 512, NT], BF16)
                self.g_vsb = gk("g_vsb", [2, NT, 512], BF16)
                self.g_vfx = gk("g_vfx", [2, NT, 512], BF16)
                self.g_lf = gk("g_lf", [2, 8, NT], F32)
            self.s_mrow = self.dscr("s_mrow", [8, NT], BF16)
            self.s_crow = self.dscr("s_crow", [3, 8, S], BF16)
            self.s_osb = self.dscr("s_osb", [512, NT], BF16)
            self.s_ofx = self.dscr("s_ofx", [512, NT], BF16)
            self.s_omem = self.dscr("s_omem", [512, NT], BF16)
        self.dbg_out = {}
        for name, shape in self.dbg.items():
            if name not in ("stop", "plvl", "gph", "split", "v1", "v2", "overlap"):
                self.dbg_out[name] = self.dout("dbg_" + name, shape)

        self.h = self.sb("h", [128, NTILE, D], F32)
        self.hb = [Buf(f"h{i}") for i in range(NTILE)]
        self.ident = self.sb("ident", [128, 128], BF16)
        self.negtri = self.sb("negtri", [128, 128], BF16)
        self.ones = self.sb("ones", [128, 128], BF16)
        self.negones = self.sb("negones", [128, 128], BF16)
        self.identf = self.sb("identf", [128, 128], F32)
        self.pregs = self.sb("pregs", [128, nl * 24], F32)
        self.cb = Buf("consts", const=True)
        self.ssq = self.sb("ssq", [128, 8], F32)
        self.lnv = self.sb("lnv", [128, 8], F32)
        self.rstd = self.sb("rstd", [128, 8], F32)
        self.b_ssq = Buf("ssq")
        self.b_lnv = Buf("lnv")
        self.b_rstd = Buf("rstd")
        self.small = self.sb("small", [128, 4 * 4], F32)
        self.r_small = Ring([self.small[:, 4 * k:4 * k + 4] for k in range(4)])
        self.b_gpost = Buf("gpost")

        self.ps2 = [self.stack.enter_context(nc.psum_tensor(f"ps{k}", [128, 1024], F32)) for k in range(4)]
        self.psb = [Buf(f"bank{k}") for k in range(8)]
        for b_ in self.psb:
            b_.excl = True
        self.bank_ctr = 0

        self.cols = self.sb("cols", [128, 4], F32)
        self.eps_col = self.cols[:, 0:1]
        self.lnhalf_col = self.cols[:, 1:2]
        self.one_col = self.cols[:, 2:3]
        P.memset("dve", self.cols[:, 0:1], EPS, writes=[self.cb])
        P.memset("dve", self.cols[:, 1:2], float(np.log(0.5)), writes=[self.cb])
        P.memset("dve", self.cols[:, 2:3], 1.0, writes=[self.cb])
        self.cdb = Buf("cdma")
        ct = P.dma_ticket(self.cdb, "sp")
        self.cdbp = Buf("cdmap")
        ctp = P.dma_ticket(self.cdbp, "pool")
        self.ct = ct
        for k, t in enumerate((self.ident, self.negtri, self.ones, self.negones)):
            P.dma("pool", t[:, :], self.cmat[k], writes=[self.cb], sembuf=self.cdbp, ticket=ctp)
        P.dma("sp", self.identf[:, :], self.cmat[0], writes=[self.cb], sembuf=self.cdb, ticket=ct)
        P.dma("sp", self.pregs[:, :], self.pregT[:, :], writes=[self.cb], sembuf=self.cdb, ticket=ct)
        self.hdb = Buf("hdma")
        ht = P.dma_ticket(self.hdb, "sp")
        for i in range(NTILE):
            P.dma("sp", self.h[:, i, :], self.x_in[i * 128:(i + 1) * 128, :], writes=[self.hb[i]],
                  sembuf=self.hdb, ticket=ht)

        if hasBC:
            self.memT = self.sb("memT", [128, KC, 256], BF16)
            self.memgs = self.sb("memgs", [128, 8], F32)
            self.bgs = self.sb("bgs", [128, nl * 24], F32)
            P.dma("sp", self.memgs[:, :], self.memgT[:, :], writes=[self.cb], sembuf=self.cdb, ticket=ct)
            P.dma("sp", self.bgs[:, :], self.bgT[:, :], writes=[self.cb], sembuf=self.cdb, ticket=ct)
        if hasA:
            self.nbfs = self.sb("nbfs", [8, nl], F32)
            P.dma("sp", self.nbfs[:, :], self.nbf[:, :], writes=[self.cb], sembuf=self.cdb, ticket=ct)
            P.ts("dve", self.nbfs[:, :], self.nbfs[:, :], -1.0, None, ALU.mult, None, reads=[self.cb], writes=[self.cb])
        if hasBC:
            self.prep_mem()

        stop = self.dbg.get("stop")
        for li in range(nl):
            if stop == "init":
                break
            if hasA:
                for sgi in range(NSG):
                    self.ffn(li, 0, sgi)
                    if stop in ("prenorm", "gateup", "ffn", "down", "post1"):
                        break
                if stop in ("prenorm", "gateup", "ffn", "down", "post1"):
                    break
                for sgi in range(NSG):
                    self.proj(li, sgi)
            if full:
                self.exchange(li)
            if hasBC:
                self.attention_pre(li)
                if full:
                    self.P.wait_all(self.cc_tickets[0])
                self.attention(li)
                for sgi in range(NSG):
                    self.merge(li, sgi)
                for sgi in range(NSG):
                    self.ffn(li, 2, sgi)

        P.barrier()
        ot = P.dma_ticket(self.hdb, "sp")
        for i in range(NTILE):
            P.dma("sp", self.h_out[i * 128:(i + 1) * 128, :], self.h[:, i, :], reads=[self.hb[i]],
                  sembuf=self.hdb, ticket=ot)
        P.barrier()
        P.op("dve", lambda e: e.memset(self.small[:, 0:1], 0.0), (), ())
        P.finalize()
        with nc.Block() as block:
            @block.tensor
            def _(e):
                P.replay("pe", e)

            @block.scalar
            def _(e):
                P.replay("act", e)

            @block.vector
            def _(e):
                P.replay("dve", e)

            @block.gpsimd
            def _(e):
                P.replay("pool", e)

            @block.sync
            def _(e):
                P.replay("sp", e)
        return nc

    def bank(self):
        k = self.bank_ctr % 8
        self.bank_ctr += 1
        return self.ps2[k // 2][:, (k % 2) * 512:(k % 2) * 512 + 512], self.psb[k]

    def bank2(self):
        if self.bank_ctr % 2:
            self.bank_ctr += 1
        k = self.bank_ctr % 8
        self.bank_ctr += 2
        return self.ps2[k // 2][:, :], [self.psb[k], self.psb[k + 1]]

    def load_gpost(self, li, w):
        self.P.dma("sp", self.gpost[:, :], self.postg[li * 3 + w:li * 3 + w + 1, :].partition_broadcast(128),
                   writes=[self.b_gpost])

    def prenorm(self, tiles, gcol, uT, uTb):
        self.prenorm_src([(self.h[:, i, :], self.hb[i]) for i in tiles], gcol, uT, uTb)

    def prenorm_src(self, srcs, gcol, uT, uTb):
        P = self.P
        n = len(srcs)
        for k, (xa, xbuf) in enumerate(srcs):
            junk, jb = self.r_junk.next()
            P.act(junk, xa, AF.Square, reads=[xbuf], writes=[jb, self.b_ssq],
                  accum_out=self.ssq[:, k:k + 1])
        P.act(self.lnv[:, 0:n], self.ssq[:, 0:n], AF.Ln, reads=[self.b_ssq, self.cb], writes=[self.b_lnv],
              scale=1.0 / D, bias=self.eps_col[:, 0:1])
        P.act(self.rstd[:, 0:n], self.lnv[:, 0:n], AF.Exp, reads=[self.b_lnv], writes=[self.b_rstd],
              scale=-0.5)
        for k, (xa, xbuf) in enumerate(srcs):
            xn, xb = self.r_xn.next()
            P.ts("dve", xn, xa, self.rstd[:, k:k + 1], None, ALU.mult, None,
                 reads=[xbuf, self.b_rstd], writes=[xb])
            bk, bb = self.bank()
            bkb = bk.bitcast(BF16)
            t = Ticket("pe")
            for c in range(KC):
                P.transpose(bkb[:, c * 128:(c + 1) * 128], xn[:, c * 128:(c + 1) * 128], self.ident[:, :],
                            reads=[xb, self.cb], writes=[bb], ticket=t, last=(c == KC - 1))
            P.tt("dve", uT[:, :, k * 128:(k + 1) * 128],
                 bkb.rearrange("p (c n) -> p c n", c=KC),
                 gcol.unsqueeze(2).to_broadcast([128, KC, 128]), ALU.mult,
                 reads=[bb, self.cb], writes=[uTb(k)])

    def postnorm(self, o2, o2b, gp, factor, i):
        P = self.P
        sm, smb = self.r_small.next()
        junk, jb = self.r_junk.next()
        P.act(junk, o2, AF.Square, reads=o2b, writes=[jb, smb], accum_out=sm[:, 0:1])
        lvl = int(self.dbg.get("plvl", 9))
        if lvl < 1:
            return
        P.act(sm[:, 1:2], sm[:, 0:1], AF.Ln, reads=[smb], writes=[smb], scale=1.0 / D, bias=self.eps_col[:, 0:1])
        P.act(sm[:, 2:3], sm[:, 1:2], AF.Exp, reads=[smb], writes=[smb], scale=-0.5,
              bias=(self.lnhalf_col[:, 0:1] if factor == 0.5 else None))
        if lvl < 2:
            return
        tw, twb = self.r_tw.next()
        if self.dbg.get("gph"):
            gp = self.h[:, i, :]
        if self.dbg.get("v1"):
            P.tt("dve", tw, o2, self.h[:, i, :], ALU.mult, reads=o2b, writes=[twb])
        elif self.dbg.get("v2"):
            P.tt("dve", tw, self.h[:, i, :], gp, ALU.mult, reads=o2b + [self.b_gpost], writes=[twb])
        elif self.dbg.get("split"):
            P.tt("dve", tw[:, 0:512], o2[:, 0:512], gp[:, 0:512], ALU.mult, reads=o2b + [self.b_gpost], writes=[twb])
            P.tt("dve", tw[:, 512:1024], o2[:, 512:1024], gp[:, 512:1024], ALU.mult, reads=o2b + [self.b_gpost], writes=[twb])
        else:
            P.tt("dve", tw, o2, gp, ALU.mult, reads=o2b + [self.b_gpost, smb], writes=[twb])
        if lvl < 3:
            return
        P.stt(self.h[:, i, :], tw, sm[:, 2:3], self.h[:, i, :], ALU.mult, ALU.add,
              reads=[twb, smb, self.hb[i]], writes=[self.hb[i]])

    def open_scope(self):
        self.P.barrier()
        return Scope(self.P)

    def norm_scratch(self, st):
        junk = self.sb_in(st, "junk", [128, D], BF16)
        self.r_junk = Ring([junk[:, 0:D]])
        xn = self.sb_in(st, "xn", [128, 2 * D], BF16)
        self.r_xn = Ring([xn[:, k * D:(k + 1) * D] for k in range(2)])
        tw = self.sb_in(st, "tw", [128, D], F32)
        self.r_tw = Ring([tw[:, 0:D]])
        self.gpost = self.sb_in(st, "gpost", [128, D], F32)
        self.b_gpost = Buf("gpost")

    def ffn(self, li, which, sgi):
        P, nc = self.P, self.nc
        Wg, Wu, Wd = (self.W_f1g, self.W_f1u, self.W_f1d) if which == 0 else (self.W_f2g, self.W_f2u, self.W_f2d)
        with self.open_scope() as st:
            self.norm_scratch(st)
            uT = self.sb_in(st, "uT", [128, KC, SG], BF16)
            actT = self.sb_in(st, "actT", [128, FC, SG], BF16)
            wd = self.sb_in(st, "wd", [128, FC, D], BF16)
            wgu = self.sb_in(st, "wgu", [128, 6, KC * 128], BF16)
            sil = self.sb_in(st, "sil", [128, 2, TG], F32)
            r_wg = Ring([wgu[:, k, :] for k in range(3)])
            r_wu = Ring([wgu[:, 3 + k, :] for k in range(3)])
            r_sil = Ring([sil[:, k, :] for k in range(2)])
            uTb = [Buf("uT0"), Buf("uT1")]
            actb = [Buf("act0"), Buf("act1")]
            wdb = Buf("wd")
            self.load_gpost(li, which)
            gcol = self.pregs[:, (li * 3 + which) * 8:(li * 3 + which) * 8 + 8]
            self.prenorm([sgi * 8 + k for k in range(8)], gcol, uT, lambda k: uTb[k // 4])
            if self.dbg.get("stop") == "prenorm":
                return
            wd_t = P.dma_ticket(wdb, "pool")
            for j in range(FC):
                wg, wgb = r_wg.next()
                wu, wub = r_wu.next()
                P.dma("pool", wg, Wg[li, j], writes=[wgb])
                P.dma("pool", wu, Wu[li, j], writes=[wub])
                P.dma("pool", wd[:, j, :], Wd[li, j], writes=[wdb], ticket=wd_t)
                for tg in range(2):
                    gk, gb = self.bank()
                    uk, ub = self.bank()
                    t = Ticket("pe")
                    for c in range(KC):
                        P.matmul(gk, wg[:, c * 128:(c + 1) * 128], uT[:, c, tg * TG:(tg + 1) * TG],
                                 c == 0, c == KC - 1, reads=[wgb, uTb[tg]], writes=[gb], ticket=t, last=(c == KC - 1))
                    t = Ticket("pe")
                    for c in range(KC):
                        P.matmul(uk, wu[:, c * 128:(c + 1) * 128], uT[:, c, tg * TG:(tg + 1) * TG],
                                 c == 0, c == KC - 1, reads=[wub, uTb[tg]], writes=[ub], ticket=t, last=(c == KC - 1))
                    s, sbf = r_sil.next()
                    P.act(s, gk, AF.Silu, reads=[gb], writes=[sbf])
                    P.tt("dve", actT[:, j, tg * TG:(tg + 1) * TG], s, uk, ALU.mult,
                         reads=[sbf, ub], writes=[actb[tg]])
            gp = self.gpost[:, :]
            if self.dbg.get("stop") == "gateup":
                return
            for k in range(8):
                o2, o2b = self.bank2()
                for hf in range(2):
                    t = Ticket("pe")
                    for j in range(FC):
                        P.matmul(o2[:, hf * 512:(hf + 1) * 512], actT[:, j, k * 128:(k + 1) * 128],
                                 wd[:, j, hf * 512:(hf + 1) * 512], j == 0, j == FC - 1,
                                 reads=[actb[k // 4], wdb], writes=[o2b[hf]], ticket=t, last=(j == FC - 1))
                if self.dbg.get("stop") == "down":
                    continue
                self.postnorm(o2, o2b, gp, 0.5, sgi * 8 + k)
                if self.dbg.get("stop") == "post1":
                    break
            P.barrier()

    def proj(self, li, sgi):
        P = self.P
        with self.open_scope() as st:
            self.norm_scratch(st)
            uT = self.sb_in(st, "uT", [128, KC, SG], BF16)
            wblk = self.sb_in(st, "wblk", [128, 3, KC * 128], BF16)
            wv = self.sb_in(st, "wv", [128, KC * 512], BF16)
            wf = self.sb_in(st, "wf", [128, KC * 8], BF16)
            stg = self.sb_in(st, "stg", [128, 4, TG], BF16)
            lft = self.sb_in(st, "lft", [8, 3, TG], F32)
            r_w = Ring([wblk[:, k, :] for k in range(3)])
            r_stg = Ring([stg[:, k, :] for k in range(4)])
            uTb = [Buf("uT0"), Buf("uT1")]
            wvb, wfb, lfb = Buf("wv"), Buf("wf"), Buf("lf")
            gcol = self.pregs[:, (li * 3 + 1) * 8:(li * 3 + 1) * 8 + 8]
            self.prenorm([sgi * 8 + k for k in range(8)], gcol, uT, lambda k: uTb[k // 4])
            n0 = sgi * SG
            dests = [(self.s_qsb, 0.125)] * 4 + [(self.s_ksb, 1.0)] * 4 + [(self.s_qfx, 0.125)] * 4 + \
                    [(self.s_kfx, 1.0)] * 4 + [(self.s_qmem, 1.0)] * 4
            for blk in range(20):
                w, wb = r_w.next()
                P.dma("pool", w, self.W_inT[li, blk], writes=[wb])
                dst, scl = dests[blk]
                r0 = (blk % 4) * 128
                for tg in range(2):
                    bk, bb = self.bank()
                    t = Ticket("pe")
                    for c in range(KC):
                        P.matmul(bk, w[:, c * 128:(c + 1) * 128], uT[:, c, tg * TG:(tg + 1) * TG],
                                 c == 0, c == KC - 1, reads=[wb, uTb[tg]], writes=[bb], ticket=t, last=(c == KC - 1))
                    sg_, sgb = r_stg.next()
                    if (blk + tg) % 2 == 0:
                        P.act(sg_, bk, AF.Copy, reads=[bb], writes=[sgb], scale=scl)
                    else:
                        P.ts("dve", sg_, bk, scl, None, ALU.mult, None, reads=[bb], writes=[sgb])
                    P.dma("sp", dst[r0:r0 + 128, n0 + tg * TG:n0 + (tg + 1) * TG], sg_, reads=[sgb])
            for vi, dst in enumerate((self.s_vsb, self.s_vfx)):
                P.dma("pool", wv, self.W_inV[li, vi], writes=[wvb])
                for k in range(8):
                    bk, bb = self.bank()
                    t = Ticket("pe")
                    for c in range(KC):
                        P.matmul(bk, uT[:, c, k * 128:(k + 1) * 128], wv[:, c * 512:(c + 1) * 512],
                                 c == 0, c == KC - 1, reads=[wvb, uTb[k // 4]], writes=[bb], ticket=t,
                                 last=(c == KC - 1))
                    sg_, sgb = r_stg.next()
                    if k % 2 == 0:
                        P.act(sg_, bk, AF.Copy, reads=[bb], writes=[sgb])
                    else:
                        P.copy("dve", sg_, bk, reads=[bb], writes=[sgb])
                    P.dma("sp", dst[n0 + k * 128:n0 + (k + 1) * 128, :], sg_, reads=[sgb])
            P.dma("pool", wf, self.W_inF[li], writes=[wfb])
            for tg in range(2):
                bk, bb = self.bank()
                t = Ticket("pe")
                for c in range(KC):
                    P.matmul(bk[0:8, :], wf[:, c * 8:(c + 1) * 8], uT[:, c, tg * TG:(tg + 1) * TG],
                             c == 0, c == KC - 1, reads=[wfb, uTb[tg]], writes=[bb], ticket=t, last=(c == KC - 1))
                P.act(lft[:, 0, :], bk[0:8, :], AF.Exp, reads=[bb, self.cb], writes=[lfb], scale=-1.0,
                      bias=self.nbfs[:, li:li + 1])
                P.act(lft[:, 1, :], lft[:, 0, :], AF.Ln, reads=[lfb], writes=[lfb], bias=self.one_col[0:8, 0:1])
                P.ts("dve", lft[:, 2, :], lft[:, 1, :], -1.0, None, ALU.mult, None, reads=[lfb], writes=[lfb])
                P.dma("sp", self.s_lf[:, n0 + tg * TG:n0 + (tg + 1) * TG], lft[:, 2, :], reads=[lfb])
            P.barrier()

    def prep_mem(self):
        P = self.P
        with self.open_scope() as st:
            self.norm_scratch(st)
            mt = self.sb_in(st, "memtile", [128, 2, D], F32)
            mb = [Buf("m0"), Buf("m1")]
            for k in range(2):
                P.dma("sp", mt[:, k, :], self.mem_in[k * 128:(k + 1) * 128, :], writes=[mb[k]])
            ub = Buf("memT")
            self.memTb = Buf("memTc", const=True)
            self.prenorm_src([(mt[:, k, :], mb[k]) for k in range(2)], self.memgs[:, 0:8], self.memT, lambda k: ub)
            P.barrier()

    def mem_kv(self, li, KmT, Vm, kvb):
        P = self.P
        with self.open_scope() as st:
            wmk = self.sb_in(st, "wmk", [128, 2, KC * 128], BF16)
            wmv = self.sb_in(st, "wmv", [128, KC * 512], BF16)
            r_w = Ring([wmk[:, k, :] for k in range(2)])
            wvb = Buf("wmv")
            for hm in range(4):
                w, wb = r_w.next()
                P.dma("pool", w, self.W_mk[li, hm], writes=[wb])
                bk, bb = self.bank()
                t = Ticket("pe")
                for c in range(KC):
                    P.matmul(bk[:, 0:256], w[:, c * 128:(c + 1) * 128], self.memT[:, c, :], c == 0, c == KC - 1,
                             reads=[wb], writes=[bb], ticket=t, last=(c == KC - 1))
                P.copy("dve", KmT[:, hm, :], bk[:, 0:256], reads=[bb], writes=[kvb])
            P.dma("pool", wmv, self.W_mv[li], writes=[wvb])
            for blk in range(2):
                bk, bb = self.bank()
                t = Ticket("pe")
                for c in range(KC):
                    P.matmul(bk, self.memT[:, c, blk * 128:(blk + 1) * 128], wmv[:, c * 512:(c + 1) * 512],
                             c == 0, c == KC - 1, reads=[wvb], writes=[bb], ticket=t, last=(c == KC - 1))
                P.copy("dve", Vm[:, blk, :], bk, reads=[bb], writes=[kvb])
            P.barrier()

    def attention_pre(self, li):
        P, nc = self.P, self.nc
        with self.open_scope() as st0:
            KmT = self.sb_in(st0, "KmT", [128, 4, 256], BF16)
            Vm = self.sb_in(st0, "Vm", [128, 2, 512], BF16)
            kvb = Buf("memkv")
            self.mem_kv(li, KmT, Vm, kvb)
            with self.open_scope() as st:
                pT = self.sb_in(st, "pTm", [128, 3, TG], BF16)
                recT = self.sb_in(st, "recTm", [128, 2, TG], F32)
                osT = self.sb_in(st, "osTm", [128, 4, TG], BF16)
                r_p = Ring([pT[:, k, :] for k in range(3)])
                r_rec = Ring([recT[:, k, :] for k in range(2)])
                r_os = Ring([osT[:, k, :] for k in range(4)])
                qm = self.sb_in(st, "qm", [128, 2, NT], BF16)
                qmb = [Buf("qm0"), Buf("qm1")]
                scale_m = 128.0 ** -0.5
                for hm in range(4):
                    k = hm % 2
                    P.dma("sp", qm[:, k, :], self.s_qmem[hm * 128:(hm + 1) * 128, :], writes=[qmb[k]])
                    for g in range(4):
                        ps = []
                        for blk in range(2):
                            bk, bb = self.bank()
                            P.matmul(bk, KmT[:, hm, blk * 128:(blk + 1) * 128], qm[:, k, g * TG:(g + 1) * TG], True, True,
                                     reads=[kvb, qmb[k]], writes=[bb])
                            p, pb = r_p.next()
                            P.act(p, bk, AF.Exp, reads=[bb], writes=[pb], scale=scale_m)
                            ps.append((p, pb))
                        ok, okb = self.bank()
                        dk, dkb = self.bank()
                        t = Ticket("pe")
                        for blk in range(2):
                            P.matmul(ok, Vm[:, blk, hm * 128:(hm + 1) * 128], ps[blk][0], blk == 0, blk == 1,
                                     reads=[kvb, ps[blk][1]], writes=[okb], ticket=t, last=(blk == 1))
                        t = Ticket("pe")
                        for blk in range(2):
                            P.matmul(dk, self.ones[:, :], ps[blk][0], blk == 0, blk == 1,
                                     reads=[self.cb, ps[blk][1]], writes=[dkb], ticket=t, last=(blk == 1))
                        rec, recb = r_rec.next()
                        P.op("dve", lambda e, rec=rec, dk=dk: e.reciprocal(rec, dk), reads=[dkb], writes=[recb])
                        os_, osb_ = r_os.next()
                        P.tt("dve", os_, ok, rec, ALU.mult, reads=[okb, recb], writes=[osb_])
                        P.dma("sp", self.s_omem[hm * 128:(hm + 1) * 128, g * TG:(g + 1) * TG], os_, reads=[osb_])
                P.barrier()


    def attention(self, li):
        P, nc = self.P, self.nc
        with self.open_scope() as st0:
            with self.open_scope() as st:
                lfg = self.sb_in(st, "lfg", [8, S], F32)
                cT = self.sb_in(st, "cT", [8, S], F32)
                lfl = self.sb_in(st, "lfl", [8, NT], F32)
                cl = self.sb_in(st, "cl", [8, NT], F32)
                mrow = self.sb_in(st, "mrow", [8, NT], BF16)
                tot = self.sb_in(st, "tot", [8, 64], F32)
                sel = self.sb_in(st, "sel", [8, 32], F32)
                b1, b2, b3 = Buf("lfg"), Buf("lfl"), Buf("misc")
                t = P.dma_ticket(b1, "sp")
                for c in range(8):
                    r, gl = CHUNK_OWNER[c]
                    P.dma("sp", lfg[:, c * 512:(c + 1) * 512], self.g_lf[r, :, gl * 512:(gl + 1) * 512],
                          writes=[b1], ticket=t)
                P.dma("sp", lfl[:, :], self.s_lf[:, :], writes=[b2])
                P.dma("sp", sel[:, :], self.selc[:, :], writes=[b3])
                onesb = self.one_col[0:8, 0:1].to_broadcast([8, S])
                cTb = Buf("cT")
                P.op("dve", lambda e: e.tensor_tensor_scan(cT[:, :], onesb, lfg[:, :], 0.0, ALU.mult, ALU.add),
                     reads=[b1, self.cb], writes=[cTb])
                P.op("dve", lambda e: e.tensor_reduce(tot[:, 0:8], lfg[:, :].rearrange("p (c n) -> p c n", c=8),
                                                      mybir.AxisListType.X, ALU.add),
                     reads=[b1], writes=[b3])
                for g in range(4):
                    P.tt("dve", tot[:, 16 + g * 8:24 + g * 8], tot[:, 0:8], sel[:, g * 8:(g + 1) * 8], ALU.mult,
                         reads=[b3], writes=[b3])
                    P.op("dve", lambda e, g=g: e.tensor_reduce(tot[:, 8 + g:9 + g], tot[:, 16 + g * 8:24 + g * 8],
                                                               mybir.AxisListType.X, ALU.add),
                         reads=[b3], writes=[b3])
                clb = Buf("cl")
                for g in range(4):
                    ob = self.one_col[0:8, 0:1].to_broadcast([8, 512])
                    P.op("dve", lambda e, g=g, ob=ob: e.tensor_tensor_scan(
                        cl[:, g * 512:(g + 1) * 512], ob, lfl[:, g * 512:(g + 1) * 512],
                        tot[:, 8 + g:9 + g], ALU.mult, ALU.add), reads=[b2, b3, self.cb], writes=[clb])
                P.copy("dve", mrow[:, :], cl[:, :], reads=[clb], writes=[clb])
                mrow_t = P.dma("sp", self.s_mrow[:, :], mrow[:, :], reads=[clb])
                cs3 = self.sb_in(st, "cs3", [8, 3, S], BF16)
                csr = self.sb_in(st, "csr", [8, S], F32)
                csb = Buf("cs3")
                P.ts("dve", csr[:, :], cT[:, :], -1.0, None, ALU.mult, None, reads=[cTb], writes=[csb])
                P.copy("dve", cs3[:, 0, :], csr[:, :], reads=[csb], writes=[csb])
                P.tt("dve", csr[:, :], csr[:, :], cs3[:, 0, :], ALU.subtract, reads=[csb], writes=[csb])
                P.copy("dve", cs3[:, 1, :], csr[:, :], reads=[csb], writes=[csb])
                P.tt("dve", csr[:, :], csr[:, :], cs3[:, 1, :], ALU.subtract, reads=[csb], writes=[csb])
                P.copy("dve", cs3[:, 2, :], csr[:, :], reads=[csb], writes=[csb])
                crow_t = P.dma("sp", self.s_crow.rearrange("j h n -> h j n"), cs3[:, :, :], reads=[csb])
                if "cT" in self.dbg_out:
                    P.dma("sp", self.dbg_out["cT"][:, :], cT[:, :], reads=[cTb])
                if "cl" in self.dbg_out:
                    P.dma("sp", self.dbg_out["cl"][:, :], cl[:, :], reads=[clb])
                P.barrier()
            with self.open_scope() as st:
                mask = self.sb_in(st, "mask", [128, 32, 514], BF16)
                maskb = Buf("mask", const=True)
                P.dma("sp", mask[:, :, :], self.cmask.rearrange("p (u j) -> p u j", u=32), writes=[maskb])
                KT = [[self.sb_in(st, f"KT{s}{k}", [128, S], BF16) for k in range(2)] for s in range(2)]
                VA = [[self.sb_in(st, f"VA{s}{k}", [128, 32, 128], BF16) for k in range(2)] for s in range(2)]
                QT = [[self.sb_in(st, f"QT{s}{k}", [128, NT], BF16) for k in range(2)] for s in range(2)]
                KTb = [[Buf("KT") for k in range(2)] for s in range(2)]
                VAb = [[Buf("VA") for k in range(2)] for s in range(2)]
                QTb = [[Buf("QT") for k in range(2)] for s in range(2)]
                eT = self.sb_in(st, "eT", [128, 2, TG], F32)
                spT = self.sb_in(st, "spT", [128, 2, TG], BF16)
                wT = self.sb_in(st, "wT", [128, 3, TG], BF16)
                cbT = self.sb_in(st, "cbT", [128, 3, TG], BF16)
                pT = self.sb_in(st, "pT", [128, 3, TG], BF16)
                recT = self.sb_in(st, "recT", [128, 1, TG], F32)
                osT = self.sb_in(st, "osT", [128, 4, TG], BF16)
                r_e = Ring([eT[:, k, :] for k in range(2)])
                r_sp = Ring([spT[:, k, :] for k in range(2)])
                r_w = Ring([wT[:, k, :] for k in range(3)])
                r_cb = Ring([cbT[:, k, :] for k in range(3)])
                r_p = Ring([pT[:, k, :] for k in range(3)])
                r_rec = Ring([recT[:, k, :] for k in range(1)])
                r_os = Ring([osT[:, k, :] for k in range(4)])
                cst = Buf("attnconst")
                for s in range(2):
                    for k in range(2):
                        P.memset("pool", KT[s][k][64:128, :], 0.0, writes=[KTb[s][k]])
                        P.memset("pool", QT[s][k][64:128, :], 1.0 if s == 1 else 0.0, writes=[QTb[s][k]])
                        if s == 1:
                            P.memset("pool", KT[s][k][64:65, :], 1.0, writes=[KTb[s][k]])
                            P.memset("pool", VA[s][k][:, :, 64:128], 1.0, writes=[VAb[s][k]])

                def bankk(k):
                    return self.ps2[k // 2][:, (k % 2) * 512:(k % 2) * 512 + 512], self.psb[k]
                xs_ring = [bankk(0), bankk(1), bankk(2)]
                ots_ring = [bankk(3), bankk(4)]
                xf_ring = [bankk(5), bankk(6)]
                otf, otfb = bankk(7)
                self.bank_ctr = 0

                ccw = list(getattr(self, "cc_tickets", [])) if self.mode == "FULL" else []

                def load_head(h):
                    k = h % 2
                    for s, (gk, gv, sq) in enumerate(((self.g_ksb, self.g_vsb, self.s_qsb),
                                                      (self.g_kfx, self.g_vfx, self.s_qfx))):
                        tk = P.dma_ticket(KTb[s][k], "sp")
                        for c in range(8):
                            r, gl = CHUNK_OWNER[c]
                            P.dma("sp", KT[s][k][0:64, c * 512:(c + 1) * 512],
                                  gk[r, h * 64:(h + 1) * 64, gl * 512:(gl + 1) * 512], writes=[KTb[s][k]], ticket=tk,
                                  waits=ccw)
                        if s == 1:
                            for j in range(3):
                                P.dma("sp", KT[s][k][65 + j:66 + j, :], self.s_crow[j, h:h + 1, :], writes=[KTb[s][k]],
                                      ticket=tk, waits=[crow_t])
                        if s == 1 or h % 2 == 0:
                            kv = k if s == 1 else (h // 2) % 2
                            tv = P.dma_ticket(VAb[s][kv], "sp")
                            ncol = 64 if s == 1 else 128
                            c0 = h * 64
                            for c in range(8):
                                r, gl = CHUNK_OWNER[c]
                                P.dma("sp", VA[s][kv][:, c * 4:(c + 1) * 4, 0:ncol],
                                      gv[r, gl * 512:(gl + 1) * 512, c0:c0 + ncol].rearrange("(b p) d -> p b d", p=128),
                                      writes=[VAb[s][kv]], ticket=tv, waits=ccw)
                        tq = P.dma_ticket(QTb[s][k], "sp")
                        P.dma("sp", QT[s][k][0:64, :], sq[h * 64:(h + 1) * 64, :], writes=[QTb[s][k]], ticket=tq)
                        if s == 1:
                            P.dma("sp", QT[s][k][64:65, :], self.s_mrow[h:h + 1, :], writes=[QTb[s][k]], ticket=tq,
                                  waits=[mrow_t])

                def sbA(u):
                    h, g, kb, first, lastu = u[:5]
                    k = h % 2
                    x, xb = xs_ring[u[5] % 3]
                    masked = kb >= 8 * g
                    t = Ticket("pe")
                    P.matmul(x, KT[0][k][:, kb * 128:(kb + 1) * 128], QT[0][k][:, g * TG:(g + 1) * TG],
                             True, True, reads=[KTb[0][k], QTb[0][k]], writes=[xb], ticket=t, last=not masked)
                    if masked:
                        P.op("pe", lambda e: e.matmul(x, self.ident[:, :], mask[:, kb, 0:512], start=False, stop=True,
                                                      skip_group_check=True),
                             reads=[maskb, self.cb], writes=[xb], ticket=t, last=True)
                    e_, eb = r_e.next()
                    P.act(e_, x, AF.Exp, reads=[xb], writes=[eb])
                    sp, spb = r_sp.next()
                    P.act(sp, e_, AF.Ln, reads=[eb], writes=[spb], bias=1.0)
                    u[6]["x"] = (x, xb)
                    u[6]["sp"] = (sp, spb)
                    if not lastu:
                        rn, rnb = r_cb.next()
                        if first:
                            P.copy("pool", rn, sp, reads=[spb], writes=[rnb])
                        else:
                            rp, rpb = u[6]["rprev"]
                            P.tt("pool", rn, rp, sp, ALU.add, reads=[rpb, spb], writes=[rnb])
                        u[6]["R"] = (rn, rnb)

                def sbB(u):
                    h, g, kb, first, lastu = u[:5]
                    x, xb = u[6]["x"]
                    sp, spb = u[6]["sp"]
                    t = Ticket("pe")
                    P.op("pe", lambda e: e.matmul(x, self.negtri[:, :], sp, start=False, stop=True, skip_group_check=True),
                         reads=[spb, self.cb], writes=[xb], ticket=t, last=first)
                    if not first:
                        rp, rpb = u[6]["rprev"]
                        P.op("pe", lambda e: e.matmul(x, self.negones[:, :], rp, start=False, stop=True,
                                                      skip_group_check=True),
                             reads=[rpb, self.cb], writes=[xb], ticket=t, last=True)
                    w, wb = r_w.next()
                    P.act(w, x, AF.Exp, reads=[xb], writes=[wb])
                    u[6]["w"] = (w, wb)

                def sbC(u):
                    h, g, kb, first, lastu = u[:5]
                    kv = (h // 2) % 2
                    r0 = (h % 2) * 64
                    w, wb = u[6]["w"]
                    ots, otsb = ots_ring[(h * 4 + g) % 2]
                    P.op("pe", lambda e: e.matmul(ots, VA[0][kv][:, kb, :], w, start=first, stop=lastu,
                                                  skip_group_check=True),
                         reads=[wb, VAb[0][kv]], writes=[otsb])
                    if lastu:
                        os_, osb_ = r_os.next()
                        P.copy("dve", os_[r0:r0 + 64, :], ots[r0:r0 + 64, :], reads=[otsb], writes=[osb_])
                        P.dma("sp", self.s_osb[h * 64:(h + 1) * 64, g * TG:(g + 1) * TG], os_[r0:r0 + 64, :],
                              reads=[osb_])

                def fxA(u):
                    h, g, kb, first, lastu = u[:5]
                    k = h % 2
                    x, xb = xf_ring[u[5] % 2]
                    masked = kb >= 8 * g
                    t = Ticket("pe")
                    P.matmul(x, KT[1][k][:, kb * 128:(kb + 1) * 128], QT[1][k][:, g * TG:(g + 1) * TG],
                             True, True, reads=[KTb[1][k], QTb[1][k]], writes=[xb], ticket=t, last=not masked)
                    if masked:
                        P.op("pe", lambda e: e.matmul(x, self.ident[:, :], mask[:, kb, 1:513], start=False, stop=True,
                                                      skip_group_check=True),
                             reads=[maskb, self.cb], writes=[xb], ticket=t, last=True)
                    p, pb = r_p.next()
                    P.act(p, x, AF.Exp, reads=[xb], writes=[pb])
                    u[6]["p"] = (p, pb)

                def fxC(u):
                    h, g, kb, first, lastu = u[:5]
                    k = h % 2
                    p, pb = u[6]["p"]
                    P.op("pe", lambda e: e.matmul(otf, VA[1][k][:, kb, :], p, start=first, stop=lastu,
                                                  skip_group_check=True),
                         reads=[pb, VAb[1][k]], writes=[otfb])
                    if lastu:
                        r0 = 0
                        d0 = 64
                        rec, recb = r_rec.next()
                        P.op("dve", lambda e: e.reciprocal(rec[d0:d0 + 64, :], otf[d0:d0 + 64, :]),
                             reads=[otfb], writes=[recb])
                        os_, osb_ = r_os.next()
                        P.tt("dve", os_[r0:r0 + 64, :], otf[r0:r0 + 64, :], rec[d0:d0 + 64, :], ALU.mult,
                             reads=[otfb, recb], writes=[osb_])
                        P.dma("sp", self.s_ofx[h * 64:(h + 1) * 64, g * TG:(g + 1) * TG], os_[r0:r0 + 64, :],
                              reads=[osb_])

                units = []
                for h in range(8):
                    for g in range(4):
                        n = 8 * (g + 1)
                        for kb in range(n - 1, -1, -1):
                            units.append([h, g, kb, kb == n - 1, kb == 0, len(units), {}])
                units_f = [[u[0], u[1], u[2], u[3], u[4], u[5], {}] for u in units]
                load_head(0)
                N = len(units)
                for i in range(N + 2):
                    if i < N:
                        u = units[i]
                        if u[1] == 0 and u[2] == 4 and u[0] + 1 < 8:
                            load_head(u[0] + 1)
                        if not u[3]:
                            u[6]["rprev"] = units[i - 1][6]["R"]
                        sbA(units[i])
                        fxA(units_f[i])
                    if 1 <= i <= N:
                        ub_ = units[i - 1]
                        sbB(ub_)
                        fxC(units_f[i - 1])
                    if 2 <= i <= N + 1:
                        sbC(units[i - 2])

                P.barrier()

    def merge(self, li, sgi):
        P = self.P
        with self.open_scope() as st:
            self.norm_scratch(st)
            uT = self.sb_in(st, "uT", [128, KC, SG], BF16)
            oT = [self.sb_in(st, f"oT{b}", [128, 4, SG], BF16) for b in range(3)]
            mT = self.sb_in(st, "mT", [128, KC, SG], BF16)
            wgt = self.sb_in(st, "wgt", [128, 6, KC * 128], BF16)
            wbr = self.sb_in(st, "wbr", [128, 6, 4 * 128], BF16)
            wo = self.sb_in(st, "wo", [128, KC * D], BF16)
            sg = self.sb_in(st, "sg", [128, 3, TG], F32)
            mm = self.sb_in(st, "mm", [128, 4, TG], F32)
            r_wg = Ring([wgt[:, k, :] for k in range(6)])
            r_wb = Ring([wbr[:, k, :] for k in range(6)])
            r_sg = Ring([sg[:, k, :] for k in range(3)])
            r_mm = Ring([mm[:, k, :] for k in range(4)])
            uTb = [Buf("uT0"), Buf("uT1")]
            oTb = [Buf("oT") for _ in range(3)]
            mTb = [Buf("mT0"), Buf("mT1")]
            wob = Buf("wo")
            n0 = sgi * SG
            self.load_gpost(li, 1)
            for b, src_ in enumerate((self.s_osb, self.s_ofx, self.s_omem)):
                t = P.dma_ticket(oTb[b], "sp")
                for c in range(4):
                    P.dma("sp", oT[b][:, c, :], src_[c * 128:(c + 1) * 128, n0:n0 + SG], writes=[oTb[b]], ticket=t)
            P.dma("pool", wo, self.W_out[li], writes=[wob])
            gcol = self.pregs[:, (li * 3 + 1) * 8:(li * 3 + 1) * 8 + 8]
            self.prenorm([sgi * 8 + k for k in range(8)], gcol, uT, lambda k: uTb[k // 4])
            for fc in range(8):
                ws = []
                for b in range(3):
                    wg_, wgb = r_wg.next()
                    wb_, wbb = r_wb.next()
                    P.dma("pool", wg_, self.W_gate[li, b * 8 + fc], writes=[wgb])
                    P.dma("pool", wb_, self.W_br[li, b * 8 + fc], writes=[wbb])
                    ws.append((wg_, wgb, wb_, wbb))
                for tg in range(2):
                    ms = []
                    for b in range(3):
                        wg_, wgb, wb_, wbb = ws[b]
                        gk, gb = self.bank()
                        t = Ticket("pe")
                        for c in range(KC):
                            P.matmul(gk, wg_[:, c * 128:(c + 1) * 128], uT[:, c, tg * TG:(tg + 1) * TG],
                                     c == 0, c == KC - 1, reads=[wgb, uTb[tg]], writes=[gb], ticket=t,
                                     last=(c == KC - 1))
                        s_, sb_ = r_sg.next()
                        col = li * 24 + b * 8 + fc
                        P.act(s_, gk, AF.Sigmoid, reads=[gb, self.cb], writes=[sb_], bias=self.bgs[:, col:col + 1])
                        bk, bb = self.bank()
                        t = Ticket("pe")
                        for c in range(4):
                            P.matmul(bk, wb_[:, c * 128:(c + 1) * 128], oT[b][:, c, tg * TG:(tg + 1) * TG],
                                     c == 0, c == 3, reads=[wbb, oTb[b]], writes=[bb], ticket=t, last=(c == 3))
                        m_, mb_ = r_mm.next()
                        P.tt("dve", m_, s_, bk, ALU.mult, reads=[sb_, bb], writes=[mb_])
                        ms.append((m_, mb_))
                    P.tt("pool", ms[0][0], ms[0][0], ms[1][0], ALU.add, reads=[ms[0][1], ms[1][1]], writes=[ms[0][1]])
                    P.tt("pool", mT[:, fc, tg * TG:(tg + 1) * TG], ms[0][0], ms[2][0], ALU.add,
                         reads=[ms[0][1], ms[2][1]], writes=[mTb[tg]])
            gp = self.gpost[:, :]
            for k in range(8):
                o2, o2b = self.bank2()
                for hf in range(2):
                    t = Ticket("pe")
                    for c in range(KC):
                        P.matmul(o2[:, hf * 512:(hf + 1) * 512], mT[:, c, k * 128:(k + 1) * 128],
                                 wo[:, c * D + hf * 512:c * D + (hf + 1) * 512], c == 0, c == KC - 1,
                                 reads=[mTb[k // 4], wob], writes=[o2b[hf]], ticket=t, last=(c == KC - 1))
                self.postnorm(o2, o2b, gp, 1.0, sgi * 8 + k)
            P.barrier()

    def exchange(self, li):
        P = self.P
        P.barrier()
        groups = [[2 * i, 2 * i + 1] for i in range(self.ncores // 2)]
        self.cc_tickets = []
        for loc_, gat in ((self.s_lf, self.g_lf), (self.s_ksb, self.g_ksb), (self.s_kfx, self.g_kfx),
                          (self.s_vsb, self.g_vsb), (self.s_vfx, self.g_vfx)):
            self.cc_tickets.append(P.collective(loc_[:, :], gat.rearrange("r a b -> (r a) b"), groups))
        self.cc_ticket = self.cc_tickets[-1]
        if not self.dbg.get("overlap", True):
            P.wait_all(self.cc_ticket)
            P.barrier()


_BF = ml_dtypes.bfloat16
_CACHE = {}


def _prog(mode, nl):
    key = (mode, nl)
    if key not in _CACHE:
        b = Builder(mode, list(range(nl)))
        _CACHE[key] = b.build()
    return _CACHE[key]


def _pkn(w, n):
    lead = w.shape[:-2]
    k = w.shape[-2] // 128
    w = w.reshape(lead + (k, 128, n))
    w = np.swapaxes(w, -3, -2)
    return np.ascontiguousarray(w.reshape(lead + (128, k * n)))


def _blocks(w, starts, width):
    return np.stack([_pkn(w[:, :, s:s + width], width) for s in starts], axis=1)


def _consts():
    cm = np.zeros((4, 128, 128), np.float32)
    cm[3] = -1.0
    cm[0] = np.eye(128, dtype=np.float32)
    j = np.arange(128)[:, None]
    s = np.arange(128)[None, :]
    cm[1] = np.where(j >= s, -1.0, 0.0)
    cm[2] = 1.0
    masks, sels = [], []
    for r in range(2):
        m = np.zeros((128, 32, 514), np.float32)
        p = np.arange(128)[:, None]
        jj = np.arange(514)[None, :]
        for kb in range(32):
            g = kb // 8
            c = RANK_CHUNKS[r][g]
            qpos = 512 * c + jj - 1
            kpos = 128 * kb + p
            m[:, kb, :] = np.where(kpos > qpos, NEG, 0.0)
        masks.append(m.reshape(128, 32 * 514).astype(_BF))
        sel = np.zeros((8, 4, 8), np.float32)
        for g in range(4):
            sel[:, g, :RANK_CHUNKS[r][g]] = 1.0
        sels.append(sel.reshape(8, 32))
    return cm, masks, sels


def _layout(inp):
    f = lambda k: np.asarray(inp[k], np.float32)
    W = {}
    for tag, pre in (("f1", "ffn1"), ("f2", "ffn2")):
        W[tag + "g"] = _blocks(f(pre + "_w_gate"), [j * 128 for j in range(FC)], 128)
        W[tag + "u"] = _blocks(f(pre + "_w_up"), [j * 128 for j in range(FC)], 128)
        W[tag + "d"] = np.ascontiguousarray(f(pre + "_w_down").reshape(L, FC, 128, D))
    w_in = f("w_in")
    st = [0, 128, 256, 384, 512, 640, 768, 896, 1536, 1664, 1792, 1920, 2048, 2176, 2304, 2432,
          3080, 3208, 3336, 3464]
    W["winT"] = _blocks(w_in, st, 128)
    W["winV"] = _blocks(w_in, [1024, 2560], 512)
    W["winF"] = _pkn(w_in[:, :, 3072:3080], 8)
    W["nbf"] = np.ascontiguousarray(np.transpose(f("b_forget"), (1, 0)))
    wg = f("w_gate")
    W["wgate"] = _blocks(wg, [b * 1024 + fc * 128 for b in range(3) for fc in range(8)], 128)
    W["bgT"] = np.ascontiguousarray(f("b_gate").reshape(L, 24, 128).transpose(2, 0, 1).reshape(128, L * 24))
    br = [f("w_br_sb"), f("w_br_fox"), f("w_br_mem")]
    W["wbr"] = np.stack([_pkn(br[b][:, :, fc * 128:(fc + 1) * 128], 128) for b in range(3) for fc in range(8)], axis=1)
    W["wout"] = _pkn(f("w_out"), D)
    wm = f("w_mem_kv")
    W["wmk"] = _blocks(wm, [0, 128, 256, 384], 128)
    W["wmv"] = _pkn(wm[:, :, 512:1024], 512)
    pre = np.stack([f("ffn1_pre_g"), f("mix_pre_g"), f("ffn2_pre_g")], axis=1)
    W["pregT"] = np.ascontiguousarray(pre.reshape(L, 3, 8, 128).transpose(3, 0, 1, 2).reshape(128, L * 24))
    W["postg"] = np.ascontiguousarray(
        np.stack([f("ffn1_post_g"), f("mix_post_g"), f("ffn2_post_g")], axis=1).reshape(L * 3, D))
    W["memgT"] = np.ascontiguousarray(f("mem_norm_g").reshape(8, 128).T)
    return W


_PER_LAYER = {"f1g", "f1u", "f1d", "f2g", "f2u", "f2d", "winT", "winV", "winF", "wgate", "wbr", "wout", "wmk", "wmv"}


def _layer_slice(W, name, l):
    a = W[name]
    if name in _PER_LAYER:
        return a[l:l + 1]
    if name == "nbf":
        return np.ascontiguousarray(a[:, l:l + 1])
    if name in ("bgT", "pregT"):
        return np.ascontiguousarray(a[:, l * 24:(l + 1) * 24])
    if name == "postg":
        return a[l * 3:(l + 1) * 3]
    return a


A_NAMES = ["pregT", "postg", "f1g", "f1u", "f1d", "winT", "winV", "winF", "nbf"]
BC_NAMES = ["pregT", "postg", "f2g", "f2u", "f2d", "wgate", "bgT", "wbr", "wout", "wmk", "wmv", "memgT"]
LOC = ["s_qsb", "s_qfx", "s_qmem", "s_ksb", "s_kfx", "s_vsb", "s_vfx", "s_lf"]


def kernel(**inputs):
    x = np.asarray(inputs["x"], np.float32)
    mem = np.asarray(inputs["mem"], np.float32)
    W = _layout(inputs)
    cm, masks, sels = _consts()
    ncore = 8
    prog = _prog("FULL", L)
    maps = []
    for c in range(ncore):
        b, r = c // 2, c % 2
        m = {"x_in": np.ascontiguousarray(np.concatenate([x[b, ch * 512:(ch + 1) * 512] for ch in RANK_CHUNKS[r]], 0)),
             "cmat": cm, "mem": np.ascontiguousarray(mem[b]), "cmask": masks[r], "selc": sels[r]}
        for n in set(A_NAMES + BC_NAMES):
            m[n] = W[n]
        maps.append(m)
    res = run_bass_kernel_spmd(prog, maps, core_ids=list(range(ncore))).results
    out = np.zeros((4, S, D), np.float32)
    for c in range(ncore):
        b, r = c // 2, c % 2
        hc = np.asarray(res[c]["h_out"])
        for g, ch in enumerate(RANK_CHUNKS[r]):
            out[b, ch * 512:(ch + 1) * 512] = hc[g * 512:(g + 1) * 512]
    return out
```

```python
import numpy as np
import ml_dtypes
from contextlib import ExitStack

import concourse.bass as bass
import concourse.mybir as mybir
from concourse.bass_utils import run_bass_kernel_spmd

F32 = mybir.dt.float32
BF16 = mybir.dt.bfloat16
AF = mybir.ActivationFunctionType
ALU = mybir.AluOpType

L = 4
D = 1024
KC = 8
FF = 2816
FC = 22
S = 4096
NT = 2048
NTILE = 16
TG = 512
SG = 1024
NSG = NT // SG
IN_W = 3592
EPS = 1e-6
NEG = -30000.0
RANK_CHUNKS = [[0, 3, 4, 7], [1, 2, 5, 6]]
CHUNK_OWNER = {}
for _r in range(2):
    for _g, _c in enumerate(RANK_CHUNKS[_r]):
        CHUNK_OWNER[_c] = (_r, _g)

ENGS = ("pe", "act", "dve", "pool", "sp")


def _ap(x):
    if isinstance(x, bass.AP):
        return x
    return x[tuple(slice(None) for _ in x.shape)]


class Ticket:
    __slots__ = ("eng", "sem", "val", "pos")

    def __init__(self, eng):
        self.eng = eng
        self.sem = None
        self.val = None
        self.pos = None


class Buf:
    __slots__ = ("name", "w", "r", "const", "dsem", "dcount", "dq", "excl")

    def __init__(self, name, const=False):
        self.name = name
        self.w = []
        self.r = []
        self.const = const
        self.dsem = None
        self.dcount = 0
        self.dq = None
        self.excl = False


class Item:
    __slots__ = ("fn", "deps", "sig", "dma_t", "inc1")

    def __init__(self, fn, deps, sig, dma_t=None):
        self.fn = fn
        self.deps = deps
        self.sig = sig
        self.dma_t = dma_t
        self.inc1 = False


class Prog:
    def __init__(self, nc, stack):
        self.nc = nc
        self.stack = stack
        self.items = {e: [] for e in ENGS}
        self.nsem = 0
        self.pre = {}
        self.pending_dma = []
        self.sem_free = {"sp": [], "pool": [], "act": []}
        self.scopes = [[]]

    def new_sem(self, name):
        self.nsem += 1
        return self.stack.enter_context(self.nc.semaphore(f"{name}_{self.nsem}"))

    def barrier(self):
        ts = []
        for eng in ENGS:
            if eng == "sp" or not self.items[eng]:
                continue
            for it in reversed(self.items[eng]):
                if it.dma_t is not None:
                    continue
                assert it.sig is not None, eng
                ts.append(it.sig)
                break
        ts += self.pending_dma
        self.pending_dma = []
        for eng in ENGS:
            self.pre[eng] = self.pre.get(eng, []) + ts

    def push_scope(self):
        self.scopes.append([])

    def pop_scope(self):
        for b in self.scopes.pop():
            if b.dsem is not None and b.dcount < 24000:
                self.sem_free[b.dq].append((b.dsem, b.dcount))
            b.dsem = None
            b.dq = None

    def _deps(self, eng, t, reads, writes, waits):
        deps = list(self.pre.pop(eng, []))
        for b in reads:
            deps += b.w
            if b.excl:
                deps += [d for d in b.r if d.eng != eng]
        for b in writes:
            deps += b.w
            deps += b.r
        deps += list(waits)
        out = []
        seen = set()
        for d in deps:
            if d is t or id(d) in seen:
                continue
            if eng == "pe" and d.eng == "pe":
                continue
            seen.add(id(d))
            out.append(d)
        return out

    def _book(self, t, reads, writes):
        for b in reads:
            if b.const:
                continue
            if not b.r or b.r[-1] is not t:
                b.r.append(t)
        for b in writes:
            if b.const:
                if not b.w or b.w[-1] is not t:
                    b.w.append(t)
            else:
                b.w = [t]
                b.r = []

    def op(self, eng, fn, reads=(), writes=(), ticket=None, last=True, waits=()):
        t = ticket if ticket is not None else Ticket(eng)
        deps = self._deps(eng, t, reads, writes, waits)
        self.items[eng].append(Item(fn, deps, t if last else None))
        self._book(t, reads, writes)
        return t

    def dma_ticket(self, buf, q):
        t = Ticket("dma")
        assert buf.dq in (None, q), (buf.name, buf.dq, q)
        buf.dq = q
        if buf.dsem is None:
            if self.sem_free[q]:
                buf.dsem, buf.dcount = self.sem_free[q].pop()
            else:
                buf.dsem = self.new_sem("d")
                buf.dcount = 0
            self.scopes[-1].append(buf)
        t.sem = buf.dsem
        t.val = buf.dcount
        t.pos = buf
        return t

    def dma(self, q, out_ap, in_ap, reads=(), writes=(), sembuf=None, ticket=None, waits=()):
        sb = sembuf if sembuf is not None else (writes[0] if writes else reads[0])
        t = ticket if ticket is not None else self.dma_ticket(sb, q)
        assert t.pos is sb and sb.dq == q
        sb.dcount += 16
        t.val = sb.dcount
        deps = self._deps(q, t, reads, writes, waits)

        out_ap, in_ap = _ap(out_ap), _ap(in_ap)

        def fn(e, out_ap=out_ap, in_ap=in_ap):
            return e.dma_start(out=out_ap, in_=in_ap)

        self.items[q].append(Item(fn, deps, None, dma_t=t))
        self._book(t, reads, writes)
        if not self.pending_dma or self.pending_dma[-1] is not t:
            self.pending_dma.append(t)
        return t

    def wait_all(self, t):
        for eng in ENGS:
            self.pre[eng] = self.pre.get(eng, []) + [t]

    def collective(self, in_ap, out_ap, groups):
        if not hasattr(self, "ccsem"):
            self.ccsem = self.new_sem("cc")
            self.cccount = 0
        self.cccount += 1
        t = Ticket("dma")
        t.sem = self.ccsem
        t.val = self.cccount
        deps = self._deps("pool", t, (), (), ())

        def fn(e):
            return e.collective_compute("AllGather", ALU.bypass, replica_groups=groups, ins=[in_ap], outs=[out_ap])

        it = Item(fn, deps, None, dma_t=t)
        it.inc1 = True
        self.items["pool"].append(it)
        return t

    def matmul(self, out, lhsT, rhs, start, stop, reads, writes, ticket=None, last=True):
        return self.op("pe", lambda e: e.matmul(out, lhsT, rhs, start=start, stop=stop),
                       reads, writes, ticket, last)

    def transpose(self, out, in_, ident, reads, writes, ticket=None, last=True):
        return self.op("pe", lambda e: e.transpose(out, in_, ident), reads, writes, ticket, last)

    def act(self, out, in_, func, reads, writes, bias=None, scale=None, accum_out=None, eng="act"):
        kw = {}
        if bias is not None:
            kw["bias"] = bias
        if scale is not None:
            kw["scale"] = scale
        if accum_out is not None:
            kw["accum_out"] = accum_out
        return self.op("act", lambda e: e.activation(out, in_, func, **kw), reads, writes)

    def tt(self, eng, out, in0, in1, op, reads, writes):
        return self.op(eng, lambda e: e.tensor_tensor(out, in0, in1, op), reads, writes)

    def ts(self, eng, out, in0, s1, s2, op0, op1, reads, writes):
        if op1 is None:
            return self.op(eng, lambda e: e.tensor_scalar(out, in0, s1, None, op0), reads, writes)
        return self.op(eng, lambda e: e.tensor_scalar(out, in0, s1, s2, op0, op1), reads, writes)

    def stt(self, out, in0, scalar, in1, op0, op1, reads, writes):
        return self.op("dve", lambda e: e.scalar_tensor_tensor(out, in0, scalar, in1, op0, op1),
                       reads, writes)

    def copy(self, eng, out, in_, reads, writes):
        if eng == "act":
            return self.op("act", lambda e: e.copy(out, in_), reads, writes)
        return self.op(eng, lambda e: e.tensor_copy(out, in_), reads, writes)

    def memset(self, eng, ap, val, writes):
        return self.op(eng, lambda e: e.memset(ap, val), (), writes)

    def finalize(self):
        LIM = 30000
        for eng in ENGS:
            if eng == "sp":
                continue
            sem = None
            cnt = 0
            for pos, it in enumerate(self.items[eng]):
                if it.sig is not None:
                    if sem is None or cnt >= LIM:
                        sem = self.new_sem("e" + eng)
                        cnt = 0
                    cnt += 1
                    it.sig.sem = sem
                    it.sig.val = cnt
                    it.sig.pos = pos
        for eng in ENGS:
            for pos, it in enumerate(self.items[eng]):
                for d in it.deps:
                    assert d.val is not None and d.sem is not None, (eng, pos, d.eng)
                    if d.eng == eng:
                        assert d.pos < pos, ("self-deadlock", eng, pos, d.pos)

    def replay(self, eng, e):
        waited = {}
        for it in self.items[eng]:
            for d in it.deps:
                k = id(d.sem)
                if waited.get(k, 0) < d.val:
                    e.wait_ge(d.sem, d.val)
                    waited[k] = d.val
            ins = it.fn(e)
            if it.dma_t is not None:
                if it.inc1:
                    ins.then_inc(it.dma_t.sem)
                else:
                    ins.then_inc(it.dma_t.sem, 16)
            elif it.sig is not None:
                ins.then_inc(it.sig.sem, 1)


class Scope(ExitStack):
    def __init__(self, P):
        super().__init__()
        self.P = P
        P.push_scope()

    def __exit__(self, *a):
        self.P.barrier()
        self.P.pop_scope()
        return super().__exit__(*a)


class Ring:
    def __init__(self, aps):
        self.aps = aps
        self.bufs = [Buf(f"ring{i}") for i in range(len(aps))]
        self.i = 0

    def next(self):
        k = self.i % len(self.aps)
        self.i += 1
        return self.aps[k], self.bufs[k]


class Builder:
    def __init__(self, mode, layers, dbg=None, ncores=8):
        self.mode = mode
        self.layers = layers
        self.ncores = ncores
        self.dbg = dbg or {}
        self.nc = bass.Bass("TRN2", target_bir_lowering=False)
        self.stack = ExitStack()
        self.P = Prog(self.nc, self.stack)
        self.dram = {}
        self.final_tickets = []

    def din(self, name, shape, dt=F32):
        t = self.nc.dram_tensor(name, list(shape), dt, kind="ExternalInput")
        self.dram[name] = t
        return t.ap()

    def dout(self, name, shape, dt=F32):
        t = self.nc.dram_tensor(name, list(shape), dt, kind="ExternalOutput")
        self.dram[name] = t
        return t.ap()

    def dscr(self, name, shape, dt):
        t = self.nc.dram_tensor(name, list(shape), dt)
        self.dram[name] = t
        return t.ap()

    def sb(self, name, shape, dt):
        return self.stack.enter_context(self.nc.sbuf_tensor(name, list(shape), dt))

    def sb_in(self, st, name, shape, dt):
        self.uid = getattr(self, "uid", 0) + 1
        if not hasattr(self, "tn"):
            self.tn = {}
        self.tn.setdefault(name, []).append(f"{name}_{self.uid}")
        return st.enter_context(self.nc.sbuf_tensor(f"{name}_{self.uid}", list(shape), dt))

    def build(self):
        nc, P = self.nc, self.P
        mode = self.mode
        nl = len(self.layers)
        self.nl = nl
        hasA = mode in ("A", "FULL")
        hasBC = mode in ("BC", "FULL")
        full = mode == "FULL"

        self.x_in = self.din("x_in", [NT, D])
        self.h_out = self.dout("h_out", [NT, D])
        self.cmat = self.din("cmat", [4, 128, 128])
        self.pregT = self.din("pregT", [128, nl * 3 * 8])
        self.postg = self.din("postg", [nl * 3, D])
        if hasA:
            self.W_f1g = self.din("f1g", [nl, FC, 128, KC * 128])
            self.W_f1u = self.din("f1u", [nl, FC, 128, KC * 128])
            self.W_f1d = self.din("f1d", [nl, FC, 128, D])
            self.W_inT = self.din("winT", [nl, 20, 128, KC * 128])
            self.W_inV = self.din("winV", [nl, 2, 128, KC * 512])
            self.W_inF = self.din("winF", [nl, 128, KC * 8])
            self.nbf = self.din("nbf", [8, nl])
        if hasBC:
            self.W_f2g = self.din("f2g", [nl, FC, 128, KC * 128])
            self.W_f2u = self.din("f2u", [nl, FC, 128, KC * 128])
            self.W_f2d = self.din("f2d", [nl, FC, 128, D])
            self.W_gate = self.din("wgate", [nl, 24, 128, KC * 128])
            self.bgT = self.din("bgT", [128, nl * 24])
            self.W_br = self.din("wbr", [nl, 24, 128, 4 * 128])
            self.W_out = self.din("wout", [nl, 128, KC * D])
            self.W_mk = self.din("wmk", [nl, 4, 128, KC * 128])
            self.W_mv = self.din("wmv", [nl, 128, KC * 512])
            self.mem_in = self.din("mem", [256, D])
            self.memgT = self.din("memgT", [128, 8])
            self.cmask = self.din("cmask", [128, 32 * 514], BF16)
            self.selc = self.din("selc", [8, 32])
        mk_loc_in = self.din if mode == "BC" else (self.dout if mode == "A" else self.dscr)
        mk_loc_out = self.dout if mode == "A" else self.dscr

        def loc(name, shape, dt, needed_in_bc):
            if mode == "A":
                return self.dout(name, shape, dt)
            if mode == "BC":
                return self.din(name, shape, dt) if needed_in_bc else None
            return self.dscr(name, shape, dt)

        self.s_qsb = loc("s_qsb", [512, NT], BF16, True)
        self.s_qfx = loc("s_qfx", [512, NT], BF16, True)
        self.s_qmem = loc("s_qmem", [512, NT], BF16, True)
        if False:
            self.s_ksb = pk[0:512, :]
            self.s_kfx = pk[512:1024, :]
            self.s_vsb = pk[1024:1536, :].rearrange("r (b c) -> (r b) c", c=512)
            self.s_vfx = pk[1536:2048, :].rearrange("r (b c) -> (r b) c", c=512)
            self.s_lf = self.s_lfd
        else:
            self.s_ksb = loc("s_ksb", [512, NT], BF16, False)
            self.s_kfx = loc("s_kfx", [512, NT], BF16, False)
            self.s_vsb = loc("s_vsb", [NT, 512], BF16, False)
            self.s_vfx = loc("s_vfx", [NT, 512], BF16, False)
            self.s_lf = loc("s_lf", [8, NT], F32, True)
        if hasBC:
            if False:
                self.g_ksb = g3[:, 0:512, :]
                self.g_kfx = g3[:, 512:1024, :]
                self.g_vsb = g3[:, 1024:1536, :].rearrange("r a (b c) -> r (a b) c", c=512)
                self.g_vfx = g3[:, 1536:2048, :].rearrange("r a (b c) -> r (a b) c", c=512)
                self.g_lf = self.g_lfd.rearrange("(r h) n -> r h n", r=2)
            else:
                gk = self.din if mode == "BC" else self.dscr
                self.g_ksb = gk("g_ksb", [2, 512, NT], BF16)
                self.g_kfx = gk("g_kfx", [2, 512, NT], BF16)
                self.g_vsb = gk("g_vsb", [2, NT, 512], BF16)
                self.g_vfx = gk("g_vfx", [2, NT, 512], BF16)
                self.g_lf = gk("g_lf", [2, 8, NT], F32)
            self.s_mrow = self.dscr("s_mrow", [8, NT], BF16)
            self.s_crow = self.dscr("s_crow", [3, 8, S], BF16)
            self.s_osb = self.dscr("s_osb", [512, NT], BF16)
            self.s_ofx = self.dscr("s_ofx", [512, NT], BF16)
            self.s_omem = self.dscr("s_omem", [512, NT], BF16)
        self.dbg_out = {}
        for name, shape in self.dbg.items():
            if name not in ("stop", "plvl", "gph", "split", "v1", "v2", "overlap", "ffnpair", "ebf"):
                self.dbg_out[name] = self.dout("dbg_" + name, shape)

        self.h = self.sb("h", [128, NTILE, D], F32)
        self.hb = [Buf(f"h{i}") for i in range(NTILE)]
        self.ident = self.sb("ident", [128, 128], BF16)
        self.negtri = self.sb("negtri", [128, 128], BF16)
        self.ones = self.sb("ones", [128, 128], BF16)
        self.negones = self.sb("negones", [128, 128], BF16)
        self.identf = self.sb("identf", [128, 128], F32)
        self.pregs = self.sb("pregs", [128, nl * 24], F32)
        self.cb = Buf("consts", const=True)
        self.ssq = self.sb("ssq", [128, 8], F32)
        self.lnv = self.sb("lnv", [128, 8], F32)
        self.rstd = self.sb("rstd", [128, 8], F32)
        self.b_ssq = Buf("ssq")
        self.b_lnv = Buf("lnv")
        self.b_rstd = Buf("rstd")
        self.small = self.sb("small", [128, 4 * 4], F32)
        self.r_small = Ring([self.small[:, 4 * k:4 * k + 4] for k in range(4)])
        self.b_gpost = Buf("gpost")

        self.ps2 = [self.stack.enter_context(nc.psum_tensor(f"ps{k}", [128, 1024], F32)) for k in range(4)]
        self.psb = [Buf(f"bank{k}") for k in range(8)]
        for b_ in self.psb:
            b_.excl = True
        self.bank_ctr = 0

        self.cols = self.sb("cols", [128, 4], F32)
        self.eps_col = self.cols[:, 0:1]
        self.lnhalf_col = self.cols[:, 1:2]
        self.one_col = self.cols[:, 2:3]
        P.memset("dve", self.cols[:, 0:1], EPS, writes=[self.cb])
        P.memset("dve", self.cols[:, 1:2], float(np.log(0.5)), writes=[self.cb])
        P.memset("dve", self.cols[:, 2:3], 1.0, writes=[self.cb])
        self.cdb = Buf("cdma")
        ct = P.dma_ticket(self.cdb, "sp")
        self.cdbp = Buf("cdmap")
        ctp = P.dma_ticket(self.cdbp, "pool")
        self.ct = ct
        for k, t in enumerate((self.ident, self.negtri, self.ones, self.negones)):
            P.dma("pool", t[:, :], self.cmat[k], writes=[self.cb], sembuf=self.cdbp, ticket=ctp)
        P.dma("sp", self.identf[:, :], self.cmat[0], writes=[self.cb], sembuf=self.cdb, ticket=ct)
        P.dma("sp", self.pregs[:, :], self.pregT[:, :], writes=[self.cb], sembuf=self.cdb, ticket=ct)
        self.hdb = Buf("hdma")
        ht = P.dma_ticket(self.hdb, "sp")
        for i in range(NTILE):
            P.dma("sp", self.h[:, i, :], self.x_in[i * 128:(i + 1) * 128, :], writes=[self.hb[i]],
                  sembuf=self.hdb, ticket=ht)

        if hasBC:
            self.memT = self.sb("memT", [128, KC, 256], BF16)
            self.memgs = self.sb("memgs", [128, 8], F32)
            self.bgs = self.sb("bgs", [128, nl * 24], F32)
            P.dma("sp", self.memgs[:, :], self.memgT[:, :], writes=[self.cb], sembuf=self.cdb, ticket=ct)
            P.dma("sp", self.bgs[:, :], self.bgT[:, :], writes=[self.cb], sembuf=self.cdb, ticket=ct)
        if hasA:
            self.nbfs = self.sb("nbfs", [8, nl], F32)
            P.dma("sp", self.nbfs[:, :], self.nbf[:, :], writes=[self.cb], sembuf=self.cdb, ticket=ct)
            P.ts("dve", self.nbfs[:, :], self.nbfs[:, :], -1.0, None, ALU.mult, None, reads=[self.cb], writes=[self.cb])
        if hasBC:
            self.prep_mem()

        stop = self.dbg.get("stop")
        for li in range(nl):
            if stop == "init":
                break
            if hasA:
                if self.dbg.get("ffnpair", True) and NSG == 2 and not stop:
                    self.ffn_pair(li, 0)
                else:
                    for sgi in range(NSG):
                        self.ffn(li, 0, sgi)
                        if stop in ("prenorm", "gateup", "ffn", "down", "post1"):
                            break
                if stop in ("prenorm", "gateup", "ffn", "down", "post1"):
                    break
                for sgi in range(NSG):
                    self.proj(li, sgi)
            if full:
                self.exchange(li)
            if hasBC:
                self.attention_pre(li)
                if full:
                    self.P.wait_all(self.cc_tickets[0])
                self.attention(li)
                for sgi in range(NSG):
                    self.merge(li, sgi)
                if self.dbg.get("ffnpair", True) and NSG == 2:
                    self.ffn_pair(li, 2)
                else:
                    for sgi in range(NSG):
                        self.ffn(li, 2, sgi)

        P.barrier()
        ot = P.dma_ticket(self.hdb, "sp")
        for i in range(NTILE):
            P.dma("sp", self.h_out[i * 128:(i + 1) * 128, :], self.h[:, i, :], reads=[self.hb[i]],
                  sembuf=self.hdb, ticket=ot)
        P.barrier()
        P.op("dve", lambda e: e.memset(self.small[:, 0:1], 0.0), (), ())
        P.finalize()
        with nc.Block() as block:
            @block.tensor
            def _(e):
                P.replay("pe", e)

            @block.scalar
            def _(e):
                P.replay("act", e)

            @block.vector
            def _(e):
                P.replay("dve", e)

            @block.gpsimd
            def _(e):
                P.replay("pool", e)

            @block.sync
            def _(e):
                P.replay("sp", e)
        return nc

    def bank(self):
        k = self.bank_ctr % 8
        self.bank_ctr += 1
        return self.ps2[k // 2][:, (k % 2) * 512:(k % 2) * 512 + 512], self.psb[k]

    def bank2(self):
        if self.bank_ctr % 2:
            self.bank_ctr += 1
        k = self.bank_ctr % 8
        self.bank_ctr += 2
        return self.ps2[k // 2][:, :], [self.psb[k], self.psb[k + 1]]

    def load_gpost(self, li, w):
        self.P.dma("sp", self.gpost[:, :], self.postg[li * 3 + w:li * 3 + w + 1, :].partition_broadcast(128),
                   writes=[self.b_gpost])

    def prenorm(self, tiles, gcol, uT, uTb):
        self.prenorm_src([(self.h[:, i, :], self.hb[i]) for i in tiles], gcol, uT, uTb)

    def prenorm_src(self, srcs, gcol, uT, uTb):
        P = self.P
        n = len(srcs)
        for k, (xa, xbuf) in enumerate(srcs):
            junk, jb = self.r_junk.next()
            P.act(junk, xa, AF.Square, reads=[xbuf], writes=[jb, self.b_ssq],
                  accum_out=self.ssq[:, k:k + 1])
        P.act(self.lnv[:, 0:n], self.ssq[:, 0:n], AF.Ln, reads=[self.b_ssq, self.cb], writes=[self.b_lnv],
              scale=1.0 / D, bias=self.eps_col[:, 0:1])
        P.act(self.rstd[:, 0:n], self.lnv[:, 0:n], AF.Exp, reads=[self.b_lnv], writes=[self.b_rstd],
              scale=-0.5)
        for k, (xa, xbuf) in enumerate(srcs):
            xn, xb = self.r_xn.next()
            P.ts("dve", xn, xa, self.rstd[:, k:k + 1], None, ALU.mult, None,
                 reads=[xbuf, self.b_rstd], writes=[xb])
            bk, bb = self.bank()
            bkb = bk.bitcast(BF16)
            t = Ticket("pe")
            for c in range(KC):
                P.transpose(bkb[:, c * 128:(c + 1) * 128], xn[:, c * 128:(c + 1) * 128], self.ident[:, :],
                            reads=[xb, self.cb], writes=[bb], ticket=t, last=(c == KC - 1))
            P.tt("dve", uT[:, :, k * 128:(k + 1) * 128],
                 bkb.rearrange("p (c n) -> p c n", c=KC),
                 gcol.unsqueeze(2).to_broadcast([128, KC, 128]), ALU.mult,
                 reads=[bb, self.cb], writes=[uTb(k)])

    def prenorm_stats(self, srcs):
        P = self.P
        n = len(srcs)
        for k, (xa, xbuf) in enumerate(srcs):
            junk, jb = self.r_junk.next()
            P.act(junk, xa, AF.Square, reads=[xbuf], writes=[jb, self.b_ssq],
                  accum_out=self.ssq[:, k:k + 1])
        P.act(self.lnv[:, 0:n], self.ssq[:, 0:n], AF.Ln, reads=[self.b_ssq, self.cb], writes=[self.b_lnv],
              scale=1.0 / D, bias=self.eps_col[:, 0:1])
        P.act(self.rstd[:, 0:n], self.lnv[:, 0:n], AF.Exp, reads=[self.b_lnv], writes=[self.b_rstd],
              scale=-0.5)

    def prenorm_scale(self, k, xa, xbuf):
        xn, xb = self.r_xn.next()
        self.P.ts("dve", xn, xa, self.rstd[:, k:k + 1], None, ALU.mult, None,
                  reads=[xbuf, self.b_rstd], writes=[xb])
        return xn, xb

    def prenorm_tr(self, k, xn, xb, gcol, uT, uTb):
        P = self.P
        bk, bb = self.bank()
        bkb = bk.bitcast(BF16)
        t = Ticket("pe")
        for c in range(KC):
            P.transpose(bkb[:, c * 128:(c + 1) * 128], xn[:, c * 128:(c + 1) * 128], self.ident[:, :],
                        reads=[xb, self.cb], writes=[bb], ticket=t, last=(c == KC - 1))
        P.tt("dve", uT[:, :, k * 128:(k + 1) * 128],
             bkb.rearrange("p (c n) -> p c n", c=KC),
             gcol.unsqueeze(2).to_broadcast([128, KC, 128]), ALU.mult,
             reads=[bb, self.cb], writes=[uTb(k)])

    def postnorm(self, o2, o2b, gp, factor, i):
        P = self.P
        sm, smb = self.r_small.next()
        junk, jb = self.r_junk.next()
        P.act(junk, o2, AF.Square, reads=o2b, writes=[jb, smb], accum_out=sm[:, 0:1])
        lvl = int(self.dbg.get("plvl", 9))
        if lvl < 1:
            return
        P.act(sm[:, 1:2], sm[:, 0:1], AF.Ln, reads=[smb], writes=[smb], scale=1.0 / D, bias=self.eps_col[:, 0:1])
        P.act(sm[:, 2:3], sm[:, 1:2], AF.Exp, reads=[smb], writes=[smb], scale=-0.5,
              bias=(self.lnhalf_col[:, 0:1] if factor == 0.5 else None))
        if lvl < 2:
            return
        tw, twb = self.r_tw.next()
        if self.dbg.get("gph"):
            gp = self.h[:, i, :]
        if self.dbg.get("v1"):
            P.tt("dve", tw, o2, self.h[:, i, :], ALU.mult, reads=o2b, writes=[twb])
        elif self.dbg.get("v2"):
            P.tt("dve", tw, self.h[:, i, :], gp, ALU.mult, reads=o2b + [self.b_gpost], writes=[twb])
        elif self.dbg.get("split"):
            P.tt("dve", tw[:, 0:512], o2[:, 0:512], gp[:, 0:512], ALU.mult, reads=o2b + [self.b_gpost], writes=[twb])
            P.tt("dve", tw[:, 512:1024], o2[:, 512:1024], gp[:, 512:1024], ALU.mult, reads=o2b + [self.b_gpost], writes=[twb])
        else:
            P.tt("dve", tw, o2, gp, ALU.mult, reads=o2b + [self.b_gpost, smb], writes=[twb])
        if lvl < 3:
            return
        P.stt(self.h[:, i, :], tw, sm[:, 2:3], self.h[:, i, :], ALU.mult, ALU.add,
              reads=[twb, smb, self.hb[i]], writes=[self.hb[i]])

    def open_scope(self):
        self.P.barrier()
        return Scope(self.P)

    def norm_scratch(self, st):
        junk = self.sb_in(st, "junk", [128, D], BF16)
        self.r_junk = Ring([junk[:, 0:D]])
        xn = self.sb_in(st, "xn", [128, 2 * D], BF16)
        self.r_xn = Ring([xn[:, k * D:(k + 1) * D] for k in range(2)])
        tw = self.sb_in(st, "tw", [128, D], F32)
        self.r_tw = Ring([tw[:, 0:D]])
        self.gpost = self.sb_in(st, "gpost", [128, D], F32)
        self.b_gpost = Buf("gpost")

    def ffn(self, li, which, sgi):
        P, nc = self.P, self.nc
        Wg, Wu, Wd = (self.W_f1g, self.W_f1u, self.W_f1d) if which == 0 else (self.W_f2g, self.W_f2u, self.W_f2d)
        with self.open_scope() as st:
            self.norm_scratch(st)
            uT = self.sb_in(st, "uT", [128, KC, SG], BF16)
            actT = self.sb_in(st, "actT", [128, FC, SG], BF16)
            wd = self.sb_in(st, "wd", [128, FC, D], BF16)
            wgu = self.sb_in(st, "wgu", [128, 6, KC * 128], BF16)
            sil = self.sb_in(st, "sil", [128, 2, TG], F32)
            r_wg = Ring([wgu[:, k, :] for k in range(3)])
            r_wu = Ring([wgu[:, 3 + k, :] for k in range(3)])
            r_sil = Ring([sil[:, k, :] for k in range(2)])
            uTb = [Buf("uT0"), Buf("uT1")]
            actb = [Buf("act0"), Buf("act1")]
            wdb = Buf("wd")
            self.load_gpost(li, which)
            gcol = self.pregs[:, (li * 3 + which) * 8:(li * 3 + which) * 8 + 8]
            self.prenorm([sgi * 8 + k for k in range(8)], gcol, uT, lambda k: uTb[k // 4])
            if self.dbg.get("stop") == "prenorm":
                return
            wd_t = P.dma_ticket(wdb, "pool")
            for j in range(FC):
                wg, wgb = r_wg.next()
                wu, wub = r_wu.next()
                P.dma("pool", wg, Wg[li, j], writes=[wgb])
                P.dma("pool", wu, Wu[li, j], writes=[wub])
                P.dma("pool", wd[:, j, :], Wd[li, j], writes=[wdb], ticket=wd_t)
                for tg in range(2):
                    gk, gb = self.bank()
                    uk, ub = self.bank()
                    t = Ticket("pe")
                    for c in range(KC):
                        P.matmul(gk, wg[:, c * 128:(c + 1) * 128], uT[:, c, tg * TG:(tg + 1) * TG],
                                 c == 0, c == KC - 1, reads=[wgb, uTb[tg]], writes=[gb], ticket=t, last=(c == KC - 1))
                    t = Ticket("pe")
                    for c in range(KC):
                        P.matmul(uk, wu[:, c * 128:(c + 1) * 128], uT[:, c, tg * TG:(tg + 1) * TG],
                                 c == 0, c == KC - 1, reads=[wub, uTb[tg]], writes=[ub], ticket=t, last=(c == KC - 1))
                    s, sbf = r_sil.next()
                    P.act(s, gk, AF.Silu, reads=[gb], writes=[sbf])
                    P.tt("dve", actT[:, j, tg * TG:(tg + 1) * TG], s, uk, ALU.mult,
                         reads=[sbf, ub], writes=[actb[tg]])
            gp = self.gpost[:, :]
            if self.dbg.get("stop") == "gateup":
                return
            for k in range(8):
                o2, o2b = self.bank2()
                for hf in range(2):
                    t = Ticket("pe")
                    for j in range(FC):
                        P.matmul(o2[:, hf * 512:(hf + 1) * 512], actT[:, j, k * 128:(k + 1) * 128],
                                 wd[:, j, hf * 512:(hf + 1) * 512], j == 0, j == FC - 1,
                                 reads=[actb[k // 4], wdb], writes=[o2b[hf]], ticket=t, last=(j == FC - 1))
                if self.dbg.get("stop") == "down":
                    continue
                self.postnorm(o2, o2b, gp, 0.5, sgi * 8 + k)
                if self.dbg.get("stop") == "post1":
                    break
            P.barrier()

    def ffn_pair(self, li, which):
        P, nc = self.P, self.nc
        Wg, Wu, Wd = (self.W_f1g, self.W_f1u, self.W_f1d) if which == 0 else (self.W_f2g, self.W_f2u, self.W_f2d)
        with self.open_scope() as st:
            self.norm_scratch(st)
            uT = self.sb_in(st, "uT", [128, KC, SG], BF16)
            actT = self.sb_in(st, "actT", [128, FC, SG], BF16)
            wd = self.sb_in(st, "wd", [128, FC, D], BF16)
            wgu = self.sb_in(st, "wgu", [128, 6, KC * 128], BF16)
            sil = self.sb_in(st, "sil", [128, 2, TG], F32)
            r_wg = Ring([wgu[:, k, :] for k in range(3)])
            r_wu = Ring([wgu[:, 3 + k, :] for k in range(3)])
            r_sil = Ring([sil[:, k, :] for k in range(2)])
            uTb = [Buf("uT0"), Buf("uT1")]
            actb = [Buf("act0"), Buf("act1")]
            wdb = Buf("wd")
            uTbf = lambda k: uTb[k // 4]
            self.load_gpost(li, which)
            gcol = self.pregs[:, (li * 3 + which) * 8:(li * 3 + which) * 8 + 8]
            self.prenorm([k for k in range(8)], gcol, uT, uTbf)
            wd_t = P.dma_ticket(wdb, "pool")
            gp = self.gpost[:, :]
            for sgi in range(2):
                for j in range(FC):
                    wg, wgb = r_wg.next()
                    wu, wub = r_wu.next()
                    P.dma("pool", wg, Wg[li, j], writes=[wgb])
                    P.dma("pool", wu, Wu[li, j], writes=[wub])
                    if sgi == 0:
                        P.dma("pool", wd[:, j, :], Wd[li, j], writes=[wdb], ticket=wd_t)
                    for tg in range(2):
                        gk, gb = self.bank()
                        uk, ub = self.bank()
                        t = Ticket("pe")
                        for c in range(KC):
                            P.matmul(gk, wg[:, c * 128:(c + 1) * 128], uT[:, c, tg * TG:(tg + 1) * TG],
                                     c == 0, c == KC - 1, reads=[wgb, uTb[tg]], writes=[gb], ticket=t,
                                     last=(c == KC - 1))
                        t = Ticket("pe")
                        for c in range(KC):
                            P.matmul(uk, wu[:, c * 128:(c + 1) * 128], uT[:, c, tg * TG:(tg + 1) * TG],
                                     c == 0, c == KC - 1, reads=[wub, uTb[tg]], writes=[ub], ticket=t,
                                     last=(c == KC - 1))
                        s, sbf = r_sil.next()
                        P.act(s, gk, AF.Silu, reads=[gb], writes=[sbf])
                        P.tt("dve", actT[:, j, tg * TG:(tg + 1) * TG], s, uk, ALU.mult,
                             reads=[sbf, ub], writes=[actb[tg]])
                nxt = None
                if sgi == 0:
                    nxt = [(self.h[:, 8 + k, :], self.hb[8 + k]) for k in range(8)]
                    self.prenorm_stats(nxt)
                    xnc = self.prenorm_scale(0, *nxt[0])
                for k in range(8):
                    o2, o2b = self.bank2()
                    for hf in range(2):
                        t = Ticket("pe")
                        for j in range(FC):
                            P.matmul(o2[:, hf * 512:(hf + 1) * 512], actT[:, j, k * 128:(k + 1) * 128],
                                     wd[:, j, hf * 512:(hf + 1) * 512], j == 0, j == FC - 1,
                                     reads=[actb[k // 4], wdb], writes=[o2b[hf]], ticket=t, last=(j == FC - 1))
                    if nxt is not None:
                        self.prenorm_tr(k, xnc[0], xnc[1], gcol, uT, uTbf)
                        if k + 1 < 8:
                            xnc = self.prenorm_scale(k + 1, *nxt[k + 1])
                    self.postnorm(o2, o2b, gp, 0.5, sgi * 8 + k)
            P.barrier()

    def proj(self, li, sgi):
        P = self.P
        with self.open_scope() as st:
            self.norm_scratch(st)
            uT = self.sb_in(st, "uT", [128, KC, SG], BF16)
            wblk = self.sb_in(st, "wblk", [128, 3, KC * 128], BF16)
            wv = self.sb_in(st, "wv", [128, KC * 512], BF16)
            wf = self.sb_in(st, "wf", [128, KC * 8], BF16)
            stg = self.sb_in(st, "stg", [128, 4, TG], BF16)
            lft = self.sb_in(st, "lft", [8, 3, TG], F32)
            r_w = Ring([wblk[:, k, :] for k in range(3)])
            r_stg = Ring([stg[:, k, :] for k in range(4)])
            uTb = [Buf("uT0"), Buf("uT1")]
            wvb, wfb, lfb = Buf("wv"), Buf("wf"), Buf("lf")
            gcol = self.pregs[:, (li * 3 + 1) * 8:(li * 3 + 1) * 8 + 8]
            self.prenorm([sgi * 8 + k for k in range(8)], gcol, uT, lambda k: uTb[k // 4])
            n0 = sgi * SG
            dests = [(self.s_qsb, 0.125)] * 4 + [(self.s_ksb, 1.0)] * 4 + [(self.s_qfx, 0.125)] * 4 + \
                    [(self.s_kfx, 1.0)] * 4 + [(self.s_qmem, 1.0)] * 4
            for blk in range(20):
                w, wb = r_w.next()
                P.dma("pool", w, self.W_inT[li, blk], writes=[wb])
                dst, scl = dests[blk]
                r0 = (blk % 4) * 128
                for tg in range(2):
                    bk, bb = self.bank()
                    t = Ticket("pe")
                    for c in range(KC):
                        P.matmul(bk, w[:, c * 128:(c + 1) * 128], uT[:, c, tg * TG:(tg + 1) * TG],
                                 c == 0, c == KC - 1, reads=[wb, uTb[tg]], writes=[bb], ticket=t, last=(c == KC - 1))
                    sg_, sgb = r_stg.next()
                    if (blk + tg) % 2 == 0:
                        P.act(sg_, bk, AF.Copy, reads=[bb], writes=[sgb], scale=scl)
                    else:
                        P.ts("dve", sg_, bk, scl, None, ALU.mult, None, reads=[bb], writes=[sgb])
                    P.dma("sp", dst[r0:r0 + 128, n0 + tg * TG:n0 + (tg + 1) * TG], sg_, reads=[sgb])
            for vi, dst in enumerate((self.s_vsb, self.s_vfx)):
                P.dma("pool", wv, self.W_inV[li, vi], writes=[wvb])
                for k in range(8):
                    bk, bb = self.bank()
                    t = Ticket("pe")
                    for c in range(KC):
                        P.matmul(bk, uT[:, c, k * 128:(k + 1) * 128], wv[:, c * 512:(c + 1) * 512],
                                 c == 0, c == KC - 1, reads=[wvb, uTb[k // 4]], writes=[bb], ticket=t,
                                 last=(c == KC - 1))
                    sg_, sgb = r_stg.next()
                    if k % 2 == 0:
                        P.act(sg_, bk, AF.Copy, reads=[bb], writes=[sgb])
                    else:
                        P.copy("dve", sg_, bk, reads=[bb], writes=[sgb])
                    P.dma("sp", dst[n0 + k * 128:n0 + (k + 1) * 128, :], sg_, reads=[sgb])
            P.dma("pool", wf, self.W_inF[li], writes=[wfb])
            for tg in range(2):
                bk, bb = self.bank()
                t = Ticket("pe")
                for c in range(KC):
                    P.matmul(bk[0:8, :], wf[:, c * 8:(c + 1) * 8], uT[:, c, tg * TG:(tg + 1) * TG],
                             c == 0, c == KC - 1, reads=[wfb, uTb[tg]], writes=[bb], ticket=t, last=(c == KC - 1))
                P.act(lft[:, 0, :], bk[0:8, :], AF.Exp, reads=[bb, self.cb], writes=[lfb], scale=-1.0,
                      bias=self.nbfs[:, li:li + 1])
                P.act(lft[:, 1, :], lft[:, 0, :], AF.Ln, reads=[lfb], writes=[lfb], bias=self.one_col[0:8, 0:1])
                P.ts("dve", lft[:, 2, :], lft[:, 1, :], -1.0, None, ALU.mult, None, reads=[lfb], writes=[lfb])
                P.dma("sp", self.s_lf[:, n0 + tg * TG:n0 + (tg + 1) * TG], lft[:, 2, :], reads=[lfb])
            P.barrier()

    def prep_mem(self):
        P = self.P
        with self.open_scope() as st:
            self.norm_scratch(st)
            mt = self.sb_in(st, "memtile", [128, 2, D], F32)
            mb = [Buf("m0"), Buf("m1")]
            for k in range(2):
                P.dma("sp", mt[:, k, :], self.mem_in[k * 128:(k + 1) * 128, :], writes=[mb[k]])
            ub = Buf("memT")
            self.memTb = Buf("memTc", const=True)
            self.prenorm_src([(mt[:, k, :], mb[k]) for k in range(2)], self.memgs[:, 0:8], self.memT, lambda k: ub)
            P.barrier()

    def mem_kv(self, li, KmT, Vm, kvb):
        P = self.P
        with self.open_scope() as st:
            wmk = self.sb_in(st, "wmk", [128, 2, KC * 128], BF16)
            wmv = self.sb_in(st, "wmv", [128, KC * 512], BF16)
            r_w = Ring([wmk[:, k, :] for k in range(2)])
            wvb = Buf("wmv")
            for hm in range(4):
                w, wb = r_w.next()
                P.dma("pool", w, self.W_mk[li, hm], writes=[wb])
                bk, bb = self.bank()
                t = Ticket("pe")
                for c in range(KC):
                    P.matmul(bk[:, 0:256], w[:, c * 128:(c + 1) * 128], self.memT[:, c, :], c == 0, c == KC - 1,
                             reads=[wb], writes=[bb], ticket=t, last=(c == KC - 1))
                P.copy("dve", KmT[:, hm, :], bk[:, 0:256], reads=[bb], writes=[kvb])
            P.dma("pool", wmv, self.W_mv[li], writes=[wvb])
            for blk in range(2):
                bk, bb = self.bank()
                t = Ticket("pe")
                for c in range(KC):
                    P.matmul(bk, self.memT[:, c, blk * 128:(blk + 1) * 128], wmv[:, c * 512:(c + 1) * 512],
                             c == 0, c == KC - 1, reads=[wvb], writes=[bb], ticket=t, last=(c == KC - 1))
                P.copy("dve", Vm[:, blk, :], bk, reads=[bb], writes=[kvb])
            P.barrier()

    def attention_pre(self, li):
        P, nc = self.P, self.nc
        with self.open_scope() as st0:
            KmT = self.sb_in(st0, "KmT", [128, 4, 256], BF16)
            Vm = self.sb_in(st0, "Vm", [128, 2, 512], BF16)
            kvb = Buf("memkv")
            self.mem_kv(li, KmT, Vm, kvb)
            with self.open_scope() as st:
                pT = self.sb_in(st, "pTm", [128, 3, TG], BF16)
                recT = self.sb_in(st, "recTm", [128, 2, TG], F32)
                osT = self.sb_in(st, "osTm", [128, 4, TG], BF16)
                r_p = Ring([pT[:, k, :] for k in range(3)])
                r_rec = Ring([recT[:, k, :] for k in range(2)])
                r_os = Ring([osT[:, k, :] for k in range(4)])
                qm = self.sb_in(st, "qm", [128, 2, NT], BF16)
                qmb = [Buf("qm0"), Buf("qm1")]
                scale_m = 128.0 ** -0.5
                for hm in range(4):
                    k = hm % 2
                    P.dma("sp", qm[:, k, :], self.s_qmem[hm * 128:(hm + 1) * 128, :], writes=[qmb[k]])
                    for g in range(4):
                        ps = []
                        for blk in range(2):
                            bk, bb = self.bank()
                            P.matmul(bk, KmT[:, hm, blk * 128:(blk + 1) * 128], qm[:, k, g * TG:(g + 1) * TG], True, True,
                                     reads=[kvb, qmb[k]], writes=[bb])
                            p, pb = r_p.next()
                            P.act(p, bk, AF.Exp, reads=[bb], writes=[pb], scale=scale_m)
                            ps.append((p, pb))
                        ok, okb = self.bank()
                        dk, dkb = self.bank()
                        t = Ticket("pe")
                        for blk in range(2):
                            P.matmul(ok, Vm[:, blk, hm * 128:(hm + 1) * 128], ps[blk][0], blk == 0, blk == 1,
                                     reads=[kvb, ps[blk][1]], writes=[okb], ticket=t, last=(blk == 1))
                        t = Ticket("pe")
                        for blk in range(2):
                            P.matmul(dk, self.ones[:, :], ps[blk][0], blk == 0, blk == 1,
                                     reads=[self.cb, ps[blk][1]], writes=[dkb], ticket=t, last=(blk == 1))
                        rec, recb = r_rec.next()
                        P.op("dve", lambda e, rec=rec, dk=dk: e.reciprocal(rec, dk), reads=[dkb], writes=[recb])
                        os_, osb_ = r_os.next()
                        P.tt("dve", os_, ok, rec, ALU.mult, reads=[okb, recb], writes=[osb_])
                        P.dma("sp", self.s_omem[hm * 128:(hm + 1) * 128, g * TG:(g + 1) * TG], os_, reads=[osb_])
                P.barrier()


    def attention(self, li):
        P, nc = self.P, self.nc
        with self.open_scope() as st0:
            with self.open_scope() as st:
                lfg = self.sb_in(st, "lfg", [8, S], F32)
                cT = self.sb_in(st, "cT", [8, S], F32)
                lfl = self.sb_in(st, "lfl", [8, NT], F32)
                cl = self.sb_in(st, "cl", [8, NT], F32)
                mrow = self.sb_in(st, "mrow", [8, NT], BF16)
                tot = self.sb_in(st, "tot", [8, 64], F32)
                sel = self.sb_in(st, "sel", [8, 32], F32)
                b1, b2, b3 = Buf("lfg"), Buf("lfl"), Buf("misc")
                t = P.dma_ticket(b1, "sp")
                for c in range(8):
                    r, gl = CHUNK_OWNER[c]
                    P.dma("sp", lfg[:, c * 512:(c + 1) * 512], self.g_lf[r, :, gl * 512:(gl + 1) * 512],
                          writes=[b1], ticket=t)
                P.dma("sp", lfl[:, :], self.s_lf[:, :], writes=[b2])
                P.dma("sp", sel[:, :], self.selc[:, :], writes=[b3])
                onesb = self.one_col[0:8, 0:1].to_broadcast([8, S])
                cTb = Buf("cT")
                P.op("dve", lambda e: e.tensor_tensor_scan(cT[:, :], onesb, lfg[:, :], 0.0, ALU.mult, ALU.add),
                     reads=[b1, self.cb], writes=[cTb])
                P.op("dve", lambda e: e.tensor_reduce(tot[:, 0:8], lfg[:, :].rearrange("p (c n) -> p c n", c=8),
                                                      mybir.AxisListType.X, ALU.add),
                     reads=[b1], writes=[b3])
                for g in range(4):
                    P.tt("dve", tot[:, 16 + g * 8:24 + g * 8], tot[:, 0:8], sel[:, g * 8:(g + 1) * 8], ALU.mult,
                         reads=[b3], writes=[b3])
                    P.op("dve", lambda e, g=g: e.tensor_reduce(tot[:, 8 + g:9 + g], tot[:, 16 + g * 8:24 + g * 8],
                                                               mybir.AxisListType.X, ALU.add),
                         reads=[b3], writes=[b3])
                clb = Buf("cl")
                for g in range(4):
                    ob = self.one_col[0:8, 0:1].to_broadcast([8, 512])
                    P.op("dve", lambda e, g=g, ob=ob: e.tensor_tensor_scan(
                        cl[:, g * 512:(g + 1) * 512], ob, lfl[:, g * 512:(g + 1) * 512],
                        tot[:, 8 + g:9 + g], ALU.mult, ALU.add), reads=[b2, b3, self.cb], writes=[clb])
                P.copy("dve", mrow[:, :], cl[:, :], reads=[clb], writes=[clb])
                mrow_t = P.dma("sp", self.s_mrow[:, :], mrow[:, :], reads=[clb])
                cs3 = self.sb_in(st, "cs3", [8, 3, S], BF16)
                csr = self.sb_in(st, "csr", [8, S], F32)
                csb = Buf("cs3")
                P.ts("dve", csr[:, :], cT[:, :], -1.0, None, ALU.mult, None, reads=[cTb], writes=[csb])
                P.copy("dve", cs3[:, 0, :], csr[:, :], reads=[csb], writes=[csb])
                P.tt("dve", csr[:, :], csr[:, :], cs3[:, 0, :], ALU.subtract, reads=[csb], writes=[csb])
                P.copy("dve", cs3[:, 1, :], csr[:, :], reads=[csb], writes=[csb])
                P.tt("dve", csr[:, :], csr[:, :], cs3[:, 1, :], ALU.subtract, reads=[csb], writes=[csb])
                P.copy("dve", cs3[:, 2, :], csr[:, :], reads=[csb], writes=[csb])
                crow_t = P.dma("sp", self.s_crow.rearrange("j h n -> h j n"), cs3[:, :, :], reads=[csb])
                if "cT" in self.dbg_out:
                    P.dma("sp", self.dbg_out["cT"][:, :], cT[:, :], reads=[cTb])
                if "cl" in self.dbg_out:
                    P.dma("sp", self.dbg_out["cl"][:, :], cl[:, :], reads=[clb])
                P.barrier()
            with self.open_scope() as st:
                mask = self.sb_in(st, "mask", [128, 32, 514], BF16)
                maskb = Buf("mask", const=True)
                P.dma("sp", mask[:, :, :], self.cmask.rearrange("p (u j) -> p u j", u=32), writes=[maskb])
                KT = [[self.sb_in(st, f"KT{s}{k}", [128, S], BF16) for k in range(2)] for s in range(2)]
                VA = [[self.sb_in(st, f"VA{s}{k}", [128, 32, 128], BF16) for k in range(2)] for s in range(2)]
                QT = [[self.sb_in(st, f"QT{s}{k}", [128, NT], BF16) for k in range(2)] for s in range(2)]
                KTb = [[Buf("KT") for k in range(2)] for s in range(2)]
                VAb = [[Buf("VA") for k in range(2)] for s in range(2)]
                QTb = [[Buf("QT") for k in range(2)] for s in range(2)]
                eT = self.sb_in(st, "eT", [128, 2, TG], F32)
                spT = self.sb_in(st, "spT", [128, 2, TG], BF16)
                wT = self.sb_in(st, "wT", [128, 3, TG], BF16)
                cbT = self.sb_in(st, "cbT", [128, 3, TG], BF16)
                pT = self.sb_in(st, "pT", [128, 3, TG], BF16)
                recT = self.sb_in(st, "recT", [128, 1, TG], F32)
                osT = self.sb_in(st, "osT", [128, 4, TG], BF16)
                r_e = Ring([eT[:, k, :] for k in range(2)])
                r_sp = Ring([spT[:, k, :] for k in range(2)])
                r_w = Ring([wT[:, k, :] for k in range(3)])
                r_cb = Ring([cbT[:, k, :] for k in range(3)])
                r_p = Ring([pT[:, k, :] for k in range(3)])
                r_rec = Ring([recT[:, k, :] for k in range(1)])
                r_os = Ring([osT[:, k, :] for k in range(4)])
                cst = Buf("attnconst")
                for s in range(2):
                    for k in range(2):
                        P.memset("pool", KT[s][k][64:128, :], 0.0, writes=[KTb[s][k]])
                        P.memset("pool", QT[s][k][64:128, :], 1.0 if s == 1 else 0.0, writes=[QTb[s][k]])
                        if s == 1:
                            P.memset("pool", KT[s][k][64:65, :], 1.0, writes=[KTb[s][k]])
                            P.memset("pool", VA[s][k][:, :, 64:128], 1.0, writes=[VAb[s][k]])

                def bankk(k):
                    return self.ps2[k // 2][:, (k % 2) * 512:(k % 2) * 512 + 512], self.psb[k]
                xs_ring = [bankk(0), bankk(1), bankk(2)]
                ots_ring = [bankk(3), bankk(4)]
                xf_ring = [bankk(5), bankk(6)]
                otf, otfb = bankk(7)
                self.bank_ctr = 0

                ccw = list(getattr(self, "cc_tickets", [])) if self.mode == "FULL" else []

                def load_head(h):
                    k = h % 2
                    for s, (gk, gv, sq) in enumerate(((self.g_ksb, self.g_vsb, self.s_qsb),
                                                      (self.g_kfx, self.g_vfx, self.s_qfx))):
                        tk = P.dma_ticket(KTb[s][k], "sp")
                        for c in range(8):
                            r, gl = CHUNK_OWNER[c]
                            P.dma("sp", KT[s][k][0:64, c * 512:(c + 1) * 512],
                                  gk[r, h * 64:(h + 1) * 64, gl * 512:(gl + 1) * 512], writes=[KTb[s][k]], ticket=tk,
                                  waits=ccw)
                        if s == 1:
                            for j in range(3):
                                P.dma("sp", KT[s][k][65 + j:66 + j, :], self.s_crow[j, h:h + 1, :], writes=[KTb[s][k]],
                                      ticket=tk, waits=[crow_t])
                        if s == 1 or h % 2 == 0:
                            kv = k if s == 1 else (h // 2) % 2
                            tv = P.dma_ticket(VAb[s][kv], "sp")
                            ncol = 64 if s == 1 else 128
                            c0 = h * 64
                            for c in range(8):
                                r, gl = CHUNK_OWNER[c]
                                P.dma("sp", VA[s][kv][:, c * 4:(c + 1) * 4, 0:ncol],
                                      gv[r, gl * 512:(gl + 1) * 512, c0:c0 + ncol].rearrange("(b p) d -> p b d", p=128),
                                      writes=[VAb[s][kv]], ticket=tv, waits=ccw)
                        tq = P.dma_ticket(QTb[s][k], "sp")
                        P.dma("sp", QT[s][k][0:64, :], sq[h * 64:(h + 1) * 64, :], writes=[QTb[s][k]], ticket=tq)
                        if s == 1:
                            P.dma("sp", QT[s][k][64:65, :], self.s_mrow[h:h + 1, :], writes=[QTb[s][k]], ticket=tq,
                                  waits=[mrow_t])

                def sbA(u):
                    h, g, kb, first, lastu = u[:5]
                    k = h % 2
                    x, xb = xs_ring[u[5] % 3]
                    masked = kb >= 8 * g
                    t = Ticket("pe")
                    P.matmul(x, KT[0][k][:, kb * 128:(kb + 1) * 128], QT[0][k][:, g * TG:(g + 1) * TG],
                             True, True, reads=[KTb[0][k], QTb[0][k]], writes=[xb], ticket=t, last=not masked)
                    if masked:
                        P.op("pe", lambda e: e.matmul(x, self.ident[:, :], mask[:, kb, 0:512], start=False, stop=True,
                                                      skip_group_check=True),
                             reads=[maskb, self.cb], writes=[xb], ticket=t, last=True)
                    e_, eb = r_e.next()
                    P.act(e_, x, AF.Exp, reads=[xb], writes=[eb])
                    sp, spb = r_sp.next()
                    P.act(sp, e_, AF.Ln, reads=[eb], writes=[spb], bias=1.0)
                    u[6]["x"] = (x, xb)
                    u[6]["sp"] = (sp, spb)
                    if not lastu:
                        rn, rnb = r_cb.next()
                        if first:
                            P.copy("pool", rn, sp, reads=[spb], writes=[rnb])
                        else:
                            rp, rpb = u[6]["rprev"]
                            P.tt("pool", rn, rp, sp, ALU.add, reads=[rpb, spb], writes=[rnb])
                        u[6]["R"] = (rn, rnb)

                def sbB(u):
                    h, g, kb, first, lastu = u[:5]
                    x, xb = u[6]["x"]
                    sp, spb = u[6]["sp"]
                    t = Ticket("pe")
                    P.op("pe", lambda e: e.matmul(x, self.negtri[:, :], sp, start=False, stop=True, skip_group_check=True),
                         reads=[spb, self.cb], writes=[xb], ticket=t, last=first)
                    if not first:
                        rp, rpb = u[6]["rprev"]
                        P.op("pe", lambda e: e.matmul(x, self.negones[:, :], rp, start=False, stop=True,
                                                      skip_group_check=True),
                             reads=[rpb, self.cb], writes=[xb], ticket=t, last=True)
                    w, wb = r_w.next()
                    P.act(w, x, AF.Exp, reads=[xb], writes=[wb])
                    u[6]["w"] = (w, wb)

                def sbC(u):
                    h, g, kb, first, lastu = u[:5]
                    kv = (h // 2) % 2
                    r0 = (h % 2) * 64
                    w, wb = u[6]["w"]
                    ots, otsb = ots_ring[(h * 4 + g) % 2]
                    P.op("pe", lambda e: e.matmul(ots, VA[0][kv][:, kb, :], w, start=first, stop=lastu,
                                                  skip_group_check=True),
                         reads=[wb, VAb[0][kv]], writes=[otsb])
                    if lastu:
                        os_, osb_ = r_os.next()
                        P.copy("dve", os_[r0:r0 + 64, :], ots[r0:r0 + 64, :], reads=[otsb], writes=[osb_])
                        P.dma("sp", self.s_osb[h * 64:(h + 1) * 64, g * TG:(g + 1) * TG], os_[r0:r0 + 64, :],
                              reads=[osb_])

                def fxA(u):
                    h, g, kb, first, lastu = u[:5]
                    k = h % 2
                    x, xb = xf_ring[u[5] % 2]
                    masked = kb >= 8 * g
                    t = Ticket("pe")
                    P.matmul(x, KT[1][k][:, kb * 128:(kb + 1) * 128], QT[1][k][:, g * TG:(g + 1) * TG],
                             True, True, reads=[KTb[1][k], QTb[1][k]], writes=[xb], ticket=t, last=not masked)
                    if masked:
                        P.op("pe", lambda e: e.matmul(x, self.ident[:, :], mask[:, kb, 1:513], start=False, stop=True,
                                                      skip_group_check=True),
                             reads=[maskb, self.cb], writes=[xb], ticket=t, last=True)
                    p, pb = r_p.next()
                    P.act(p, x, AF.Exp, reads=[xb], writes=[pb])
                    u[6]["p"] = (p, pb)

                def fxC(u):
                    h, g, kb, first, lastu = u[:5]
                    k = h % 2
                    p, pb = u[6]["p"]
                    P.op("pe", lambda e: e.matmul(otf, VA[1][k][:, kb, :], p, start=first, stop=lastu,
                                                  skip_group_check=True),
                         reads=[pb, VAb[1][k]], writes=[otfb])
                    if lastu:
                        r0 = 0
                        d0 = 64
                        rec, recb = r_rec.next()
                        P.op("dve", lambda e: e.reciprocal(rec[d0:d0 + 64, :], otf[d0:d0 + 64, :]),
                             reads=[otfb], writes=[recb])
                        os_, osb_ = r_os.next()
                        P.tt("dve", os_[r0:r0 + 64, :], otf[r0:r0 + 64, :], rec[d0:d0 + 64, :], ALU.mult,
                             reads=[otfb, recb], writes=[osb_])
                        P.dma("sp", self.s_ofx[h * 64:(h + 1) * 64, g * TG:(g + 1) * TG], os_[r0:r0 + 64, :],
                              reads=[osb_])

                units = []
                for h in range(8):
                    for g in range(4):
                        n = 8 * (g + 1)
                        for kb in range(n - 1, -1, -1):
                            units.append([h, g, kb, kb == n - 1, kb == 0, len(units), {}])
                units_f = [[u[0], u[1], u[2], u[3], u[4], u[5], {}] for u in units]
                load_head(0)
                N = len(units)
                for i in range(N + 2):
                    if i < N:
                        u = units[i]
                        if u[1] == 0 and u[2] == 4 and u[0] + 1 < 8:
                            load_head(u[0] + 1)
                        if not u[3]:
                            u[6]["rprev"] = units[i - 1][6]["R"]
                        sbA(units[i])
                        fxA(units_f[i])
                    if 1 <= i <= N:
                        ub_ = units[i - 1]
                        sbB(ub_)
                        fxC(units_f[i - 1])
                    if 2 <= i <= N + 1:
                        sbC(units[i - 2])

                P.barrier()

    def merge(self, li, sgi):
        P = self.P
        with self.open_scope() as st:
            self.norm_scratch(st)
            uT = self.sb_in(st, "uT", [128, KC, SG], BF16)
            oT = [self.sb_in(st, f"oT{b}", [128, 4, SG], BF16) for b in range(3)]
            mT = self.sb_in(st, "mT", [128, KC, SG], BF16)
            wgt = self.sb_in(st, "wgt", [128, 6, KC * 128], BF16)
            wbr = self.sb_in(st, "wbr", [128, 6, 4 * 128], BF16)
            wo = self.sb_in(st, "wo", [128, KC * D], BF16)
            sg = self.sb_in(st, "sg", [128, 3, TG], F32)
            mm = self.sb_in(st, "mm", [128, 4, TG], F32)
            r_wg = Ring([wgt[:, k, :] for k in range(6)])
            r_wb = Ring([wbr[:, k, :] for k in range(6)])
            r_sg = Ring([sg[:, k, :] for k in range(3)])
            r_mm = Ring([mm[:, k, :] for k in range(4)])
            uTb = [Buf("uT0"), Buf("uT1")]
            oTb = [Buf("oT") for _ in range(3)]
            mTb = [Buf("mT0"), Buf("mT1")]
            wob = Buf("wo")
            n0 = sgi * SG
            self.load_gpost(li, 1)
            for b, src_ in enumerate((self.s_osb, self.s_ofx, self.s_omem)):
                t = P.dma_ticket(oTb[b], "sp")
                for c in range(4):
                    P.dma("sp", oT[b][:, c, :], src_[c * 128:(c + 1) * 128, n0:n0 + SG], writes=[oTb[b]], ticket=t)
            P.dma("pool", wo, self.W_out[li], writes=[wob])
            gcol = self.pregs[:, (li * 3 + 1) * 8:(li * 3 + 1) * 8 + 8]
            self.prenorm([sgi * 8 + k for k in range(8)], gcol, uT, lambda k: uTb[k // 4])
            for fc in range(8):
                ws = []
                for b in range(3):
                    wg_, wgb = r_wg.next()
                    wb_, wbb = r_wb.next()
                    P.dma("pool", wg_, self.W_gate[li, b * 8 + fc], writes=[wgb])
                    P.dma("pool", wb_, self.W_br[li, b * 8 + fc], writes=[wbb])
                    ws.append((wg_, wgb, wb_, wbb))
                for tg in range(2):
                    ms = []
                    for b in range(3):
                        wg_, wgb, wb_, wbb = ws[b]
                        gk, gb = self.bank()
                        t = Ticket("pe")
                        for c in range(KC):
                            P.matmul(gk, wg_[:, c * 128:(c + 1) * 128], uT[:, c, tg * TG:(tg + 1) * TG],
                                     c == 0, c == KC - 1, reads=[wgb, uTb[tg]], writes=[gb], ticket=t,
                                     last=(c == KC - 1))
                        s_, sb_ = r_sg.next()
                        col = li * 24 + b * 8 + fc
                        P.act(s_, gk, AF.Sigmoid, reads=[gb, self.cb], writes=[sb_], bias=self.bgs[:, col:col + 1])
                        bk, bb = self.bank()
                        t = Ticket("pe")
                        for c in range(4):
                            P.matmul(bk, wb_[:, c * 128:(c + 1) * 128], oT[b][:, c, tg * TG:(tg + 1) * TG],
                                     c == 0, c == 3, reads=[wbb, oTb[b]], writes=[bb], ticket=t, last=(c == 3))
                        m_, mb_ = r_mm.next()
                        P.tt("dve", m_, s_, bk, ALU.mult, reads=[sb_, bb], writes=[mb_])
                        ms.append((m_, mb_))
                    P.tt("pool", ms[0][0], ms[0][0], ms[1][0], ALU.add, reads=[ms[0][1], ms[1][1]], writes=[ms[0][1]])
                    P.tt("pool", mT[:, fc, tg * TG:(tg + 1) * TG], ms[0][0], ms[2][0], ALU.add,
                         reads=[ms[0][1], ms[2][1]], writes=[mTb[tg]])
            gp = self.gpost[:, :]
            for k in range(8):
                o2, o2b = self.bank2()
                for hf in range(2):
                    t = Ticket("pe")
                    for c in range(KC):
                        P.matmul(o2[:, hf * 512:(hf + 1) * 512], mT[:, c, k * 128:(k + 1) * 128],
                                 wo[:, c * D + hf * 512:c * D + (hf + 1) * 512], c == 0, c == KC - 1,
                                 reads=[mTb[k // 4], wob], writes=[o2b[hf]], ticket=t, last=(c == KC - 1))
                self.postnorm(o2, o2b, gp, 1.0, sgi * 8 + k)
            P.barrier()

    def exchange(self, li):
        P = self.P
        P.barrier()
        groups = [[2 * i, 2 * i + 1] for i in range(self.ncores // 2)]
        self.cc_tickets = []
        for loc_, gat in ((self.s_lf, self.g_lf), (self.s_ksb, self.g_ksb), (self.s_kfx, self.g_kfx),
                          (self.s_vsb, self.g_vsb), (self.s_vfx, self.g_vfx)):
            self.cc_tickets.append(P.collective(loc_[:, :], gat.rearrange("r a b -> (r a) b"), groups))
        self.cc_ticket = self.cc_tickets[-1]
        if not self.dbg.get("overlap", True):
            P.wait_all(self.cc_ticket)
            P.barrier()


_BF = ml_dtypes.bfloat16
_CACHE = {}


def _prog(mode, nl):
    key = (mode, nl)
    if key not in _CACHE:
        b = Builder(mode, list(range(nl)))
        _CACHE[key] = b.build()
    return _CACHE[key]


def _pkn(w, n):
    lead = w.shape[:-2]
    k = w.shape[-2] // 128
    w = w.reshape(lead + (k, 128, n))
    w = np.swapaxes(w, -3, -2)
    return np.ascontiguousarray(w.reshape(lead + (128, k * n)))


def _blocks(w, starts, width):
    return np.stack([_pkn(w[:, :, s:s + width], width) for s in starts], axis=1)


def _consts():
    cm = np.zeros((4, 128, 128), np.float32)
    cm[3] = -1.0
    cm[0] = np.eye(128, dtype=np.float32)
    j = np.arange(128)[:, None]
    s = np.arange(128)[None, :]
    cm[1] = np.where(j >= s, -1.0, 0.0)
    cm[2] = 1.0
    masks, sels = [], []
    for r in range(2):
        m = np.zeros((128, 32, 514), np.float32)
        p = np.arange(128)[:, None]
        jj = np.arange(514)[None, :]
        for kb in range(32):
            g = kb // 8
            c = RANK_CHUNKS[r][g]
            qpos = 512 * c + jj - 1
            kpos = 128 * kb + p
            m[:, kb, :] = np.where(kpos > qpos, NEG, 0.0)
        masks.append(m.reshape(128, 32 * 514).astype(_BF))
        sel = np.zeros((8, 4, 8), np.float32)
        for g in range(4):
            sel[:, g, :RANK_CHUNKS[r][g]] = 1.0
        sels.append(sel.reshape(8, 32))
    return cm, masks, sels


def _layout(inp):
    f = lambda k: np.asarray(inp[k], np.float32)
    W = {}
    for tag, pre in (("f1", "ffn1"), ("f2", "ffn2")):
        W[tag + "g"] = _blocks(f(pre + "_w_gate"), [j * 128 for j in range(FC)], 128)
        W[tag + "u"] = _blocks(f(pre + "_w_up"), [j * 128 for j in range(FC)], 128)
        W[tag + "d"] = np.ascontiguousarray(f(pre + "_w_down").reshape(L, FC, 128, D))
    w_in = f("w_in")
    st = [0, 128, 256, 384, 512, 640, 768, 896, 1536, 1664, 1792, 1920, 2048, 2176, 2304, 2432,
          3080, 3208, 3336, 3464]
    W["winT"] = _blocks(w_in, st, 128)
    W["winV"] = _blocks(w_in, [1024, 2560], 512)
    W["winF"] = _pkn(w_in[:, :, 3072:3080], 8)
    W["nbf"] = np.ascontiguousarray(np.transpose(f("b_forget"), (1, 0)))
    wg = f("w_gate")
    W["wgate"] = _blocks(wg, [b * 1024 + fc * 128 for b in range(3) for fc in range(8)], 128)
    W["bgT"] = np.ascontiguousarray(f("b_gate").reshape(L, 24, 128).transpose(2, 0, 1).reshape(128, L * 24))
    br = [f("w_br_sb"), f("w_br_fox"), f("w_br_mem")]
    W["wbr"] = np.stack([_pkn(br[b][:, :, fc * 128:(fc + 1) * 128], 128) for b in range(3) for fc in range(8)], axis=1)
    W["wout"] = _pkn(f("w_out"), D)
    wm = f("w_mem_kv")
    W["wmk"] = _blocks(wm, [0, 128, 256, 384], 128)
    W["wmv"] = _pkn(wm[:, :, 512:1024], 512)
    pre = np.stack([f("ffn1_pre_g"), f("mix_pre_g"), f("ffn2_pre_g")], axis=1)
    W["pregT"] = np.ascontiguousarray(pre.reshape(L, 3, 8, 128).transpose(3, 0, 1, 2).reshape(128, L * 24))
    W["postg"] = np.ascontiguousarray(
        np.stack([f("ffn1_post_g"), f("mix_post_g"), f("ffn2_post_g")], axis=1).reshape(L * 3, D))
    W["memgT"] = np.ascontiguousarray(f("mem_norm_g").reshape(8, 128).T)
    return W


_PER_LAYER = {"f1g", "f1u", "f1d", "f2g", "f2u", "f2d", "winT", "winV", "winF", "wgate", "wbr", "wout", "wmk", "wmv"}


def _layer_slice(W, name, l):
    a = W[name]
    if name in _PER_LAYER:
        return a[l:l + 1]
    if name == "nbf":
        return np.ascontiguousarray(a[:, l:l + 1])
    if name in ("bgT", "pregT"):
        return np.ascontiguousarray(a[:, l * 24:(l + 1) * 24])
    if name == "postg":
        return a[l * 3:(l + 1) * 3]
    return a


A_NAMES = ["pregT", "postg", "f1g", "f1u", "f1d", "winT", "winV", "winF", "nbf"]
BC_NAMES = ["pregT", "postg", "f2g", "f2u", "f2d", "wgate", "bgT", "wbr", "wout", "wmk", "wmv", "memgT"]
LOC = ["s_qsb", "s_qfx", "s_qmem", "s_ksb", "s_kfx", "s_vsb", "s_vfx", "s_lf"]


def kernel(**inputs):
    x = np.asarray(inputs["x"], np.float32)
    mem = np.asarray(inputs["mem"], np.float32)
    W = _layout(inputs)
    cm, masks, sels = _consts()
    ncore = 8
    prog = _prog("FULL", L)
    maps = []
    for c in range(ncore):
        b, r = c // 2, c % 2
        m = {"x_in": np.ascontiguousarray(np.concatenate([x[b, ch * 512:(ch + 1) * 512] for ch in RANK_CHUNKS[r]], 0)),
             "cmat": cm, "mem": np.ascontiguousarray(mem[b]), "cmask": masks[r], "selc": sels[r]}
        for n in set(A_NAMES + BC_NAMES):
            m[n] = W[n]
        maps.append(m)
    res = run_bass_kernel_spmd(prog, maps, core_ids=list(range(ncore))).results
    out = np.zeros((4, S, D), np.float32)
    for c in range(ncore):
        b, r = c // 2, c % 2
        hc = np.asarray(res[c]["h_out"])
        for g, ch in enumerate(RANK_CHUNKS[r]):
            out[b, ch * 512:(ch + 1) * 512] = hc[g * 512:(g + 1) * 512]
    return out
```
